# Optimizing a Trainium2 kernel written in Bass

```python
import math
import jax
import jax.numpy as jnp
from jax import lax
import numpy as np

D_MODEL = 1024
BATCH = 16
SEQ = 2048
DEPTH = 2

CHUNK = 64
Q_BLOCK = 128
N_BRANCH = 3
BRANCH_W = D_MODEL
EPS = 1e-6

FOX_HD = 64
FOX_HEADS = BRANCH_W // FOX_HD

SSM_HD = 64
SSM_HEADS = BRANCH_W // SSM_HD
SSM_W = SSM_HEADS * SSM_HD
SSM_GROUPS = 4
SSM_STATE = 128
SSM_CONV_K = 4
SSM_CONV_DIM = SSM_W + 2 * SSM_GROUPS * SSM_STATE

DIFF_HD = 64
DIFF_HEADS = BRANCH_W // (2 * DIFF_HD)
DIFF_QK_W = DIFF_HEADS * 2 * DIFF_HD
DIFF_V_W = DIFF_HEADS * 2 * DIFF_HD
ROPE_THETA = 500000.0
ROT_DIM = DIFF_HD // 4

SPLIT_SIZES = (
    BRANCH_W, BRANCH_W, BRANCH_W, FOX_HEADS, BRANCH_W,
    SSM_W, SSM_CONV_DIM, SSM_HEADS,
    DIFF_QK_W, DIFF_QK_W, DIFF_V_W, DIFF_V_W,
    N_BRANCH * D_MODEL,
)
N_IN = sum(SPLIT_SIZES)

kernel_name = 'hybrid_fox_ssd_diffattn_streaming'


def rmsnorm(x, w):
    x32 = x.astype(jnp.float32)
    y = x32 * lax.rsqrt(jnp.mean(x32 * x32, axis=-1, keepdims=True) + EPS)
    return (y * w.astype(jnp.float32)).astype(x.dtype)


def partial_rope(x, cos, sin):
    half = ROT_DIM // 2
    x1 = x[..., :half]
    x2 = x[..., half:ROT_DIM]
    rest = x[..., ROT_DIM:]
    rot = jnp.concatenate([x1 * cos - x2 * sin, x2 * cos + x1 * sin], axis=-1)
    return jnp.concatenate([rot.astype(x.dtype), rest], axis=-1)


def causal_depthwise_conv(u, w, b):
    y = lax.conv_general_dilated(
        u, w[:, None, :].astype(u.dtype), window_strides=(1,),
        padding=[(SSM_CONV_K - 1, 0)], dimension_numbers=('NWC', 'WIO', 'NWC'),
        feature_group_count=u.shape[-1])
    return y + b


def forgetting_attention(q, k, v, log_f):
    L = q.shape[2]
    F = jnp.cumsum(log_f, axis=-1)
    scale = q.shape[-1] ** -0.5
    outs = []
    for i in range(L // Q_BLOCK):
        q0, q1 = i * Q_BLOCK, (i + 1) * Q_BLOCK
        s = jnp.einsum('bhqd,bhkd->bhqk', q[:, :, q0:q1], k[:, :, :q1]).astype(jnp.float32) * scale
        s = s + F[:, :, q0:q1, None] - F[:, :, None, :q1]
        tq = jnp.arange(q0, q1)[:, None]
        tk = jnp.arange(q1)[None, :]
        s = jnp.where(tk <= tq, s, -jnp.inf)
        p = jax.nn.softmax(s, axis=-1).astype(v.dtype)
        outs.append(jnp.einsum('bhqk,bhkd->bhqd', p, v[:, :, :q1]))
    return jnp.concatenate(outs, axis=2)


def differential_attention(q, k, v, lam):
    L = q.shape[3]
    scale = q.shape[-1] ** -0.5
    outs = []
    for i in range(L // Q_BLOCK):
        q0, q1 = i * Q_BLOCK, (i + 1) * Q_BLOCK
        s = jnp.einsum('bhcqd,bhckd->bhcqk', q[:, :, :, q0:q1], k[:, :, :, :q1]).astype(jnp.float32) * scale
        cq = (jnp.arange(q0, q1) // CHUNK)[:, None]
        ck = (jnp.arange(q1) // CHUNK)[None, :]
        s = jnp.where(ck <= cq, s, -jnp.inf)
        p = jax.nn.softmax(s, axis=-1)
        a = (p[:, :, 0] - lam * p[:, :, 1]).astype(v.dtype)
        outs.append(jnp.einsum('bhqk,bhkd->bhqd', a, v[:, :, :q1]))
    return jnp.concatenate(outs, axis=2)


def ssd_scan(x, dt, a, b, c):
    bsz, L, H, P = x.shape
    G, N = b.shape[-2], b.shape[-1]
    nc = L // CHUNK
    hg = H // G
    x = x.reshape(bsz, nc, CHUNK, G, hg, P)
    dt = dt.reshape(bsz, nc, CHUNK, G, hg)
    b = b.reshape(bsz, nc, CHUNK, G, N)
    c = c.reshape(bsz, nc, CHUNK, G, N)
    xdt = x * dt[..., None].astype(x.dtype)
    a_cs = jnp.cumsum(dt * a.reshape(G, hg), axis=2)
    seg = a_cs[:, :, :, None] - a_cs[:, :, None, :]
    tril = jnp.tril(jnp.ones((CHUNK, CHUNK), dtype=bool))[:, :, None, None]
    decay = jnp.exp(jnp.where(tril, seg, -jnp.inf))
    cb = jnp.einsum('bclgn,bcsgn->bclsg', c, b)
    y_diag = jnp.einsum('bclsg,bclsgh,bcsghp->bclghp', cb, decay, xdt)
    decay_states = jnp.exp(a_cs[:, :, -1:] - a_cs)
    states = jnp.einsum('bcsgn,bcsgh,bcsghp->bcghpn', b, decay_states, xdt)
    chunk_decay = jnp.exp(a_cs[:, :, -1])

    def step(h, inp):
        st, dec = inp
        h_new = h * dec[..., None, None] + st
        return h_new, h

    init = jnp.zeros((bsz, G, hg, P, N), dtype=states.dtype)
    _, prev = lax.scan(step, init, (states.swapaxes(0, 1), chunk_decay.swapaxes(0, 1)))
    prev = prev.swapaxes(0, 1)
    y_off = jnp.einsum('bclgn,bcghpn,bclgh->bclghp', c, prev, jnp.exp(a_cs))
    return (y_diag + y_off).reshape(bsz, L, H, P)


def hybrid_layer(x, layer, norm_w, w_in, b_forget, conv_w, conv_b, dt_bias, a_log,
                 d_skip, ssm_norm_w, diff_lambda, subln_w, w_branch, w_out, cos, sin):
    bsz, L, _ = x.shape
    h = rmsnorm(x, norm_w)
    proj = jnp.einsum('bld,de->ble', h, w_in)
    split_points = np.cumsum(SPLIT_SIZES)[:-1].tolist()
    (fq, fk, fv, ff, fg, sz, sxbc, sdt, dq, dk, dv, dg, mg) = jnp.split(proj, split_points, axis=-1)

    def heads(t, n, d):
        return t.reshape(bsz, L, n, d).transpose(0, 2, 1, 3)
    log_f = jax.nn.log_sigmoid((ff + b_forget).astype(jnp.float32)).transpose(0, 2, 1)
    o_a = forgetting_attention(heads(fq, FOX_HEADS, FOX_HD), heads(fk, FOX_HEADS, FOX_HD),
                               heads(fv, FOX_HEADS, FOX_HD), log_f)
    y_a = o_a.transpose(0, 2, 1, 3).reshape(bsz, L, BRANCH_W) * jax.nn.silu(fg)

    xbc = jax.nn.silu(causal_depthwise_conv(sxbc, conv_w, conv_b))
    gn = SSM_GROUPS * SSM_STATE
    xs, bs, cs = jnp.split(xbc, [SSM_W, SSM_W + gn], axis=-1)
    dt = jax.nn.softplus((sdt + dt_bias).astype(jnp.float32))
    a = -jnp.exp(a_log.astype(jnp.float32))
    xh = xs.reshape(bsz, L, SSM_HEADS, SSM_HD)
    y = ssd_scan(xh, dt, a, bs.reshape(bsz, L, SSM_GROUPS, SSM_STATE),
                 cs.reshape(bsz, L, SSM_GROUPS, SSM_STATE))
    y = (y + xh * d_skip[:, None]).reshape(bsz, L, SSM_W)
    yg = (y * jax.nn.silu(sz)).reshape(bsz, L, SSM_GROUPS, SSM_W // SSM_GROUPS)
    y_b = rmsnorm(yg, jnp.ones((SSM_W // SSM_GROUPS,), jnp.float32)).reshape(bsz, L, SSM_W) * ssm_norm_w

    q = partial_rope(dq.reshape(bsz, L, DIFF_HEADS, 2, DIFF_HD), cos, sin).transpose(0, 2, 3, 1, 4)
    k = partial_rope(dk.reshape(bsz, L, DIFF_HEADS, 2, DIFF_HD), cos, sin).transpose(0, 2, 3, 1, 4)
    v = heads(dv, DIFF_HEADS, 2 * DIFF_HD)
    lam_init = 0.8 - 0.6 * math.exp(-0.3 * layer)
    lp = diff_lambda.astype(jnp.float32)
    lam = jnp.exp(jnp.sum(lp[0] * lp[1])) - jnp.exp(jnp.sum(lp[2] * lp[3])) + lam_init
    o_c = differential_attention(q, k, v, lam)
    o_c = rmsnorm(o_c, subln_w) * (1.0 - lam_init)
    y_c = o_c.transpose(0, 2, 1, 3).reshape(bsz, L, DIFF_V_W) * jax.nn.silu(dg)

    gates = jax.nn.sigmoid(mg).reshape(bsz, L, N_BRANCH, D_MODEL)
    branches = jnp.stack([y_a, y_b, y_c], axis=2)
    proj_br = jnp.einsum('blnw,nwd->blnd', branches, w_branch)
    merged = jnp.sum(gates * proj_br, axis=2)
    return x + jnp.einsum('bld,de->ble', merged, w_out)


def setup_inputs(seed: int = 0) -> dict:
    key = jax.random.key(seed)
    ks = jax.random.split(key, 16)
    f32 = jnp.float32
    x = jax.random.normal(ks[0], (BATCH, SEQ, D_MODEL), f32)
    norm_w = 1.0 + 0.02 * jax.random.normal(ks[1], (DEPTH, D_MODEL), f32)
    w_in = jax.random.normal(ks[2], (DEPTH, D_MODEL, N_IN), f32) * D_MODEL ** -0.5
    b_forget = jax.random.uniform(ks[3], (DEPTH, FOX_HEADS), f32, minval=1.0, maxval=4.0)
    conv_w = jax.random.normal(ks[4], (DEPTH, SSM_CONV_K, SSM_CONV_DIM), f32) * SSM_CONV_K ** -0.5
    conv_b = 0.01 * jax.random.normal(ks[5], (DEPTH, SSM_CONV_DIM), f32)
    dt0 = jnp.exp(jax.random.uniform(ks[6], (DEPTH, SSM_HEADS), f32,
                                     minval=math.log(1e-3), maxval=math.log(1e-1)))
    dt_bias = dt0 + jnp.log(-jnp.expm1(-dt0))
    a_log = jnp.log(jax.random.uniform(ks[7], (DEPTH, SSM_HEADS), f32, minval=1.0, maxval=16.0))
    d_skip = 1.0 + 0.02 * jax.random.normal(ks[8], (DEPTH, SSM_HEADS), f32)
    ssm_norm_w = 1.0 + 0.02 * jax.random.normal(ks[9], (DEPTH, SSM_W), f32)
    diff_lambda = 0.1 * jax.random.normal(ks[10], (DEPTH, 4, DIFF_HD), f32)
    subln_w = 1.0 + 0.02 * jax.random.normal(ks[11], (DEPTH, 2 * DIFF_HD), f32)
    w_branch = jax.random.normal(ks[12], (DEPTH, N_BRANCH, BRANCH_W, D_MODEL), f32) * BRANCH_W ** -0.5
    w_out = jax.random.normal(ks[13], (DEPTH, D_MODEL, D_MODEL), f32) * D_MODEL ** -0.5
    final_norm_w = 1.0 + 0.02 * jax.random.normal(ks[14], (D_MODEL,), f32)
    return {'x': x, 'norm_w': norm_w, 'w_in': w_in, 'b_forget': b_forget, 'conv_w': conv_w,
            'conv_b': conv_b, 'dt_bias': dt_bias, 'a_log': a_log, 'd_skip': d_skip,
            'ssm_norm_w': ssm_norm_w, 'diff_lambda': diff_lambda, 'subln_w': subln_w,
            'w_branch': w_branch, 'w_out': w_out, 'final_norm_w': final_norm_w}


def reference(x, norm_w, w_in, b_forget, conv_w, conv_b, dt_bias, a_log, d_skip,
              ssm_norm_w, diff_lambda, subln_w, w_branch, w_out, final_norm_w):
    L = x.shape[1]
    pos = jnp.arange(L, dtype=jnp.float32)
    inv_freq = ROPE_THETA ** (-jnp.arange(0, ROT_DIM, 2, dtype=jnp.float32) / ROT_DIM)
    ang = pos[:, None] * inv_freq[None, :]
    cos = jnp.cos(ang)[:, None, None, :]
    sin = jnp.sin(ang)[:, None, None, :]
    for layer in range(DEPTH):
        x = hybrid_layer(x, layer, norm_w[layer], w_in[layer], b_forget[layer], conv_w[layer],
                         conv_b[layer], dt_bias[layer], a_log[layer], d_skip[layer],
                         ssm_norm_w[layer], diff_lambda[layer], subln_w[layer],
                         w_branch[layer], w_out[layer], cos, sin)
    return rmsnorm(x, final_norm_w)
```

```python
import numpy as np
import concourse.bass as bass
import concourse.mybir as mybir
from concourse.bass_utils import run_bass_kernel_spmd

F32 = mybir.dt.float32
BF16 = mybir.dt.bfloat16
AF = mybir.ActivationFunctionType
ALU = mybir.AluOpType
AX = mybir.AxisListType

PE, DVE, ACT, POOL, SP = 'tensor', 'vector', 'scalar', 'gpsimd', 'sync'
ENGS = [PE, DVE, ACT, POOL, SP]


class Buf:
    __slots__ = ('name', 'w', 'r', 'rd', 'dsem')

    def __init__(self, name):
        self.name = name
        self.w = None
        self.r = {}
        self.rd = []
        self.dsem = None


class Op:
    __slots__ = ('eng', 'fn', 'deps', 'dma_sem', 'tok', 'waited')

    def __init__(self, eng, fn):
        self.eng = eng
        self.fn = fn
        self.deps = []
        self.dma_sem = None
        self.tok = None
        self.waited = False


class DmaSem:
    def __init__(self, sem):
        self.sem = sem
        self.count = 0
        self.last = None
        self.q = None


class Prog:
    def __init__(self, nc, stack):
        self.nc = nc
        self.stack = stack
        self.ops = {e: [] for e in ENGS}
        self.esem = {}
        self.nops = 0
        self.dsems = []
        self.nsem = 0
        self.dpool = {}
        self.drr = {}

    def enter(self, cm):
        return self.stack.enter_context(cm)

    def sbuf(self, name, shape, dtype):
        return self.enter(self.nc.sbuf_tensor(name, list(shape), dtype))

    def psum(self, name, shape, dtype):
        return self.enter(self.nc.psum_tensor(name, list(shape), dtype))

    def dram(self, name, shape, dtype, kind):
        return self.nc.dram_tensor(name, list(shape), dtype, kind=kind).ap()

    def dsem(self, name=None):
        self.nsem += 1
        d = DmaSem(self.enter(self.nc.semaphore(name or ('ds%d' % self.nsem))))
        self.dsems.append(d)
        return d

    def op(self, eng, fn, reads=(), writes=(), dsem=None):
        o = Op(eng, fn)
        deps = []
        for b in reads:
            if b.w is not None:
                deps.append(b.w)
        for b in writes:
            if b.w is not None:
                deps.append(b.w)
            deps.extend(b.r.values())
            deps.extend(b.rd)
        seen = set()
        for d in deps:
            if id(d) in seen or d is o:
                continue
            seen.add(id(d))
            if d.dma_sem is None and d.eng == eng and eng == PE:
                continue
            o.deps.append(d)
        if dsem is not None:
            if dsem.last is not None and all(d is not dsem.last for d in o.deps):
                o.deps.append(dsem.last)
            o.dma_sem = dsem
            dsem.count += 16
            dsem.last = o
            o.tok = (dsem.sem, dsem.count)
        for b in reads:
            if dsem is not None:
                b.rd.append(o)
            else:
                b.r[eng] = o
        for b in writes:
            b.w = o
            b.r = {}
            b.rd = []
        self.ops[eng].append(o)
        self.nops += 1
        return o

    def dma(self, eng, out, in_, reads=(), writes=(), **kw):
        b = writes[0]
        if b.dsem is None or b.dsem.q != eng:
            pool = self.dpool.setdefault(eng, [])
            if len(pool) < 24:
                d = self.dsem()
                d.q = eng
                pool.append(d)
            else:
                d = pool[self.drr.get(eng, 0) % 24]
                self.drr[eng] = self.drr.get(eng, 0) + 1
            b.dsem = d
        return self.op(eng, lambda e: e.dma_start(out=out, in_=in_, **kw), reads=reads, writes=writes, dsem=b.dsem)

    def barrier(self):
        lasts = []
        for e in ENGS:
            if self.ops[e]:
                lasts.append(self.ops[e][-1])
        for d in self.dsems:
            if d.last is not None:
                lasts.append(d.last)
        new = []
        for e in ENGS:
            o = Op(e, lambda eng: eng.nop())
            for d in lasts:
                if d.dma_sem is None and d.eng == e:
                    continue
                o.deps.append(d)
            new.append(o)
        for o in new:
            self.ops[o.eng].append(o)

    def finish(self):
        o = Op(SP, lambda eng: eng.nop())
        for d in self.dsems:
            if d.last is not None:
                o.deps.append(d.last)
        self.ops[SP].append(o)
        self.emit()

    def emit(self):
        nc = self.nc
        for e in ENGS:
            self.esem[e] = self.enter(nc.semaphore('sem_' + e))
        for e in ENGS:
            for o in self.ops[e]:
                for d in o.deps:
                    d.waited = True
        self.nmile = {}
        for e in ENGS:
            c = 0
            for o in self.ops[e]:
                if o.dma_sem is None and o.waited:
                    c += 1
                    o.tok = (self.esem[e], c)
            self.nmile[e] = c
        prog = self

        def body(e):
            def run(eng):
                have = {}
                for o in prog.ops[e]:
                    need = {}
                    for d in o.deps:
                        s, v = d.tok
                        k = id(s)
                        if have.get(k, 0) >= v:
                            continue
                        if k not in need or need[k][1] < v:
                            need[k] = (s, v)
                    for k, (s, v) in need.items():
                        eng.wait_ge(s, v)
                        have[k] = v
                    ins = o.fn(eng)
                    if o.dma_sem is not None:
                        ins.then_inc(o.tok[0], 16)
                    elif o.waited:
                        ins.then_inc(o.tok[0], 1)
            return run

        with nc.Block() as block:
            block.tensor(body(PE))
            block.vector(body(DVE))
            block.scalar(body(ACT))
            block.gpsimd(body(POOL))
            block.sync(body(SP))


import math
from contextlib import ExitStack

L = 2048
D = 1024
NT = 16
NCH = 8
N_IN = 14368
C_FQ, C_FK, C_FV, C_FF, C_FG = 0, 1024, 2048, 3072, 3088
C_SZ, C_SX, C_SB, C_SC, C_SDT = 4112, 5136, 6160, 6672, 7184
C_DQ, C_DK, C_DV, C_DG, C_MG = 7200, 8224, 9248, 10272, 11296
NEG = -30000.0
ARENA = 24576
NSLOT = 6


class Gen:
    def __init__(self, nseq, nlayers, branches, final=True, dbg=None):
        self.nseq, self.nlayers, self.branches, self.final, self.dbg = nseq, nlayers, branches, final, dbg

    def mm(self, out, lhsT, rhs, start, stop, reads, writes):
        return self.P.op(PE, lambda e: e.matmul(out, lhsT=lhsT, rhs=rhs, start=start, stop=stop), reads, writes)

    def tr(self, out, in_, reads, writes):
        ident = self.ident
        return self.P.op(PE, lambda e: e.transpose(out=out, in_=in_, identity=ident[:]), list(reads) + [self.b_const], writes)

    def act(self, out, in_, func, reads, writes, **kw):
        return self.P.op(ACT, lambda e: e.activation(out=out, in_=in_, func=func, **kw), reads, writes)

    def tt(self, eng, out, in0, in1, op, reads, writes):
        return self.P.op(eng, lambda e: e.tensor_tensor(out=out, in0=in0, in1=in1, op=op), reads, writes)

    def ts(self, eng, out, in0, s1, s2, op0, op1, reads, writes, **kw):
        if op1 is None:
            return self.P.op(eng, lambda e: e.tensor_scalar(out=out, in0=in0, scalar1=s1, scalar2=None, op0=op0, **kw), reads, writes)
        return self.P.op(eng, lambda e: e.tensor_scalar(out=out, in0=in0, scalar1=s1, scalar2=s2, op0=op0, op1=op1, **kw), reads, writes)

    def stt(self, out, in0, scalar, in1, op0, op1, reads, writes, **kw):
        return self.P.op(DVE, lambda e: e.scalar_tensor_tensor(out=out, in0=in0, scalar=scalar, in1=in1, op0=op0, op1=op1, **kw), reads, writes)

    def cp(self, eng, out, in_, reads, writes):
        return self.P.op(eng, lambda e: e.tensor_copy(out=out, in_=in_), reads, writes)

    def memset(self, eng, ap, val, writes):
        return self.P.op(eng, lambda e: e.memset(ap, val), (), writes)

    def recip(self, out, in_, reads, writes):
        return self.P.op(DVE, lambda e: e.reciprocal(out=out, in_=in_), reads, writes)

    def carve(self, off, n, dtype=BF16):
        if dtype == F32:
            return self.arena[:, off:off + 2 * n].bitcast(F32)
        return self.arena[:, off:off + n]

    def wload(self, src, n):
        i = self.wrr % NSLOT
        self.wrr += 1
        ap = self.wslot[i][:, :, 0:n]
        buf = self.b_wslot[i]
        self.P.dma(POOL, out=ap, in_=src.rearrange("(c p) n -> p c n", p=128), writes=[buf])
        return ap, buf

    def w_in_cols(self, l, c0, n):
        return self.w_in_d[l, :, c0:c0 + n]

    def pj(self):
        i = self.pjrr % 2
        self.pjrr += 1
        return self.pb[i], self.b_pb[i]

    def hbufs(self, tb):
        return self.b_hT[4 * tb:4 * tb + 4]

    def proj_fm(self, w, bw, M, tb, ps, bps, rhs_src=None, rhs_bufs=None):
        src = self.hT if rhs_src is None else rhs_src
        rb = self.hbufs(tb) if rhs_bufs is None else rhs_bufs
        for c in range(NCH):
            self.mm(ps[0:M, :], w[:, c, 0:M], src[:, c, tb * 512:(tb + 1) * 512], c == 0, c == NCH - 1,
                    [bw] + list(rb), [bps])

    def build(self):
        nc = bass.Bass("TRN2", target_bir_lowering=False)
        self.nc = nc
        nseq = self.nseq
        di = lambda name, shape: nc.dram_tensor(name, list(shape), F32, kind="ExternalInput").ap()
        self.x_d = di("x", [nseq, L, D])
        self.norm_w_d = di("norm_w", [2, D])
        self.w_in_d = di("w_in", [2, D, N_IN])
        self.b_forget_d = di("b_forget", [2, 16])
        self.conv_w_d = di("conv_w", [2, 4, 2048])
        self.conv_b_d = di("conv_b", [2, 2048])
        self.dt_bias_d = di("dt_bias", [2, 16])
        self.a_log_d = di("a_log", [2, 16])
        self.d_skip_d = di("d_skip", [2, 16])
        self.ssm_norm_w_d = di("ssm_norm_w", [2, D])
        self.diff_lambda_d = di("diff_lambda", [2, 4, 64])
        self.subln_w_d = di("subln_w", [2, 128])
        self.w_branch_d = di("w_branch", [2, 3, D, D])
        self.w_out_d = di("w_out", [2, D, D])
        self.final_norm_w_d = di("final_norm_w", [1, D])
        self.rope_d = di("rope", [2, 128, L])
        self.out_d = nc.dram_tensor("out", [nseq, L, D], F32, kind="ExternalOutput").ap()
        self.gk_d = nc.dram_tensor("gk_scr", [16, 6, L], BF16, kind="Internal").ap()
        self.scr_d = nc.dram_tensor("ssd_scr", [3072, L], BF16, kind="ExternalOutput").ap()
        if self.dbg:
            self.dbg_d = nc.dram_tensor("dbg", [128, 8, L], BF16, kind="ExternalOutput").ap()
        with ExitStack() as st:
            P = Prog(nc, st)
            self.P = P
            self.xres = P.sbuf("xres", [128, NT, D], F32)
            self.hT = P.sbuf("hT", [128, NCH, L], BF16)
            self.ybr = P.sbuf("ybr", [128, NCH, L], BF16)
            self.wout = P.sbuf("wout", [128, NCH, D], BF16)
            self.arena = P.sbuf("arena", [128, ARENA], BF16)
            self.wslot = [P.sbuf("wslot%d" % i, [128, NCH, 128], BF16) for i in range(NSLOT)]
            self.ident = P.sbuf("ident", [128, 128], BF16)
            self.cmask = P.sbuf("cmask", [128, 128], BF16)
            self.bmask = P.sbuf("bmask", [128, 128], BF16)
            self.onesb = P.sbuf("onesb", [128, 128], BF16)
            self.zerob = P.sbuf("zerob", [128, 128], BF16)
            self.small = P.sbuf("small", [128, 256], F32)
            self.pb = [P.psum("pb%d" % i, [128, 512], F32) for i in range(8)]
            self.b_pb = [Buf("pb%d" % i) for i in range(8)]
            self.b_x = [Buf("x%d" % t) for t in range(NT)]
            self.b_hT = [Buf("hT%d" % t) for t in range(NT)]
            self.b_ybr = [Buf("ybr%d" % c) for c in range(NCH)]
            self.b_wout = Buf("wout")
            self.b_wslot = [Buf("ws%d" % i) for i in range(NSLOT)]
            self.b_const = Buf("const")
            self.b_small = Buf("small")
            self.b_gk = Buf("gk")
            self.b_out = [Buf("out%d" % t) for t in range(NT)]
            self.wrr = 0
            self.pjrr = 0
            sm = self.small
            self.nwT = sm[:, 0:16]
            self.negbf = sm[0:16, 16:18]
            self.ss = sm[:, 32:48]
            self.rt = sm[:, 48:64]
            self.rstd = sm[:, 64:80]
            self.rc = sm[:, 80:96]
            self.consts()
            for s in range(nseq):
                for t in range(NT):
                    P.dma(SP, out=self.xres[:, t, :], in_=self.x_d[s, t * 128:(t + 1) * 128, :], writes=[self.b_x[t]])
                for l in range(self.nlayers):
                    self.layer(s, l)
                self.final_out(s)
            P.finish()
        return nc

    def consts(self):
        P = self.P
        bc = self.b_const
        self.memset(POOL, self.onesb[:], 1.0, [bc])
        self.memset(POOL, self.zerob[:], 0.0, [bc])
        P.op(POOL, lambda e: e.affine_select(out=self.ident[:], in_=self.onesb[:], pattern=[[-1, 128]], compare_op=ALU.is_equal,
                                              fill=0.0, base=0, channel_multiplier=1), [bc], [bc])
        P.op(POOL, lambda e: e.affine_select(out=self.cmask[:], in_=self.zerob[:], pattern=[[1, 128]], compare_op=ALU.is_ge,
                                              fill=NEG, base=0, channel_multiplier=-1), [bc], [bc])
        self.memset(POOL, self.bmask[:], 0.0, [bc])
        self.memset(POOL, self.bmask[64:128, 0:64], NEG, [bc])
        P.dma(SP, out=self.nwT.rearrange("p (l c) -> p l c", l=2), in_=self.norm_w_d.rearrange("l (c p) -> p l c", p=128),
              writes=[self.b_small], allow_slow_non_contiguous=True)
        P.dma(SP, out=self.negbf, in_=self.b_forget_d.rearrange("l h -> h l"), writes=[self.b_small], allow_slow_non_contiguous=True)
        self.ts(DVE, self.negbf, self.negbf, -1.0, None, ALU.mult, None, [self.b_small], [self.b_small])

    def layer(self, s, l):
        P = self.P
        for eb in range(2):
            P.dma(POOL, out=self.wout[:, :, eb * 512:(eb + 1) * 512],
                  in_=self.w_out_d[l, :, eb * 512:(eb + 1) * 512].rearrange("(c p) n -> p c n", p=128), writes=[self.b_wout])
        P.barrier()
        self.rmsnorm_T(l)
        for n, br in enumerate("abc"):
            if br not in self.branches:
                continue
            P.barrier()
            if br == 'a':
                self.fox(s, l)
            elif br == 'b':
                self.ssd(s, l)
            else:
                self.diff(s, l)
            if self.dbg == br and l == self.nlayers - 1:
                for c in range(NCH):
                    P.dma(SP, out=self.dbg_d[:, c, :], in_=self.ybr[:, c, :], reads=[self.b_ybr[c]], writes=[Buf("dbg%d" % c)])
            P.barrier()
            self.merge(l, n)

    def rmsnorm_T(self, l):
        junk = self.carve(0, 1024)
        hn = [self.carve(1024, 1024), self.carve(2048, 1024)]
        b_junk = Buf("junk")
        b_hn = [Buf("hn0"), Buf("hn1")]
        b_ss = Buf("ss")
        for t in range(NT):
            self.act(junk, self.xres[:, t, :], AF.Square, [self.b_x[t]], [b_junk, b_ss], accum_out=self.ss[:, t:t + 1])
        self.act(self.rt, self.ss, AF.Sqrt, [b_ss], [b_ss], scale=1.0 / D, bias=1e-6)
        self.recip(self.rstd, self.rt, [b_ss], [b_ss])
        for t in range(NT):
            k = t % 2
            self.act(hn[k], self.xres[:, t, :], AF.Copy, [self.b_x[t], b_ss], [b_hn[k]], scale=self.rstd[:, t:t + 1])
            ps, bps = self.pj()
            psb = ps[:].bitcast(BF16)
            for c in range(NCH):
                self.tr(psb[:, c * 128:(c + 1) * 128], hn[k][:, c * 128:(c + 1) * 128], [b_hn[k]], [bps])
            self.tt(DVE, self.hT[:, :, t * 128:(t + 1) * 128], psb.rearrange("p (c n) -> p c n", c=NCH),
                    self.nwT[:, l * 8:(l + 1) * 8].unsqueeze(2).to_broadcast([128, NCH, 128]), ALU.mult,
                    [bps, self.b_small], [self.b_hT[t]])

    def merge(self, l, n):
        gp = self.carve(0, NCH * L).rearrange("p (c t) -> p c t", c=NCH)
        th = [self.carve(NCH * L, 512, F32), self.carve(NCH * L + 1024, 512, F32)]
        b_gp = [Buf("gp%d" % t) for t in range(4)]
        b_th = [Buf("th0"), Buf("th1")]
        k = 0
        for co in range(NCH):
            wb, bwb = self.wload(self.w_branch_d[l, n, :, co * 128:(co + 1) * 128], 128)
            wg, bwg = self.wload(self.w_in_cols(l, C_MG + n * 1024 + co * 128, 128), 128)
            for tb in range(4):
                ps1, bps1 = self.pj()
                self.proj_fm(wb, bwb, 128, tb, ps1, bps1, rhs_src=self.ybr, rhs_bufs=self.b_ybr)
                ps2, bps2 = self.pb[2 + k % 2], self.b_pb[2 + k % 2]
                self.proj_fm(wg, bwg, 128, tb, ps2, bps2)
                t_ = th[k % 2]
                self.act(t_, ps2[:, :], AF.Tanh, [bps2], [b_th[k % 2]], scale=0.5)
                self.stt(gp[:, co, tb * 512:(tb + 1) * 512], t_, 1.0, ps1[:, :], ALU.add, ALU.mult,
                         [b_th[k % 2], bps1], [b_gp[tb]])
                k += 1
        for t in range(NT):
            for eb in range(2):
                ps, bps = self.pb[4 + (2 * t + eb) % 4], self.b_pb[4 + (2 * t + eb) % 4]
                for co in range(NCH):
                    self.mm(ps[:, :], gp[:, co, t * 128:(t + 1) * 128], self.wout[:, co, eb * 512:(eb + 1) * 512],
                            co == 0, co == NCH - 1, [b_gp[t // 4], self.b_wout], [bps])
                xs = self.xres[:, t, eb * 512:(eb + 1) * 512]
                self.stt(xs, ps[:, :], 0.5, xs, ALU.mult, ALU.add, [bps, self.b_x[t]], [self.b_x[t]])

    def final_out(self, s):
        P = self.P
        P.barrier()
        b_ss = Buf("fss")
        if self.final:
            junk = self.carve(0, 1024)
            fnw = self.carve(1024, 1024, F32)
            b_junk, b_fnw = Buf("fjunk"), Buf("fnw")
            P.dma(SP, out=fnw, in_=self.final_norm_w_d.partition_broadcast(128), writes=[b_fnw])
            for t in range(NT):
                self.act(junk, self.xres[:, t, :], AF.Square, [self.b_x[t]], [b_junk, b_ss], accum_out=self.ss[:, t:t + 1])
            self.act(self.rt, self.ss, AF.Sqrt, [b_ss], [b_ss], scale=1.0 / D, bias=1e-6)
            self.recip(self.rstd, self.rt, [b_ss], [b_ss])
            for t in range(NT):
                xs = self.xres[:, t, :]
                self.stt(xs, xs, self.rstd[:, t:t + 1], fnw, ALU.mult, ALU.mult, [self.b_x[t], b_ss, b_fnw], [self.b_x[t]])
        for t in range(NT):
            P.dma(SP, out=self.out_d[s, t * 128:(t + 1) * 128, :], in_=self.xres[:, t, :], reads=[self.b_x[t]], writes=[self.b_out[t]])

    def fox(self, s, l):
        P = self.P
        lf = self.carve(0, L, F32)[0:16, :]
        G = self.carve(4096, L, F32)[0:16, :]
        r = self.carve(8192, L, F32)[0:16, :]
        parts = [self.carve(12288 + 2048 * j, L)[0:16, :] for j in range(6)]
        b_lf, b_G, b_r = Buf("lf"), Buf("G"), Buf("r")
        b_parts = [Buf("part%d" % j) for j in range(6)]
        wff, bwff = self.wload(self.w_in_cols(l, C_FF, 16), 16)
        for tb in range(4):
            ps, bps = self.pj()
            self.proj_fm(wff, bwff, 16, tb, ps, bps)
            self.act(lf[:, tb * 512:(tb + 1) * 512], ps[0:16, :], AF.Exp, [bps, self.b_small], [b_lf],
                     scale=-1.0, bias=self.negbf[:, l:l + 1])
        self.act(lf, lf, AF.Ln, [b_lf], [b_lf], bias=1.0)
        P.op(DVE, lambda e: e.tensor_tensor_scan(out=G, data0=lf, data1=lf, initial=0.0, op0=ALU.add, op1=ALU.max), [b_lf], [b_G])
        self.cp(DVE, parts[0], G, [b_G], [b_parts[0]])
        self.tt(DVE, r, G, parts[0], ALU.subtract, [b_G, b_parts[0]], [b_r])
        self.cp(DVE, parts[1], r, [b_r], [b_parts[1]])
        self.tt(DVE, r, r, parts[1], ALU.subtract, [b_r, b_parts[1]], [b_r])
        self.cp(DVE, parts[2], r, [b_r], [b_parts[2]])
        for j in range(3):
            self.ts(DVE, parts[3 + j], parts[j], -1.0, None, ALU.mult, None, [b_parts[j]], [b_parts[3 + j]])
        for j in range(6):
            P.dma(SP, out=self.gk_d[:, j, :], in_=parts[j], reads=[b_parts[j]], writes=[self.b_gk])
        P.barrier()
        qaug = [self.carve(0, L), self.carve(2048, L)]
        kaug = [self.carve(4096, L), self.carve(6144, L)]
        vaug = self.carve(8192, 2080).rearrange("p (t h d) -> p t h d", t=NT, h=2)
        sg = self.carve(10368, L)
        opair = self.carve(12416, L).rearrange("p (t d) -> p t d", t=NT)
        pT = [self.carve(14464 + 512 * i, 512) for i in range(4)]
        th = [self.carve(16512, 512, F32), self.carve(17536, 512, F32)]
        b_q = [Buf("qaug0"), Buf("qaug1")]
        b_k = [Buf("kaug0"), Buf("kaug1")]
        b_v = [Buf("vaug%d" % i) for i in range(4)]
        b_sg = [Buf("sg%d" % i) for i in range(4)]
        b_op = [Buf("op%d" % i) for i in range(NT)]
        b_pT = [Buf("pT%d" % i) for i in range(4)]
        b_th = [Buf("fth0"), Buf("fth1")]
        b_rc = [Buf("rc%d" % i) for i in range(4)]
        for hh in range(2):
            self.memset(DVE, kaug[hh][64:67, :], 1.0, [b_k[hh]])
            self.memset(DVE, qaug[hh][64:67, :], 1.0, [b_q[hh]])
            P.dma(SP, out=qaug[hh][67:70, :], in_=qaug[hh][64:67, :], reads=[b_q[hh]], writes=[b_q[hh]])
        for g4 in range(4):
            self.memset(DVE, vaug[:, 4 * g4:4 * g4 + 4, :, 64:65], 1.0, [b_v[g4]])
        pS = [self.pb[2], self.pb[3]]
        b_pS = [self.b_pb[2], self.b_pb[3]]
        pO = self.pb[4:8]
        b_pO = self.b_pb[4:8]
        sidx = 0
        thk = 0
        for j in range(8):
            wq, bwq = self.wload(self.w_in_cols(l, C_FQ + 128 * j, 128), 128)
            wk, bwk = self.wload(self.w_in_cols(l, C_FK + 128 * j, 128), 128)
            wv, bwv = self.wload(self.w_in_cols(l, C_FV + 128 * j, 128), 128)
            wg, bwg = self.wload(self.w_in_cols(l, C_FG + 128 * j, 128), 128)
            for hh in range(2):
                h = 2 * j + hh
                P.dma(SP, out=kaug[hh][67:70, :], in_=self.gk_d[h, 0:3, :], reads=[self.b_gk], writes=[b_k[hh]])
                P.dma(SP, out=qaug[hh][64:67, :], in_=self.gk_d[h, 3:6, :], reads=[self.b_gk], writes=[b_q[hh]])
            for tb in range(4):
                sl = slice(tb * 512, (tb + 1) * 512)
                ps, bps = self.pj()
                self.proj_fm(wq, bwq, 128, tb, ps, bps)
                for hh in range(2):
                    self.ts(DVE, qaug[hh][0:64, sl], ps[64 * hh:64 * hh + 64, :], 0.125, None, ALU.mult, None, [bps], [b_q[hh]])
                ps, bps = self.pj()
                self.proj_fm(wk, bwk, 128, tb, ps, bps)
                for hh in range(2):
                    self.cp(DVE, kaug[hh][0:64, sl], ps[64 * hh:64 * hh + 64, :], [bps], [b_k[hh]])
                ps, bps = self.pj()
                self.proj_fm(wg, bwg, 128, tb, ps, bps)
                t_ = th[thk % 2]
                self.act(t_, ps[:, :], AF.Tanh, [bps], [b_th[thk % 2]], scale=0.5)
                self.stt(sg[:, sl], t_, 1.0, ps[:, :], ALU.add, ALU.mult, [b_th[thk % 2], bps], [b_sg[tb]])
                thk += 1
            for g4 in range(4):
                ps, bps = self.pj()
                for u in range(4):
                    kt = 4 * g4 + u
                    for c in range(NCH):
                        self.mm(ps[:, u * 128:(u + 1) * 128], self.hT[:, c, kt * 128:(kt + 1) * 128], wv[:, c, :],
                                c == 0, c == NCH - 1, [bwv, self.b_hT[kt]], [bps])
                self.cp(DVE, vaug[:, 4 * g4:4 * g4 + 4, :, 0:64], ps[:, :].rearrange("p (t h d) -> p t h d", t=4, h=2),
                        [bps], [b_v[g4]])
            tiles = [(hh, jq, i) for hh in range(2) for jq in range(4) for i in range(4 * jq + 4)]

            def emitS(tile, idx):
                hh, jq, i = tile
                r_ = i - 4 * jq
                qlo = max(512 * jq, 128 * i)
                w = 512 * (jq + 1) - qlo
                S, bS = pS[idx % 2], b_pS[idx % 2]
                pt, bpt = pT[idx % 4], b_pT[idx % 4]
                self.mm(S[:, 0:w], kaug[hh][0:70, i * 128:(i + 1) * 128], qaug[hh][0:70, qlo:qlo + w],
                        True, r_ < 0, [b_k[hh], b_q[hh]], [bS])
                if r_ >= 0:
                    self.mm(S[:, 0:128], self.ident[:], self.cmask[:], False, True, [self.b_const], [bS])
                self.act(pt[:, 0:w], S[:, 0:w], AF.Exp, [bS], [bpt])

            def emitPV(tile, idx):
                hh, jq, i = tile
                qlo = max(512 * jq, 128 * i)
                w = 512 * (jq + 1) - qlo
                pt, bpt = pT[idx % 4], b_pT[idx % 4]
                for cc in range(w // 128):
                    qc = qlo // 128 + cc
                    O, bO = pO[qc % 4], b_pO[qc % 4]
                    self.mm(O[:, 0:65], pt[:, cc * 128:(cc + 1) * 128], vaug[:, i, hh, :], i == 0, i == qc,
                            [bpt, b_v[i // 4]], [bO])
                    if i == qc:
                        self.recip(self.rc[:, qc:qc + 1], O[:, 64:65], [bO], [b_rc[qc % 4]])
                        self.ts(DVE, opair[:, qc, 64 * hh:64 * hh + 64], O[:, 0:64], self.rc[:, qc:qc + 1], 0.5,
                                ALU.mult, ALU.mult, [bO, b_rc[qc % 4]], [b_op[qc]])

            for n_, tile in enumerate(tiles):
                emitS(tile, sidx + n_)
                if n_ > 0:
                    emitPV(tiles[n_ - 1], sidx + n_ - 1)
            emitPV(tiles[-1], sidx + len(tiles) - 1)
            sidx += len(tiles)
            for g4 in range(4):
                ps, bps = self.pj()
                psb = ps[:].bitcast(BF16)
                for u in range(4):
                    qc = 4 * g4 + u
                    self.tr(psb[:, u * 128:(u + 1) * 128], opair[:, qc, :], [b_op[qc]], [bps])
                self.tt(DVE, self.ybr[:, j, g4 * 512:(g4 + 1) * 512], psb[:, 0:512], sg[:, g4 * 512:(g4 + 1) * 512], ALU.mult,
                        [bps, b_sg[g4]], [self.b_ybr[j]])

    def ssd(self, s, l):
        P = self.P
        scr = self.scr_d
        HB = 16
        dt_all = self.carve(0, 256, F32)
        dA_all = self.carve(512, 256, F32)
        bc3 = self.carve(1024, 48, F32)
        cwT = self.carve(1120, 80, F32).rearrange("p (c j) -> p c j", c=16)
        snw = self.carve(1280, 8, F32)
        Lm = self.carve(1296, 128, F32)
        U = self.carve(1552, 128, F32)
        onesf = self.carve(1808, 128, F32)
        mask01 = self.carve(2064, 128)
        E = self.carve(2192, 48, F32)
        b_dt, b_bc3, b_cw, b_snw, b_cm, b_E = Buf("dt_all"), Buf("bc3"), Buf("cwT"), Buf("snw"), Buf("ssdconst"), Buf("E")
        self.memset(POOL, onesf, 1.0, [b_cm])
        P.op(POOL, lambda e: e.affine_select(out=Lm, in_=self.onesb[:], pattern=[[1, 128]], compare_op=ALU.is_ge, fill=0.0,
                                              base=0, channel_multiplier=-1), [self.b_const], [b_cm])
        P.op(POOL, lambda e: e.affine_select(out=mask01, in_=self.onesb[:], pattern=[[1, 128]], compare_op=ALU.is_ge, fill=0.0,
                                              base=0, channel_multiplier=-1), [self.b_const], [b_cm])
        P.op(POOL, lambda e: e.affine_select(out=U, in_=self.onesb[:], pattern=[[-1, 128]], compare_op=ALU.is_gt, fill=0.0,
                                              base=0, channel_multiplier=1), [self.b_const], [b_cm])
        P.dma(SP, out=bc3[:, 0:16], in_=self.dt_bias_d[l:l + 1].partition_broadcast(128), writes=[b_bc3])
        P.dma(SP, out=bc3[:, 16:32], in_=self.a_log_d[l:l + 1].partition_broadcast(128), writes=[b_bc3])
        P.dma(SP, out=bc3[:, 32:48], in_=self.d_skip_d[l:l + 1].partition_broadcast(128), writes=[b_bc3])
        self.act(bc3[:, 16:32], bc3[:, 16:32], AF.Exp, [b_bc3], [b_bc3])
        self.ts(DVE, bc3[:, 16:32], bc3[:, 16:32], -1.0, None, ALU.mult, None, [b_bc3], [b_bc3])
        for j in range(4):
            P.dma(SP, out=cwT[:, :, j:j + 1], in_=self.conv_w_d[l, j:j + 1, :].rearrange("o (c p) -> p c o", p=128), writes=[b_cw],
                  allow_slow_non_contiguous=True)
        P.dma(SP, out=cwT[:, :, 4:5], in_=self.conv_b_d[l:l + 1, :].rearrange("o (c p) -> p c o", p=128), writes=[b_cw],
              allow_slow_non_contiguous=True)
        P.dma(SP, out=snw, in_=self.ssm_norm_w_d[l:l + 1, :].rearrange("o (c p) -> p (o c)", p=128), writes=[b_snw],
              allow_slow_non_contiguous=True)
        wdt, bwdt = self.wload(self.w_in_cols(l, C_SDT, 16), 16)
        for g4 in range(4):
            ps, bps = self.pj()
            for u_ in range(4):
                kt = 4 * g4 + u_
                for c in range(NCH):
                    self.mm(ps[:, u_ * 16:(u_ + 1) * 16], self.hT[:, c, kt * 128:(kt + 1) * 128], wdt[:, c, 0:16],
                            c == 0, c == NCH - 1, [bwdt, self.b_hT[kt]], [bps])
            self.tt(DVE, dt_all[:, g4 * 64:(g4 + 1) * 64].rearrange("p (t h) -> p t h", t=4),
                    ps[:, 0:64].rearrange("p (t h) -> p t h", t=4), bc3[:, 0:16].unsqueeze(1).to_broadcast([128, 4, 16]), ALU.add,
                    [bps, b_bc3], [b_dt])
        self.act(dt_all, dt_all, AF.Exp, [b_dt], [b_dt])
        self.act(dt_all, dt_all, AF.Ln, [b_dt], [b_dt], bias=1.0)
        self.tt(DVE, dA_all.rearrange("p (t h) -> p t h", t=NT), dt_all.rearrange("p (t h) -> p t h", t=NT),
                bc3[:, 16:32].unsqueeze(1).to_broadcast([128, NT, 16]), ALU.mult, [b_dt, b_bc3], [b_dt])
        if getattr(self, 'ssd_level', 9) < 1:
            return
        cin = self.carve(2304, 2080, F32)
        cout = [self.carve(6464, L), self.carve(8512, L)]
        pa = [self.carve(10560, 512, F32), self.carve(11584, 512, F32)]
        ah = [self.carve(12608, 512, F32), self.carve(13632, 512, F32)]
        pq = [self.carve(14656, 512, F32), self.carve(15680, 512, F32)]
        b_u, b_cout = Buf("cin"), [Buf("cout0"), Buf("cout1")]
        b_pa, b_ah, b_pq = [Buf("pa0"), Buf("pa1")], [Buf("ah0"), Buf("ah1")], [Buf("pq0"), Buf("pq1")]
        b_scr = [Buf("scr%d" % i) for i in range(24)]
        PAD = 4
        self.memset(DVE, cin[:, 0:PAD], 0.0, [b_u])
        k = 0
        for cc in range(16):
            w_, bw_ = self.wload(self.w_in_cols(l, C_SX + 128 * cc, 128), 128)
            for tb in range(4):
                ps, bps = self.pj()
                self.proj_fm(w_, bw_, 128, tb, ps, bps)
                self.act(cin[:, PAD + tb * 512:PAD + (tb + 1) * 512], ps[:, :], AF.Copy, [bps], [b_u])
            co, bco = cout[cc % 2], b_cout[cc % 2]
            for tb in range(4):
                a_, ba_ = ah[k % 2], b_ah[k % 2]
                p_, bp_ = pa[k % 2], b_pa[k % 2]
                q_, bq_ = pq[k % 2], b_pq[k % 2]
                o0 = tb * 512 + PAD - 3
                self.act(a_, cin[:, o0 + 3:o0 + 3 + 512], AF.Identity, [b_u, b_cw], [ba_], scale=cwT[:, cc, 3:4], bias=cwT[:, cc, 4:5])
                self.ts(POOL, p_, cin[:, o0 + 1:o0 + 1 + 512], cwT[:, cc, 1:2], 0.0, ALU.mult, ALU.add, [b_u, b_cw], [bp_])
                self.ts(POOL, q_, cin[:, o0:o0 + 512], cwT[:, cc, 0:1], 0.0, ALU.mult, ALU.add, [b_u, b_cw], [bq_])
                self.tt(POOL, p_, p_, q_, ALU.add, [bp_, bq_], [bp_])
                self.stt(a_, cin[:, o0 + 2:o0 + 2 + 512], cwT[:, cc, 2:3], a_, ALU.mult, ALU.add, [b_u, b_cw, ba_], [ba_])
                self.tt(DVE, a_, a_, p_, ALU.add, [ba_, bp_], [ba_])
                self.act(co[:, tb * 512:(tb + 1) * 512], a_, AF.Silu, [ba_], [bco])
                k += 1
            P.dma(SP, out=scr[cc * 128:(cc + 1) * 128, :], in_=co, reads=[bco], writes=[b_scr[cc]])
        for c in range(NCH):
            w_, bw_ = self.wload(self.w_in_cols(l, C_SZ + 128 * c, 128), 128)
            co, bco = cout[c % 2], b_cout[c % 2]
            for tb in range(4):
                ps, bps = self.pj()
                self.proj_fm(w_, bw_, 128, tb, ps, bps)
                self.act(co[:, tb * 512:(tb + 1) * 512], ps[:, :], AF.Silu, [bps], [bco])
            P.dma(SP, out=scr[2048 + c * 128:2048 + (c + 1) * 128, :], in_=co, reads=[bco], writes=[b_scr[16 + c]])
        if getattr(self, 'ssd_level', 9) < 2:
            return
        P.barrier()
        xsT = self.carve(2304, 2048).rearrange("p (c t) -> p c t", c=8)
        BT = self.carve(4352, 1024).rearrange("p (c t) -> p c t", c=4)
        CT = self.carve(5376, 1024).rearrange("p (c t) -> p c t", c=4)
        zsT = self.carve(6400, 2048).rearrange("p (c t) -> p c t", c=8)
        rhsseg = self.carve(8448, 1024, F32).rearrange("p (h l) -> p h l", h=8)
        dec = self.carve(10496, 2048).rearrange("p (h l) -> p h l", h=16)
        CBm = self.carve(12544, 512).rearrange("p (g l) -> p g l", g=4)
        xdt = self.carve(13056, 1024).rearrange("p (h d) -> p h d", h=16)
        xdd = self.carve(14080, 1024).rearrange("p (h d) -> p h d", h=16)
        ytok = self.carve(15104, 1024)
        y1 = self.carve(16128, 1024, F32)
        Btok = self.carve(18176, 512).rearrange("p (g n) -> p g n", g=4)
        state = self.carve(18688, 1024, F32)
        state_bf = self.carve(20736, 1024)
        sq = self.carve(21760, 512).rearrange("p (c t) -> p c t", c=2)
        rt = self.carve(22272, 256, F32)
        b_xs, b_B, b_C, b_z = Buf("xsT"), Buf("BT"), Buf("CT"), Buf("zsT")
        b_rs, b_dec, b_CBm, b_xdt, b_xdd, b_ytok, b_y1 = [Buf(n) for n in "rhsseg dec CBm xdt xdd ytok y1".split()]
        b_Btok, b_state, b_sbf, b_sq, b_rt = [Buf(n) for n in "Btok state state_bf sq rt".split()]
        pb, bpb = self.pb, self.b_pb
        for bk in range(8):
            tsl = slice(bk * 256, (bk + 1) * 256)
            P.dma(SP, out=xsT, in_=scr[0:1024, tsl].rearrange("(c p) t -> p c t", p=128), reads=b_scr[0:8], writes=[b_xs])
            P.dma(SP, out=BT, in_=scr[1024:1536, tsl].rearrange("(c p) t -> p c t", p=128), reads=b_scr[8:12], writes=[b_B])
            P.dma(SP, out=CT, in_=scr[1536:2048, tsl].rearrange("(c p) t -> p c t", p=128), reads=b_scr[12:16], writes=[b_C])
            P.dma(SP, out=zsT, in_=scr[2048:3072, tsl].rearrange("(c p) t -> p c t", p=128), reads=b_scr[16:24], writes=[b_z])
            for c2 in range(2):
                kt = 2 * bk + c2
                osl = slice(c2 * 128, (c2 + 1) * 128)
                dAk = dA_all[:, kt * 16:(kt + 1) * 16]
                self.mm(pb[0][:, 0:16], Lm, dAk, True, True, [b_cm, b_dt], [bpb[0]])
                self.mm(pb[0][:, 16:32], U, dAk, True, True, [b_cm, b_dt], [bpb[0]])
                self.mm(pb[0][:, 32:48], onesf, dAk, True, True, [b_cm, b_dt], [bpb[0]])
                self.act(E, pb[0][:, 0:48], AF.Exp, [bpb[0]], [b_E])
                for half in range(2):
                    self.tt(DVE, rhsseg, Lm.unsqueeze(1).to_broadcast([128, 8, 128]),
                            dAk[:, 8 * half:8 * half + 8].unsqueeze(2).to_broadcast([128, 8, 128]), ALU.mult,
                            [b_cm, b_dt], [b_rs])
                    for q in range(2):
                        self.mm(pb[2 + q][:, :], U, rhsseg[:, 4 * q:4 * q + 4, :].rearrange("p h l -> p (h l)"), True, True,
                                [b_cm, b_rs], [bpb[2 + q]])
                        self.act(dec[:, 8 * half + 4 * q:8 * half + 4 * q + 4, :].rearrange("p h l -> p (h l)"), pb[2 + q][:, :],
                                 AF.Exp, [bpb[2 + q]], [b_dec])
                for g in range(4):
                    self.mm(pb[0][:, g * 128:(g + 1) * 128], BT[:, g, osl], CT[:, g, osl], True, True, [b_B, b_C], [bpb[0]])
                self.tt(DVE, CBm, pb[0][:, :].rearrange("p (g l) -> p g l", g=4), mask01.unsqueeze(1).to_broadcast([128, 4, 128]),
                        ALU.mult, [bpb[0], b_cm], [b_CBm])
                d4 = dec.rearrange("p (g j) l -> p g j l", g=4)
                self.tt(POOL, d4, d4, CBm.unsqueeze(2).to_broadcast([128, 4, 4, 128]), ALU.mult, [b_dec, b_CBm], [b_dec])
                pxb = pb[1][:].bitcast(BF16)
                for c in range(NCH):
                    self.tr(pxb[:, c * 128:(c + 1) * 128], xsT[:, c, osl], [b_xs], [bpb[1]])
                px3 = pxb.rearrange("p (h d) -> p h d", h=16)
                self.tt(DVE, xdt, px3, dt_all[:, kt * 16:(kt + 1) * 16].unsqueeze(2).to_broadcast([128, 16, 64]), ALU.mult,
                        [bpb[1], b_dt], [b_xdt])
                self.tt(DVE, ytok.rearrange("p (h d) -> p h d", h=16), px3, bc3[:, 32:48].unsqueeze(2).to_broadcast([128, 16, 64]),
                        ALU.mult, [bpb[1], b_bc3], [b_ytok])
                self.tt(POOL, xdd, xdt, E[:, 16:32].unsqueeze(2).to_broadcast([128, 16, 64]), ALU.mult, [b_xdt, b_E], [b_xdd])
                for h in range(HB):
                    self.mm(pb[4 + h // 8][:, (h % 8) * 64:(h % 8 + 1) * 64], dec[:, h, :], xdt[:, h, :], True, True,
                            [b_dec, b_xdt], [bpb[4 + h // 8]])
                if kt > 0:
                    for g in range(4):
                        self.mm(pb[6 + g // 2][:, (g % 2) * 256:(g % 2 + 1) * 256], CT[:, g, osl], state_bf[:, g * 256:(g + 1) * 256],
                                True, True, [b_C, b_sbf], [bpb[6 + g // 2]])
                for hb in range(2):
                    hs = slice(hb * 512, (hb + 1) * 512)
                    if kt > 0:
                        self.tt(DVE, y1[:, hs].rearrange("p (h d) -> p h d", h=8), pb[6 + hb][:, :].rearrange("p (h d) -> p h d", h=8),
                                E[:, 8 * hb:8 * hb + 8].unsqueeze(2).to_broadcast([128, 8, 64]), ALU.mult, [bpb[6 + hb], b_E], [b_y1])
                        self.tt(DVE, y1[:, hs], y1[:, hs], pb[4 + hb][:, :], ALU.add, [b_y1, bpb[4 + hb]], [b_y1])
                        self.tt(DVE, ytok[:, hs], y1[:, hs], ytok[:, hs], ALU.add, [b_y1, b_ytok], [b_ytok])
                    else:
                        self.tt(DVE, ytok[:, hs], pb[4 + hb][:, :], ytok[:, hs], ALU.add, [bpb[4 + hb], b_ytok], [b_ytok])
                if kt < NT - 1:
                    pbb = pb[0][:].bitcast(BF16)
                    for g in range(4):
                        self.tr(pbb[:, g * 128:(g + 1) * 128], BT[:, g, osl], [b_B], [bpb[0]])
                    self.act(Btok.rearrange("p g n -> p (g n)"), pbb[:, 0:512], AF.Copy, [bpb[0]], [b_Btok])
                    for g in range(4):
                        self.mm(pb[6 + g // 2][:, (g % 2) * 256:(g % 2 + 1) * 256], Btok[:, g, :],
                                xdd[:, 4 * g:4 * g + 4, :].rearrange("p h d -> p (h d)"), True, True, [b_Btok, b_xdd], [bpb[6 + g // 2]])
                    for hb in range(2):
                        hs = slice(hb * 512, (hb + 1) * 512)
                        if kt > 0:
                            s3 = state[:, hs].rearrange("p (h d) -> p h d", h=8)
                            self.tt(DVE, s3, s3, E[:, 32 + 8 * hb:32 + 8 * hb + 8].unsqueeze(2).to_broadcast([128, 8, 64]), ALU.mult,
                                    [b_state, b_E], [b_state])
                            self.tt(DVE, state[:, hs], state[:, hs], pb[6 + hb][:, :], ALU.add, [b_state, bpb[6 + hb]], [b_state])
                        else:
                            self.cp(DVE, state[:, hs], pb[6 + hb][:, :], [bpb[6 + hb]], [b_state])
                        self.cp(POOL, state_bf[:, hs], state[:, hs], [b_state], [b_sbf])
                pyb = pb[1][:].bitcast(BF16)
                for c in range(NCH):
                    self.tr(pyb[:, c * 128:(c + 1) * 128], ytok[:, c * 128:(c + 1) * 128], [b_ytok], [bpb[1]])
                self.tt(DVE, zsT[:, :, osl], pyb.rearrange("p (c t) -> p c t", c=8), zsT[:, :, osl], ALU.mult, [bpb[1], b_z], [b_z])
            for g in range(4):
                self.act(sq.rearrange("p c t -> p (c t)"), zsT[:, 2 * g:2 * g + 2, :].rearrange("p c t -> p (c t)"), AF.Square,
                         [b_z], [b_sq])
                self.mm(pb[2 + g % 2][:, 0:256], self.onesb[:], sq[:, 0, :], True, False, [self.b_const, b_sq], [bpb[2 + g % 2]])
                self.mm(pb[2 + g % 2][:, 0:256], self.onesb[:], sq[:, 1, :], False, True, [self.b_const, b_sq], [bpb[2 + g % 2]])
                self.act(rt, pb[2 + g % 2][:, 0:256], AF.Sqrt, [bpb[2 + g % 2]], [b_rt], scale=1.0 / 256, bias=1e-6)
                self.recip(rt, rt, [b_rt], [b_rt])
                for c2 in range(2):
                    c = 2 * g + c2
                    self.stt(self.ybr[:, c, tsl], zsT[:, c, :], snw[:, c:c + 1], rt, ALU.mult, ALU.mult, [b_z, b_snw, b_rt],
                             [self.b_ybr[c]])

    def diff(self, s, l):
        P = self.P
        lam_init = 0.8 - 0.6 * math.exp(-0.3 * l)
        cosE = self.carve(0, L)
        sinE = self.carve(2048, L)
        qT = self.carve(4096, L)
        kT = self.carve(6144, L)
        vaug = self.carve(8192, 2064).rearrange("p (t d) -> p t d", t=NT)
        sg = self.carve(10272, L)
        o = self.carve(12320, L, F32).rearrange("p (t d) -> p t d", t=NT)
        on = self.carve(16416, L).rearrange("p (t d) -> p t d", t=NT)
        pT = [self.carve(18464 + 512 * i, 512) for i in range(4)]
        xq = [self.carve(20512, 512), self.carve(21024, 512)]
        tf = [self.carve(21536, 512, F32), self.carve(22560, 512, F32)]
        pm = self.carve(23584, 128)
        b_tab = Buf("ropetab")
        b_q, b_k = Buf("dq"), Buf("dk")
        b_v = [Buf("dv%d" % i) for i in range(4)]
        b_sg = [Buf("dsg%d" % i) for i in range(4)]
        b_o = [Buf("o%d" % i) for i in range(NT)]
        b_on = [Buf("on%d" % i) for i in range(NT)]
        b_pT = [Buf("dpT%d" % i) for i in range(4)]
        b_xq = [Buf("xq0"), Buf("xq1")]
        b_tf = [Buf("tf0"), Buf("tf1")]
        b_pm = Buf("pm")
        b_rc = [Buf("drc%d" % i) for i in range(4)]
        b_lam = Buf("lam")
        b_ss2 = Buf("ss2")
        sm = self.small
        lamt = sm[:, 96:104]
        slw = sm[:, 104:106]
        ss2 = sm[:, 112:128]
        rt2 = sm[:, 128:144]
        rstd2 = sm[:, 144:160]
        P.dma(POOL, out=cosE, in_=self.rope_d[0], writes=[b_tab])
        P.dma(POOL, out=sinE, in_=self.rope_d[1], writes=[b_tab])
        self.memset(POOL, pm, 0.0, [b_pm])
        for a in range(2):
            P.op(POOL, lambda e, a=a: e.affine_select(out=pm[:, 64 * a:64 * a + 8], in_=self.onesb[:, 0:8], pattern=[[-1, 8]],
                                                      compare_op=ALU.is_equal, fill=0.0, base=-64 * a - 8, channel_multiplier=1),
                 [self.b_const], [b_pm])
            P.op(POOL, lambda e, a=a: e.affine_select(out=pm[:, 64 * a + 8:64 * a + 16], in_=self.onesb[:, 0:8], pattern=[[-1, 8]],
                                                      compare_op=ALU.is_equal, fill=0.0, base=-64 * a, channel_multiplier=1),
                 [self.b_const], [b_pm])
        dl = tf[0][:, 0:256]
        P.dma(SP, out=dl, in_=self.diff_lambda_d[l:l + 1].rearrange("o a d -> o (a d)").partition_broadcast(128), writes=[b_tf[0]])
        junk = tf[1][:, 0:64]
        self.stt(junk, dl[:, 0:64], 1.0, dl[:, 64:128], ALU.mult, ALU.mult, [b_tf[0]], [b_tf[1], b_lam], accum_out=lamt[:, 0:1])
        self.stt(junk, dl[:, 128:192], 1.0, dl[:, 192:256], ALU.mult, ALU.mult, [b_tf[0]], [b_tf[1], b_lam], accum_out=lamt[:, 1:2])
        self.act(lamt[:, 2:4], lamt[:, 0:2], AF.Exp, [b_lam], [b_lam])
        self.tt(DVE, lamt[:, 4:5], lamt[:, 2:3], lamt[:, 3:4], ALU.subtract, [b_lam], [b_lam])
        self.ts(DVE, lamt[:, 5:6], lamt[:, 4:5], -1.0, -lam_init, ALU.mult, ALU.add, [b_lam], [b_lam])
        neglam = lamt[:, 5:6]
        P.dma(SP, out=slw[:, 0:1], in_=self.subln_w_d[l:l + 1].rearrange("o p -> p o"), writes=[b_lam], allow_slow_non_contiguous=True)
        self.ts(DVE, slw[:, 0:1], slw[:, 0:1], (1.0 - lam_init) * 0.5, None, ALU.mult, None, [b_lam], [b_lam])
        for g4 in range(4):
            self.memset(DVE, vaug[:, 4 * g4:4 * g4 + 4, 128:129], 1.0, [b_v[g4]])
        pS = [self.pb[2], self.pb[3]]
        b_pS = [self.b_pb[2], self.b_pb[3]]
        pO = self.pb[4:8]
        b_pO = self.b_pb[4:8]
        sidx = 0
        k2 = 0
        for h in range(8):
            wq, bwq = self.wload(self.w_in_cols(l, C_DQ + 128 * h, 128), 128)
            wk, bwk = self.wload(self.w_in_cols(l, C_DK + 128 * h, 128), 128)
            wv, bwv = self.wload(self.w_in_cols(l, C_DV + 128 * h, 128), 128)
            wg, bwg = self.wload(self.w_in_cols(l, C_DG + 128 * h, 128), 128)
            for tb in range(4):
                sl = slice(tb * 512, (tb + 1) * 512)
                for (w_, bw_, dst, bdst, scale) in ((wq, bwq, qT, b_q, 0.125), (wk, bwk, kT, b_k, 1.0)):
                    ps, bps = self.pj()
                    self.proj_fm(w_, bw_, 128, tb, ps, bps)
                    x_, bx_ = xq[k2 % 2], b_xq[k2 % 2]
                    t1, bt1 = tf[k2 % 2], b_tf[k2 % 2]
                    k2 += 1
                    self.ts(DVE, x_, ps[:, :], scale, None, ALU.mult, None, [bps], [bx_])
                    S, bS = pS[sidx % 2], b_pS[sidx % 2]
                    sidx += 1
                    self.mm(S[:, :], pm, x_, True, True, [b_pm, bx_], [bS])
                    self.tt(DVE, t1, S[:, :], sinE[:, sl], ALU.mult, [bS, b_tab], [bt1])
                    self.tt(POOL, x_, x_, cosE[:, sl], ALU.mult, [bx_, b_tab], [bx_])
                    self.tt(DVE, dst[:, sl], t1, x_, ALU.add, [bt1, bx_], [bdst])
                ps, bps = self.pj()
                self.proj_fm(wg, bwg, 128, tb, ps, bps)
                t1, bt1 = tf[k2 % 2], b_tf[k2 % 2]
                k2 += 1
                self.act(t1, ps[:, :], AF.Tanh, [bps], [bt1], scale=0.5)
                self.stt(sg[:, sl], t1, 1.0, ps[:, :], ALU.add, ALU.mult, [bt1, bps], [b_sg[tb]])
            for g4 in range(4):
                ps, bps = self.pj()
                for u in range(4):
                    kt = 4 * g4 + u
                    for c in range(NCH):
                        self.mm(ps[:, u * 128:(u + 1) * 128], self.hT[:, c, kt * 128:(kt + 1) * 128], wv[:, c, :],
                                c == 0, c == NCH - 1, [bwv, self.b_hT[kt]], [bps])
                self.cp(DVE, vaug[:, 4 * g4:4 * g4 + 4, 0:128], ps[:, :].rearrange("p (t d) -> p t d", t=4), [bps], [b_v[g4]])
            tiles = [(comp, jq, i) for comp in range(2) for jq in range(4) for i in range(4 * jq + 4)]

            def emitS(tile, idx):
                comp, jq, i = tile
                pr = slice(64 * comp, 64 * comp + 64)
                r_ = i - 4 * jq
                qlo = max(512 * jq, 128 * i)
                w = 512 * (jq + 1) - qlo
                S, bS = pS[idx % 2], b_pS[idx % 2]
                pt, bpt = pT[idx % 4], b_pT[idx % 4]
                self.mm(S[:, 0:w], kT[pr, i * 128:(i + 1) * 128], qT[pr, qlo:qlo + w], True, r_ < 0, [b_k, b_q], [bS])
                if r_ >= 0:
                    self.mm(S[:, 0:128], self.ident[:], self.bmask[:], False, True, [self.b_const], [bS])
                self.act(pt[:, 0:w], S[:, 0:w], AF.Exp, [bS], [bpt])

            def emitPV(tile, idx):
                comp, jq, i = tile
                qlo = max(512 * jq, 128 * i)
                w = 512 * (jq + 1) - qlo
                pt, bpt = pT[idx % 4], b_pT[idx % 4]
                for cc in range(w // 128):
                    qc = qlo // 128 + cc
                    O, bO = pO[qc % 4], b_pO[qc % 4]
                    self.mm(O[:, 0:129], pt[:, cc * 128:(cc + 1) * 128], vaug[:, i, :], i == 0, i == qc,
                            [bpt, b_v[i // 4]], [bO])
                    if i == qc:
                        rcq = self.rc[:, qc:qc + 1]
                        self.recip(rcq, O[:, 128:129], [bO], [b_rc[qc % 4]])
                        if comp == 0:
                            self.ts(DVE, o[:, qc, :], O[:, 0:128], rcq, None, ALU.mult, None, [bO, b_rc[qc % 4]], [b_o[qc]])
                        else:
                            t1, bt1 = tf[qc % 2], b_tf[qc % 2]
                            self.ts(DVE, t1[:, 0:128], O[:, 0:128], rcq, neglam, ALU.mult, ALU.mult,
                                    [bO, b_rc[qc % 4], b_lam], [bt1])
                            self.tt(DVE, o[:, qc, :], t1[:, 0:128], o[:, qc, :], ALU.add, [bt1, b_o[qc]], [b_o[qc]])
                            self.stt(t1[:, 128:256], o[:, qc, :], 1.0, o[:, qc, :], ALU.mult, ALU.mult,
                                     [b_o[qc]], [bt1, b_ss2], accum_out=ss2[:, qc:qc + 1])

            for n_, tile in enumerate(tiles):
                emitS(tile, sidx + n_)
                if n_ > 0:
                    emitPV(tiles[n_ - 1], sidx + n_ - 1)
            emitPV(tiles[-1], sidx + len(tiles) - 1)
            sidx += len(tiles)
            self.act(rt2, ss2, AF.Sqrt, [b_ss2], [b_ss2], scale=1.0 / 128, bias=1e-6)
            self.recip(rstd2, rt2, [b_ss2], [b_ss2])
            for qc in range(NT):
                self.ts(POOL if qc % 2 else DVE, on[:, qc, :], o[:, qc, :], rstd2[:, qc:qc + 1], None, ALU.mult, None,
                        [b_o[qc], b_ss2], [b_on[qc]])
            for g4 in range(4):
                ps, bps = self.pj()
                psb = ps[:].bitcast(BF16)
                for u in range(4):
                    qc = 4 * g4 + u
                    self.tr(psb[:, u * 128:(u + 1) * 128], on[:, qc, :], [b_on[qc]], [bps])
                self.stt(self.ybr[:, h, g4 * 512:(g4 + 1) * 512], psb[:, 0:512], slw[:, 0:1], sg[:, g4 * 512:(g4 + 1) * 512],
                         ALU.mult, ALU.mult, [bps, b_sg[g4], b_lam], [self.b_ybr[h]])


def _rope_tables():
    pos = np.arange(L, dtype=np.float32)
    inv_freq = (np.float32(500000.0) ** (-np.arange(0, 16, 2, dtype=np.float32) / np.float32(16))).astype(np.float32)
    ang = (pos[:, None] * inv_freq[None, :]).astype(np.float32)
    cos = np.cos(ang).astype(np.float32)
    sin = np.sin(ang).astype(np.float32)
    cosE = np.ones((128, L), np.float32)
    sinE = np.zeros((128, L), np.float32)
    for m in range(128):
        i = m % 64
        if i < 8:
            cosE[m] = cos[:, i]
            sinE[m] = -sin[:, i]
        elif i < 16:
            cosE[m] = cos[:, i - 8]
            sinE[m] = sin[:, i - 8]
    return np.ascontiguousarray(np.stack([cosE, sinE]))


_NC_CACHE = {}


def kernel(x, norm_w, w_in, b_forget, conv_w, conv_b, dt_bias, a_log, d_skip, ssm_norm_w, diff_lambda, subln_w,
           w_branch, w_out, final_norm_w):
    ncores = 8
    nseq = x.shape[0] // ncores
    if 'nc' not in _NC_CACHE:
        _NC_CACHE['nc'] = Gen(nseq, 2, "abc", final=True).build()
    nc = _NC_CACHE['nc']
    f = lambda a: np.ascontiguousarray(np.asarray(a, dtype=np.float32))
    shared = dict(norm_w=f(norm_w), w_in=f(w_in), b_forget=f(b_forget), conv_w=f(conv_w), conv_b=f(conv_b),
                  dt_bias=f(dt_bias), a_log=f(a_log), d_skip=f(d_skip), ssm_norm_w=f(ssm_norm_w),
                  diff_lambda=f(diff_lambda), subln_w=f(subln_w), w_branch=f(w_branch), w_out=f(w_out),
                  final_norm_w=f(final_norm_w).reshape(1, -1), rope=_rope_tables())
    xs = f(x)
    in_maps = [dict(shared, x=np.ascontiguousarray(xs[i * nseq:(i + 1) * nseq])) for i in range(ncores)]
    res = run_bass_kernel_spmd(nc, in_maps, core_ids=list(range(ncores)))
    return np.concatenate([np.asarray(r["out"], dtype=np.float32) for r in res.results], axis=0)
```

```python
import numpy as np
import concourse.bass as bass
import concourse.mybir as mybir
from concourse.bass_utils import run_bass_kernel_spmd

F32 = mybir.dt.float32
BF16 = mybir.dt.bfloat16
AF = mybir.ActivationFunctionType
ALU = mybir.AluOpType
AX = mybir.AxisListType

PE, DVE, ACT, POOL, SP = 'tensor', 'vector', 'scalar', 'gpsimd', 'sync'
ENGS = [PE, DVE, ACT, POOL, SP]


class Buf:
    __slots__ = ('name', 'w', 'r', 'rd', 'dsem')

    def __init__(self, name):
        self.name = name
        self.w = None
        self.r = {}
        self.rd = []
        self.dsem = None


class Op:
    __slots__ = ('eng', 'fn', 'deps', 'dma_sem', 'tok', 'waited')

    def __init__(self, eng, fn):
        self.eng = eng
        self.fn = fn
        self.deps = []
        self.dma_sem = None
        self.tok = None
        self.waited = False


class DmaSem:
    def __init__(self, sem):
        self.sem = sem
        self.count = 0
        self.last = None
        self.q = None


class Prog:
    def __init__(self, nc, stack):
        self.nc = nc
        self.stack = stack
        self.ops = {e: [] for e in ENGS}
        self.esem = {}
        self.nops = 0
        self.dsems = []
        self.nsem = 0
        self.dpool = {}
        self.drr = {}

    def enter(self, cm):
        return self.stack.enter_context(cm)

    def sbuf(self, name, shape, dtype):
        return self.enter(self.nc.sbuf_tensor(name, list(shape), dtype))

    def psum(self, name, shape, dtype):
        return self.enter(self.nc.psum_tensor(name, list(shape), dtype))

    def dram(self, name, shape, dtype, kind):
        return self.nc.dram_tensor(name, list(shape), dtype, kind=kind).ap()

    def dsem(self, name=None):
        self.nsem += 1
        d = DmaSem(self.enter(self.nc.semaphore(name or ('ds%d' % self.nsem))))
        self.dsems.append(d)
        return d

    def op(self, eng, fn, reads=(), writes=(), dsem=None):
        o = Op(eng, fn)
        deps = []
        for b in reads:
            if b.w is not None:
                deps.append(b.w)
        for b in writes:
            if b.w is not None:
                deps.append(b.w)
            deps.extend(b.r.values())
            deps.extend(b.rd)
        seen = set()
        for d in deps:
            if id(d) in seen or d is o:
                continue
            seen.add(id(d))
            if d.dma_sem is None and d.eng == eng and eng == PE:
                continue
            o.deps.append(d)
        if dsem is not None:
            if dsem.last is not None and all(d is not dsem.last for d in o.deps):
                o.deps.append(dsem.last)
            o.dma_sem = dsem
            dsem.count += 16
            dsem.last = o
            o.tok = (dsem.sem, dsem.count)
        for b in reads:
            if dsem is not None:
                b.rd.append(o)
            else:
                b.r[eng] = o
        for b in writes:
            b.w = o
            b.r = {}
            b.rd = []
        self.ops[eng].append(o)
        self.nops += 1
        return o

    def dma(self, eng, out, in_, reads=(), writes=(), **kw):
        b = writes[0]
        if b.dsem is None or b.dsem.q != eng:
            pool = self.dpool.setdefault(eng, [])
            if len(pool) < 24:
                d = self.dsem()
                d.q = eng
                pool.append(d)
            else:
                d = pool[self.drr.get(eng, 0) % 24]
                self.drr[eng] = self.drr.get(eng, 0) + 1
            b.dsem = d
        return self.op(eng, lambda e: e.dma_start(out=out, in_=in_, **kw), reads=reads, writes=writes, dsem=b.dsem)

    def barrier(self):
        lasts = []
        for e in ENGS:
            if self.ops[e]:
                lasts.append(self.ops[e][-1])
        for d in self.dsems:
            if d.last is not None:
                lasts.append(d.last)
        new = []
        for e in ENGS:
            o = Op(e, lambda eng: eng.nop())
            for d in lasts:
                if d.dma_sem is None and d.eng == e:
                    continue
                o.deps.append(d)
            new.append(o)
        for o in new:
            self.ops[o.eng].append(o)

    def finish(self):
        o = Op(SP, lambda eng: eng.nop())
        for d in self.dsems:
            if d.last is not None:
                o.deps.append(d.last)
        self.ops[SP].append(o)
        self.emit()

    def emit(self):
        nc = self.nc
        for e in ENGS:
            self.esem[e] = self.enter(nc.semaphore('sem_' + e))
        for e in ENGS:
            for o in self.ops[e]:
                for d in o.deps:
                    d.waited = True
        self.nmile = {}
        for e in ENGS:
            c = 0
            for o in self.ops[e]:
                if o.dma_sem is None and o.waited:
                    c += 1
                    o.tok = (self.esem[e], c)
            self.nmile[e] = c
        prog = self

        def body(e):
            def run(eng):
                have = {}
                for o in prog.ops[e]:
                    need = {}
                    for d in o.deps:
                        s, v = d.tok
                        k = id(s)
                        if have.get(k, 0) >= v:
                            continue
                        if k not in need or need[k][1] < v:
                            need[k] = (s, v)
                    for k, (s, v) in need.items():
                        eng.wait_ge(s, v)
                        have[k] = v
                    ins = o.fn(eng)
                    if o.dma_sem is not None:
                        ins.then_inc(o.tok[0], 16)
                    elif o.waited:
                        ins.then_inc(o.tok[0], 1)
            return run

        with nc.Block() as block:
            block.tensor(body(PE))
            block.vector(body(DVE))
            block.scalar(body(ACT))
            block.gpsimd(body(POOL))
            block.sync(body(SP))


import math
from contextlib import ExitStack

L = 2048
D = 1024
NT = 16
NCH = 8
N_IN = 14368
C_FQ, C_FK, C_FV, C_FF, C_FG = 0, 1024, 2048, 3072, 3088
C_SZ, C_SX, C_SB, C_SC, C_SDT = 4112, 5136, 6160, 6672, 7184
C_DQ, C_DK, C_DV, C_DG, C_MG = 7200, 8224, 9248, 10272, 11296
NEG = -30000.0
ARENA = 24576
NSLOT = 6


class Gen:
    def __init__(self, nseq, nlayers, branches, final=True, dbg=None):
        self.nseq, self.nlayers, self.branches, self.final, self.dbg = nseq, nlayers, branches, final, dbg

    def mm(self, out, lhsT, rhs, start, stop, reads, writes):
        return self.P.op(PE, lambda e: e.matmul(out, lhsT=lhsT, rhs=rhs, start=start, stop=stop), reads, writes)

    def tr(self, out, in_, reads, writes):
        ident = self.ident
        return self.P.op(PE, lambda e: e.transpose(out=out, in_=in_, identity=ident[:]), list(reads) + [self.b_const], writes)

    def act(self, out, in_, func, reads, writes, **kw):
        return self.P.op(ACT, lambda e: e.activation(out=out, in_=in_, func=func, **kw), reads, writes)

    def tt(self, eng, out, in0, in1, op, reads, writes):
        return self.P.op(eng, lambda e: e.tensor_tensor(out=out, in0=in0, in1=in1, op=op), reads, writes)

    def ts(self, eng, out, in0, s1, s2, op0, op1, reads, writes, **kw):
        if op1 is None:
            return self.P.op(eng, lambda e: e.tensor_scalar(out=out, in0=in0, scalar1=s1, scalar2=None, op0=op0, **kw), reads, writes)
        return self.P.op(eng, lambda e: e.tensor_scalar(out=out, in0=in0, scalar1=s1, scalar2=s2, op0=op0, op1=op1, **kw), reads, writes)

    def stt(self, out, in0, scalar, in1, op0, op1, reads, writes, **kw):
        return self.P.op(DVE, lambda e: e.scalar_tensor_tensor(out=out, in0=in0, scalar=scalar, in1=in1, op0=op0, op1=op1, **kw), reads, writes)

    def cp(self, eng, out, in_, reads, writes):
        return self.P.op(eng, lambda e: e.tensor_copy(out=out, in_=in_), reads, writes)

    def memset(self, eng, ap, val, writes):
        return self.P.op(eng, lambda e: e.memset(ap, val), (), writes)

    def recip(self, out, in_, reads, writes):
        return self.P.op(DVE, lambda e: e.reciprocal(out=out, in_=in_), reads, writes)

    def carve(self, off, n, dtype=BF16):
        if dtype == F32:
            return self.arena[:, off:off + 2 * n].bitcast(F32)
        return self.arena[:, off:off + n]

    def wload(self, src, n):
        i = self.wrr % NSLOT
        self.wrr += 1
        ap = self.wslot[i][:, :, 0:n]
        buf = self.b_wslot[i]
        self.P.dma(POOL, out=ap, in_=src.rearrange("(c p) n -> p c n", p=128), writes=[buf])
        return ap, buf

    def w_in_cols(self, l, c0, n):
        return self.w_in_d[l, :, c0:c0 + n]

    def pj(self):
        i = self.pjrr % 2
        self.pjrr += 1
        return self.pb[i], self.b_pb[i]

    def hbufs(self, tb):
        return self.b_hT[4 * tb:4 * tb + 4]

    def proj_fm(self, w, bw, M, tb, ps, bps, rhs_src=None, rhs_bufs=None):
        src = self.hT if rhs_src is None else rhs_src
        rb = self.hbufs(tb) if rhs_bufs is None else rhs_bufs
        for c in range(NCH):
            self.mm(ps[0:M, :], w[:, c, 0:M], src[:, c, tb * 512:(tb + 1) * 512], c == 0, c == NCH - 1,
                    [bw] + list(rb), [bps])

    def build(self):
        nc = bass.Bass("TRN2", target_bir_lowering=False)
        self.nc = nc
        nseq = self.nseq
        di = lambda name, shape: nc.dram_tensor(name, list(shape), F32, kind="ExternalInput").ap()
        self.x_d = di("x", [nseq, L, D])
        self.norm_w_d = di("norm_w", [2, D])
        self.w_in_d = di("w_in", [2, D, N_IN])
        self.b_forget_d = di("b_forget", [2, 16])
        self.conv_w_d = di("conv_w", [2, 4, 2048])
        self.conv_b_d = di("conv_b", [2, 2048])
        self.dt_bias_d = di("dt_bias", [2, 16])
        self.a_log_d = di("a_log", [2, 16])
        self.d_skip_d = di("d_skip", [2, 16])
        self.ssm_norm_w_d = di("ssm_norm_w", [2, D])
        self.diff_lambda_d = di("diff_lambda", [2, 4, 64])
        self.subln_w_d = di("subln_w", [2, 128])
        self.w_branch_d = di("w_branch", [2, 3, D, D])
        self.w_out_d = di("w_out", [2, D, D])
        self.final_norm_w_d = di("final_norm_w", [1, D])
        self.rope_d = di("rope", [2, 128, L])
        self.out_d = nc.dram_tensor("out", [nseq, L, D], F32, kind="ExternalOutput").ap()
        self.gk_d = nc.dram_tensor("gk_scr", [16, 6, L], BF16, kind="Internal").ap()
        self.scr_d = nc.dram_tensor("ssd_scr", [3072, L], BF16, kind="ExternalOutput").ap()
        if self.dbg:
            self.dbg_d = nc.dram_tensor("dbg", [128, 8, L], BF16, kind="ExternalOutput").ap()
        with ExitStack() as st:
            P = Prog(nc, st)
            self.P = P
            self.xres = P.sbuf("xres", [128, NT, D], F32)
            self.hT = P.sbuf("hT", [128, NCH, L], BF16)
            self.ybr = P.sbuf("ybr", [128, NCH, L], BF16)
            self.wout = P.sbuf("wout", [128, NCH, D], BF16)
            self.arena = P.sbuf("arena", [128, ARENA], BF16)
            self.wslot = [P.sbuf("wslot%d" % i, [128, NCH, 128], BF16) for i in range(NSLOT)]
            self.ident = P.sbuf("ident", [128, 128], BF16)
            self.cmask = P.sbuf("cmask", [128, 128], BF16)
            self.bmask = P.sbuf("bmask", [128, 128], BF16)
            self.onesb = P.sbuf("onesb", [128, 128], BF16)
            self.zerob = P.sbuf("zerob", [128, 128], BF16)
            self.small = P.sbuf("small", [128, 256], F32)
            self.pb = [P.psum("pb%d" % i, [128, 512], F32) for i in range(8)]
            self.b_pb = [Buf("pb%d" % i) for i in range(8)]
            self.b_x = [Buf("x%d" % t) for t in range(NT)]
            self.b_hT = [Buf("hT%d" % t) for t in range(NT)]
            self.b_ybr = [Buf("ybr%d" % c) for c in range(NCH)]
            self.b_wout = Buf("wout")
            self.b_wslot = [Buf("ws%d" % i) for i in range(NSLOT)]
            self.b_const = Buf("const")
            self.b_small = Buf("small")
            self.b_gk = Buf("gk")
            self.b_out = [Buf("out%d" % t) for t in range(NT)]
            self.wrr = 0
            self.pjrr = 0
            sm = self.small
            self.nwT = sm[:, 0:16]
            self.negbf = sm[0:16, 16:18]
            self.ss = sm[:, 32:48]
            self.rt = sm[:, 48:64]
            self.rstd = sm[:, 64:80]
            self.rc = sm[:, 80:96]
            self.consts()
            for s in range(nseq):
                for t in range(NT):
                    P.dma(SP, out=self.xres[:, t, :], in_=self.x_d[s, t * 128:(t + 1) * 128, :], writes=[self.b_x[t]])
                for l in range(self.nlayers):
                    self.layer(s, l)
                self.final_out(s)
            P.finish()
        return nc

    def consts(self):
        P = self.P
        bc = self.b_const
        self.memset(POOL, self.onesb[:], 1.0, [bc])
        self.memset(POOL, self.zerob[:], 0.0, [bc])
        P.op(POOL, lambda e: e.affine_select(out=self.ident[:], in_=self.onesb[:], pattern=[[-1, 128]], compare_op=ALU.is_equal,
                                              fill=0.0, base=0, channel_multiplier=1), [bc], [bc])
        P.op(POOL, lambda e: e.affine_select(out=self.cmask[:], in_=self.zerob[:], pattern=[[1, 128]], compare_op=ALU.is_ge,
                                              fill=NEG, base=0, channel_multiplier=-1), [bc], [bc])
        self.memset(POOL, self.bmask[:], 0.0, [bc])
        self.memset(POOL, self.bmask[64:128, 0:64], NEG, [bc])
        P.dma(SP, out=self.nwT.rearrange("p (l c) -> p l c", l=2), in_=self.norm_w_d.rearrange("l (c p) -> p l c", p=128),
              writes=[self.b_small], allow_slow_non_contiguous=True)
        P.dma(SP, out=self.negbf, in_=self.b_forget_d.rearrange("l h -> h l"), writes=[self.b_small], allow_slow_non_contiguous=True)
        self.ts(DVE, self.negbf, self.negbf, -1.0, None, ALU.mult, None, [self.b_small], [self.b_small])

    def layer(self, s, l):
        P = self.P
        for eb in range(2):
            P.dma(POOL, out=self.wout[:, :, eb * 512:(eb + 1) * 512],
                  in_=self.w_out_d[l, :, eb * 512:(eb + 1) * 512].rearrange("(c p) n -> p c n", p=128), writes=[self.b_wout])
        P.barrier()
        self.rmsnorm_T(l)
        for n, br in enumerate("abc"):
            if br not in self.branches:
                continue
            P.barrier()
            if br == 'a':
                self.fox(s, l)
            elif br == 'b':
                self.ssd(s, l)
            else:
                self.diff(s, l)
            if self.dbg == br and l == self.nlayers - 1:
                for c in range(NCH):
                    P.dma(SP, out=self.dbg_d[:, c, :], in_=self.ybr[:, c, :], reads=[self.b_ybr[c]], writes=[Buf("dbg%d" % c)])
            P.barrier()
            self.merge(l, n)

    def rmsnorm_T(self, l):
        junk = self.carve(0, 1024)
        hn = [self.carve(1024, 1024), self.carve(2048, 1024)]
        b_junk = Buf("junk")
        b_hn = [Buf("hn0"), Buf("hn1")]
        b_ss = Buf("ss")
        for t in range(NT):
            self.act(junk, self.xres[:, t, :], AF.Square, [self.b_x[t]], [b_junk, b_ss], accum_out=self.ss[:, t:t + 1])
        self.act(self.rt, self.ss, AF.Sqrt, [b_ss], [b_ss], scale=1.0 / D, bias=1e-6)
        self.recip(self.rstd, self.rt, [b_ss], [b_ss])
        for t in range(NT):
            k = t % 2
            self.act(hn[k], self.xres[:, t, :], AF.Copy, [self.b_x[t], b_ss], [b_hn[k]], scale=self.rstd[:, t:t + 1])
            ps, bps = self.pj()
            psb = ps[:].bitcast(BF16)
            for c in range(NCH):
                self.tr(psb[:, c * 128:(c + 1) * 128], hn[k][:, c * 128:(c + 1) * 128], [b_hn[k]], [bps])
            self.tt(DVE, self.hT[:, :, t * 128:(t + 1) * 128], psb.rearrange("p (c n) -> p c n", c=NCH),
                    self.nwT[:, l * 8:(l + 1) * 8].unsqueeze(2).to_broadcast([128, NCH, 128]), ALU.mult,
                    [bps, self.b_small], [self.b_hT[t]])

    def merge(self, l, n):
        gp = self.carve(0, NCH * L).rearrange("p (c t) -> p c t", c=NCH)
        th = [self.carve(NCH * L, 512, F32), self.carve(NCH * L + 1024, 512, F32)]
        b_gp = [Buf("gp%d" % t) for t in range(4)]
        b_th = [Buf("th0"), Buf("th1")]
        k = 0
        for co in range(NCH):
            wb, bwb = self.wload(self.w_branch_d[l, n, :, co * 128:(co + 1) * 128], 128)
            wg, bwg = self.wload(self.w_in_cols(l, C_MG + n * 1024 + co * 128, 128), 128)
            for tb in range(4):
                ps1, bps1 = self.pj()
                self.proj_fm(wb, bwb, 128, tb, ps1, bps1, rhs_src=self.ybr, rhs_bufs=self.b_ybr)
                ps2, bps2 = self.pb[2 + k % 2], self.b_pb[2 + k % 2]
                self.proj_fm(wg, bwg, 128, tb, ps2, bps2)
                t_ = th[k % 2]
                self.act(t_, ps2[:, :], AF.Tanh, [bps2], [b_th[k % 2]], scale=0.5)
                self.stt(gp[:, co, tb * 512:(tb + 1) * 512], t_, 1.0, ps1[:, :], ALU.add, ALU.mult,
                         [b_th[k % 2], bps1], [b_gp[tb]])
                k += 1
        for t in range(NT):
            for eb in range(2):
                ps, bps = self.pb[4 + (2 * t + eb) % 4], self.b_pb[4 + (2 * t + eb) % 4]
                for co in range(NCH):
                    self.mm(ps[:, :], gp[:, co, t * 128:(t + 1) * 128], self.wout[:, co, eb * 512:(eb + 1) * 512],
                            co == 0, co == NCH - 1, [b_gp[t // 4], self.b_wout], [bps])
                xs = self.xres[:, t, eb * 512:(eb + 1) * 512]
                self.stt(xs, ps[:, :], 0.5, xs, ALU.mult, ALU.add, [bps, self.b_x[t]], [self.b_x[t]])

    def final_out(self, s):
        P = self.P
        P.barrier()
        b_ss = Buf("fss")
        if self.final:
            junk = self.carve(0, 1024)
            fnw = self.carve(1024, 1024, F32)
            b_junk, b_fnw = Buf("fjunk"), Buf("fnw")
            P.dma(SP, out=fnw, in_=self.final_norm_w_d.partition_broadcast(128), writes=[b_fnw])
            for t in range(NT):
                self.act(junk, self.xres[:, t, :], AF.Square, [self.b_x[t]], [b_junk, b_ss], accum_out=self.ss[:, t:t + 1])
            self.act(self.rt, self.ss, AF.Sqrt, [b_ss], [b_ss], scale=1.0 / D, bias=1e-6)
            self.recip(self.rstd, self.rt, [b_ss], [b_ss])
            for t in range(NT):
                xs = self.xres[:, t, :]
                self.stt(xs, xs, self.rstd[:, t:t + 1], fnw, ALU.mult, ALU.mult, [self.b_x[t], b_ss, b_fnw], [self.b_x[t]])
        for t in range(NT):
            P.dma(SP, out=self.out_d[s, t * 128:(t + 1) * 128, :], in_=self.xres[:, t, :], reads=[self.b_x[t]], writes=[self.b_out[t]])

    def fox(self, s, l):
        P = self.P
        lf = self.carve(0, L, F32)[0:16, :]
        G = self.carve(4096, L, F32)[0:16, :]
        r = self.carve(8192, L, F32)[0:16, :]
        parts = [self.carve(12288 + 2048 * j, L)[0:16, :] for j in range(6)]
        b_lf, b_G, b_r = Buf("lf"), Buf("G"), Buf("r")
        b_parts = [Buf("part%d" % j) for j in range(6)]
        wff, bwff = self.wload(self.w_in_cols(l, C_FF, 16), 16)
        for tb in range(4):
            ps, bps = self.pj()
            self.proj_fm(wff, bwff, 16, tb, ps, bps)
            self.act(lf[:, tb * 512:(tb + 1) * 512], ps[0:16, :], AF.Exp, [bps, self.b_small], [b_lf],
                     scale=-1.0, bias=self.negbf[:, l:l + 1])
        self.act(lf, lf, AF.Ln, [b_lf], [b_lf], bias=1.0)
        P.op(DVE, lambda e: e.tensor_tensor_scan(out=G, data0=lf, data1=lf, initial=0.0, op0=ALU.add, op1=ALU.max), [b_lf], [b_G])
        self.cp(DVE, parts[0], G, [b_G], [b_parts[0]])
        self.tt(DVE, r, G, parts[0], ALU.subtract, [b_G, b_parts[0]], [b_r])
        self.cp(DVE, parts[1], r, [b_r], [b_parts[1]])
        self.tt(DVE, r, r, parts[1], ALU.subtract, [b_r, b_parts[1]], [b_r])
        self.cp(DVE, parts[2], r, [b_r], [b_parts[2]])
        for j in range(3):
            self.ts(DVE, parts[3 + j], parts[j], -1.0, None, ALU.mult, None, [b_parts[j]], [b_parts[3 + j]])
        for j in range(6):
            P.dma(SP, out=self.gk_d[:, j, :], in_=parts[j], reads=[b_parts[j]], writes=[self.b_gk])
        P.barrier()
        qaug = [self.carve(0, L), self.carve(2048, L)]
        kaug = [self.carve(4096, L), self.carve(6144, L)]
        vaug = self.carve(8192, 2080).rearrange("p (t h d) -> p t h d", t=NT, h=2)
        sg = self.carve(10368, L)
        opair = self.carve(12416, L).rearrange("p (t d) -> p t d", t=NT)
        pT = [self.carve(14464 + 512 * i, 512) for i in range(4)]
        th = [self.carve(16512, 512, F32), self.carve(17536, 512, F32)]
        b_q = [Buf("qaug0"), Buf("qaug1")]
        b_k = [Buf("kaug0"), Buf("kaug1")]
        b_v = [Buf("vaug%d" % i) for i in range(4)]
        b_sg = [Buf("sg%d" % i) for i in range(4)]
        b_op = [Buf("op%d" % i) for i in range(NT)]
        b_pT = [Buf("pT%d" % i) for i in range(4)]
        b_th = [Buf("fth0"), Buf("fth1")]
        b_rc = [Buf("rc%d" % i) for i in range(4)]
        for hh in range(2):
            self.memset(DVE, kaug[hh][64:67, :], 1.0, [b_k[hh]])
            self.memset(DVE, qaug[hh][64:67, :], 1.0, [b_q[hh]])
            P.dma(SP, out=qaug[hh][67:70, :], in_=qaug[hh][64:67, :], reads=[b_q[hh]], writes=[b_q[hh]])
        for g4 in range(4):
            self.memset(DVE, vaug[:, 4 * g4:4 * g4 + 4, :, 64:65], 1.0, [b_v[g4]])
        pS = [self.pb[2], self.pb[3]]
        b_pS = [self.b_pb[2], self.b_pb[3]]
        pO = self.pb[4:8]
        b_pO = self.b_pb[4:8]
        sidx = 0
        thk = 0
        for j in range(8):
            wq, bwq = self.wload(self.w_in_cols(l, C_FQ + 128 * j, 128), 128)
            wk, bwk = self.wload(self.w_in_cols(l, C_FK + 128 * j, 128), 128)
            wv, bwv = self.wload(self.w_in_cols(l, C_FV + 128 * j, 128), 128)
            wg, bwg = self.wload(self.w_in_cols(l, C_FG + 128 * j, 128), 128)
            for hh in range(2):
                h = 2 * j + hh
                P.dma(SP, out=kaug[hh][67:70, :], in_=self.gk_d[h, 0:3, :], reads=[self.b_gk], writes=[b_k[hh]])
                P.dma(SP, out=qaug[hh][64:67, :], in_=self.gk_d[h, 3:6, :], reads=[self.b_gk], writes=[b_q[hh]])
            for tb in range(4):
                sl = slice(tb * 512, (tb + 1) * 512)
                ps, bps = self.pj()
                self.proj_fm(wq, bwq, 128, tb, ps, bps)
                for hh in range(2):
                    self.ts(DVE, qaug[hh][0:64, sl], ps[64 * hh:64 * hh + 64, :], 0.125, None, ALU.mult, None, [bps], [b_q[hh]])
                ps, bps = self.pj()
                self.proj_fm(wk, bwk, 128, tb, ps, bps)
                for hh in range(2):
                    self.cp(DVE, kaug[hh][0:64, sl], ps[64 * hh:64 * hh + 64, :], [bps], [b_k[hh]])
                ps, bps = self.pj()
                self.proj_fm(wg, bwg, 128, tb, ps, bps)
                t_ = th[thk % 2]
                self.act(t_, ps[:, :], AF.Tanh, [bps], [b_th[thk % 2]], scale=0.5)
                self.stt(sg[:, sl], t_, 1.0, ps[:, :], ALU.add, ALU.mult, [b_th[thk % 2], bps], [b_sg[tb]])
                thk += 1
            for g4 in range(4):
                ps, bps = self.pj()
                for u in range(4):
                    kt = 4 * g4 + u
                    for c in range(NCH):
                        self.mm(ps[:, u * 128:(u + 1) * 128], self.hT[:, c, kt * 128:(kt + 1) * 128], wv[:, c, :],
                                c == 0, c == NCH - 1, [bwv, self.b_hT[kt]], [bps])
                self.cp(DVE, vaug[:, 4 * g4:4 * g4 + 4, :, 0:64], ps[:, :].rearrange("p (t h d) -> p t h d", t=4, h=2),
                        [bps], [b_v[g4]])
            tiles = [(hh, jq, i) for hh in range(2) for jq in range(4) for i in range(4 * jq + 4)]

            def emitS(tile, idx):
                hh, jq, i = tile
                r_ = i - 4 * jq
                qlo = max(512 * jq, 128 * i)
                w = 512 * (jq + 1) - qlo
                S, bS = pS[idx % 2], b_pS[idx % 2]
                pt, bpt = pT[idx % 4], b_pT[idx % 4]
                self.mm(S[:, 0:w], kaug[hh][0:70, i * 128:(i + 1) * 128], qaug[hh][0:70, qlo:qlo + w],
                        True, r_ < 0, [b_k[hh], b_q[hh]], [bS])
                if r_ >= 0:
                    self.mm(S[:, 0:128], self.ident[:], self.cmask[:], False, True, [self.b_const], [bS])
                self.act(pt[:, 0:w], S[:, 0:w], AF.Exp, [bS], [bpt])

            def emitPV(tile, idx):
                hh, jq, i = tile
                qlo = max(512 * jq, 128 * i)
                w = 512 * (jq + 1) - qlo
                pt, bpt = pT[idx % 4], b_pT[idx % 4]
                for cc in range(w // 128):
                    qc = qlo // 128 + cc
                    O, bO = pO[qc % 4], b_pO[qc % 4]
                    self.mm(O[:, 0:65], pt[:, cc * 128:(cc + 1) * 128], vaug[:, i, hh, :], i == 0, i == qc,
                            [bpt, b_v[i // 4]], [bO])
                    if i == qc:
                        self.recip(self.rc[:, qc:qc + 1], O[:, 64:65], [bO], [b_rc[qc % 4]])
                        self.ts(DVE, opair[:, qc, 64 * hh:64 * hh + 64], O[:, 0:64], self.rc[:, qc:qc + 1], 0.5,
                                ALU.mult, ALU.mult, [bO, b_rc[qc % 4]], [b_op[qc]])

            for n_, tile in enumerate(tiles):
                emitS(tile, sidx + n_)
                if n_ > 0:
                    emitPV(tiles[n_ - 1], sidx + n_ - 1)
            emitPV(tiles[-1], sidx + len(tiles) - 1)
            sidx += len(tiles)
            for g4 in range(4):
                ps, bps = self.pj()
                psb = ps[:].bitcast(BF16)
                for u in range(4):
                    qc = 4 * g4 + u
                    self.tr(psb[:, u * 128:(u + 1) * 128], opair[:, qc, :], [b_op[qc]], [bps])
                self.tt(DVE, self.ybr[:, j, g4 * 512:(g4 + 1) * 512], psb[:, 0:512], sg[:, g4 * 512:(g4 + 1) * 512], ALU.mult,
                        [bps, b_sg[g4]], [self.b_ybr[j]])

    def ssd(self, s, l):
        P = self.P
        scr = self.scr_d
        HB = 16
        dt_all = self.carve(0, 256, F32)
        dA_all = self.carve(512, 256, F32)
        bc3 = self.carve(1024, 48, F32)
        cwT = self.carve(1120, 80, F32).rearrange("p (c j) -> p c j", c=16)
        snw = self.carve(1280, 8, F32)
        Lm = self.carve(1296, 128, F32)
        U = self.carve(1552, 128, F32)
        onesf = self.carve(1808, 128, F32)
        mask01 = self.carve(2064, 128)
        E = self.carve(2192, 48, F32)
        b_dt, b_bc3, b_cw, b_snw, b_cm, b_E = Buf("dt_all"), Buf("bc3"), Buf("cwT"), Buf("snw"), Buf("ssdconst"), Buf("E")
        self.memset(POOL, onesf, 1.0, [b_cm])
        P.op(POOL, lambda e: e.affine_select(out=Lm, in_=self.onesb[:], pattern=[[1, 128]], compare_op=ALU.is_ge, fill=0.0,
                                              base=0, channel_multiplier=-1), [self.b_const], [b_cm])
        P.op(POOL, lambda e: e.affine_select(out=mask01, in_=self.onesb[:], pattern=[[1, 128]], compare_op=ALU.is_ge, fill=0.0,
                                              base=0, channel_multiplier=-1), [self.b_const], [b_cm])
        P.op(POOL, lambda e: e.affine_select(out=U, in_=self.onesb[:], pattern=[[-1, 128]], compare_op=ALU.is_gt, fill=0.0,
                                              base=0, channel_multiplier=1), [self.b_const], [b_cm])
        P.dma(SP, out=bc3[:, 0:16], in_=self.dt_bias_d[l:l + 1].partition_broadcast(128), writes=[b_bc3])
        P.dma(SP, out=bc3[:, 16:32], in_=self.a_log_d[l:l + 1].partition_broadcast(128), writes=[b_bc3])
        P.dma(SP, out=bc3[:, 32:48], in_=self.d_skip_d[l:l + 1].partition_broadcast(128), writes=[b_bc3])
        self.act(bc3[:, 16:32], bc3[:, 16:32], AF.Exp, [b_bc3], [b_bc3])
        self.ts(DVE, bc3[:, 16:32], bc3[:, 16:32], -1.0, None, ALU.mult, None, [b_bc3], [b_bc3])
        for j in range(4):
            P.dma(SP, out=cwT[:, :, j:j + 1], in_=self.conv_w_d[l, j:j + 1, :].rearrange("o (c p) -> p c o", p=128), writes=[b_cw],
                  allow_slow_non_contiguous=True)
        P.dma(SP, out=cwT[:, :, 4:5], in_=self.conv_b_d[l:l + 1, :].rearrange("o (c p) -> p c o", p=128), writes=[b_cw],
              allow_slow_non_contiguous=True)
        P.dma(SP, out=snw, in_=self.ssm_norm_w_d[l:l + 1, :].rearrange("o (c p) -> p (o c)", p=128), writes=[b_snw],
              allow_slow_non_contiguous=True)
        wdt, bwdt = self.wload(self.w_in_cols(l, C_SDT, 16), 16)
        for g4 in range(4):
            ps, bps = self.pj()
            for u_ in range(4):
                kt = 4 * g4 + u_
                for c in range(NCH):
                    self.mm(ps[:, u_ * 16:(u_ + 1) * 16], self.hT[:, c, kt * 128:(kt + 1) * 128], wdt[:, c, 0:16],
                            c == 0, c == NCH - 1, [bwdt, self.b_hT[kt]], [bps])
            self.tt(DVE, dt_all[:, g4 * 64:(g4 + 1) * 64].rearrange("p (t h) -> p t h", t=4),
                    ps[:, 0:64].rearrange("p (t h) -> p t h", t=4), bc3[:, 0:16].unsqueeze(1).to_broadcast([128, 4, 16]), ALU.add,
                    [bps, b_bc3], [b_dt])
        self.act(dt_all, dt_all, AF.Exp, [b_dt], [b_dt])
        self.act(dt_all, dt_all, AF.Ln, [b_dt], [b_dt], bias=1.0)
        self.tt(DVE, dA_all.rearrange("p (t h) -> p t h", t=NT), dt_all.rearrange("p (t h) -> p t h", t=NT),
                bc3[:, 16:32].unsqueeze(1).to_broadcast([128, NT, 16]), ALU.mult, [b_dt, b_bc3], [b_dt])
        if getattr(self, 'ssd_level', 9) < 1:
            return
        cins = [self.carve(2304, 2080, F32), self.carve(6464, 2080, F32)]
        cout = [self.carve(10624, L), self.carve(12672, L)]
        pa = [self.carve(14720 + 1024 * i_, 512, F32) for i_ in range(4)]
        b_cin, b_cout = [Buf("cin0"), Buf("cin1")], [Buf("cout0"), Buf("cout1")]
        b_pa = [Buf("pa%d" % i_) for i_ in range(4)]
        b_scr = [Buf("scr%d" % i) for i in range(24)]
        PAD = 4
        for i_ in range(2):
            self.memset(DVE, cins[i_][:, 0:PAD], 0.0, [b_cin[i_]])
        kk = [0]
        wl = {}

        def emit_proj(cc, tb):
            if tb == 0:
                wl[cc] = self.wload(self.w_in_cols(l, C_SX + 128 * cc, 128), 128)
            w_, bw_ = wl[cc]
            cin, b_u = cins[cc % 2], b_cin[cc % 2]
            ps, bps = self.pj()
            self.proj_fm(w_, bw_, 128, tb, ps, bps)
            self.act(cin[:, PAD + tb * 512:PAD + (tb + 1) * 512], ps[:, :], AF.Copy, [bps], [b_u])

        def emit_conv(cc, tb):
            k = kk[0]
            kk[0] += 1
            cin, b_u = cins[cc % 2], b_cin[cc % 2]
            co, bco = cout[cc % 2], b_cout[cc % 2]
            a_, ba_ = self.pb[2 + k % 4], self.b_pb[2 + k % 4]
            p_, bp_ = pa[k % 4], b_pa[k % 4]
            o0 = tb * 512 + PAD - 3
            self.act(a_[:, :], cin[:, o0 + 3:o0 + 3 + 512], AF.Identity, [b_u, b_cw], [ba_], scale=cwT[:, cc, 3:4], bias=cwT[:, cc, 4:5])
            self.ts(POOL, p_, cin[:, o0 + 1:o0 + 1 + 512], cwT[:, cc, 1:2], 0.0, ALU.mult, ALU.add, [b_u, b_cw], [bp_])
            self.stt(a_[:, :], cin[:, o0 + 2:o0 + 2 + 512], cwT[:, cc, 2:3], a_[:, :], ALU.mult, ALU.add, [b_u, b_cw, ba_], [ba_])
            self.stt(a_[:, :], cin[:, o0:o0 + 512], cwT[:, cc, 0:1], a_[:, :], ALU.mult, ALU.add, [b_u, b_cw, ba_], [ba_])
            self.tt(DVE, a_[:, :], a_[:, :], p_, ALU.add, [ba_, bp_], [ba_])
            pend.append((cc, tb, a_, ba_))

        def emit_back():
            cc, tb, a_, ba_ = pend.pop(0)
            co, bco = cout[cc % 2], b_cout[cc % 2]
            self.act(co[:, tb * 512:(tb + 1) * 512], a_[:, :], AF.Silu, [ba_], [bco])
            if tb == 3:
                P.dma(SP, out=scr[cc * 128:(cc + 1) * 128, :], in_=co, reads=[bco], writes=[b_scr[cc]])

        pend = []
        for tb in range(4):
            emit_proj(0, tb)
        for cc in range(16):
            for tb in range(4):
                if cc + 1 < 16:
                    emit_proj(cc + 1, tb)
                emit_conv(cc, tb)
                if len(pend) > 2:
                    emit_back()
        while pend:
            emit_back()
        for c in range(NCH):
            w_, bw_ = self.wload(self.w_in_cols(l, C_SZ + 128 * c, 128), 128)
            co, bco = cout[c % 2], b_cout[c % 2]
            for tb in range(4):
                ps, bps = self.pj()
                self.proj_fm(w_, bw_, 128, tb, ps, bps)
                self.act(co[:, tb * 512:(tb + 1) * 512], ps[:, :], AF.Silu, [bps], [bco])
            P.dma(SP, out=scr[2048 + c * 128:2048 + (c + 1) * 128, :], in_=co, reads=[bco], writes=[b_scr[16 + c]])
        if getattr(self, 'ssd_level', 9) < 2:
            return
        P.barrier()
        xsT = self.carve(2304, 2048).rearrange("p (c t) -> p c t", c=8)
        BT = self.carve(4352, 1024).rearrange("p (c t) -> p c t", c=4)
        CT = self.carve(5376, 1024).rearrange("p (c t) -> p c t", c=4)
        zsT = self.carve(6400, 2048).rearrange("p (c t) -> p c t", c=8)
        rhsseg = self.carve(8448, 1024, F32).rearrange("p (h l) -> p h l", h=8)
        dec = self.carve(10496, 2048).rearrange("p (h l) -> p h l", h=16)
        CBm = self.carve(12544, 512).rearrange("p (g l) -> p g l", g=4)
        xdt = self.carve(13056, 1024).rearrange("p (h d) -> p h d", h=16)
        xdd = self.carve(14080, 1024).rearrange("p (h d) -> p h d", h=16)
        ytok = self.carve(15104, 1024)
        y1 = self.carve(16128, 1024, F32)
        Btok = self.carve(18176, 512).rearrange("p (g n) -> p g n", g=4)
        state = self.carve(18688, 1024, F32)
        state_bf = self.carve(20736, 1024)
        sq = self.carve(21760, 512).rearrange("p (c t) -> p c t", c=2)
        rt = self.carve(22272, 256, F32)
        b_xs, b_B, b_C, b_z = Buf("xsT"), Buf("BT"), Buf("CT"), Buf("zsT")
        b_rs, b_dec, b_CBm, b_xdt, b_xdd, b_ytok, b_y1 = [Buf(n) for n in "rhsseg dec CBm xdt xdd ytok y1".split()]
        b_Btok, b_state, b_sbf, b_sq, b_rt = [Buf(n) for n in "Btok state state_bf sq rt".split()]
        pb, bpb = self.pb, self.b_pb
        for bk in range(8):
            tsl = slice(bk * 256, (bk + 1) * 256)
            P.dma(SP, out=xsT, in_=scr[0:1024, tsl].rearrange("(c p) t -> p c t", p=128), reads=b_scr[0:8], writes=[b_xs])
            P.dma(SP, out=BT, in_=scr[1024:1536, tsl].rearrange("(c p) t -> p c t", p=128), reads=b_scr[8:12], writes=[b_B])
            P.dma(SP, out=CT, in_=scr[1536:2048, tsl].rearrange("(c p) t -> p c t", p=128), reads=b_scr[12:16], writes=[b_C])
            P.dma(SP, out=zsT, in_=scr[2048:3072, tsl].rearrange("(c p) t -> p c t", p=128), reads=b_scr[16:24], writes=[b_z])
            for c2 in range(2):
                kt = 2 * bk + c2
                osl = slice(c2 * 128, (c2 + 1) * 128)
                dAk = dA_all[:, kt * 16:(kt + 1) * 16]
                self.mm(pb[0][:, 0:16], Lm, dAk, True, True, [b_cm, b_dt], [bpb[0]])
                self.mm(pb[0][:, 16:32], U, dAk, True, True, [b_cm, b_dt], [bpb[0]])
                self.mm(pb[0][:, 32:48], onesf, dAk, True, True, [b_cm, b_dt], [bpb[0]])
                self.act(E, pb[0][:, 0:48], AF.Exp, [bpb[0]], [b_E])
                for half in range(2):
                    self.tt(POOL, rhsseg, Lm.unsqueeze(1).to_broadcast([128, 8, 128]),
                            dAk[:, 8 * half:8 * half + 8].unsqueeze(2).to_broadcast([128, 8, 128]), ALU.mult,
                            [b_cm, b_dt], [b_rs])
                    for q in range(2):
                        self.mm(pb[2 + q][:, :], U, rhsseg[:, 4 * q:4 * q + 4, :].rearrange("p h l -> p (h l)"), True, True,
                                [b_cm, b_rs], [bpb[2 + q]])
                        self.act(dec[:, 8 * half + 4 * q:8 * half + 4 * q + 4, :].rearrange("p h l -> p (h l)"), pb[2 + q][:, :],
                                 AF.Exp, [bpb[2 + q]], [b_dec])
                for g in range(4):
                    self.mm(pb[0][:, g * 128:(g + 1) * 128], BT[:, g, osl], CT[:, g, osl], True, True, [b_B, b_C], [bpb[0]])
                self.tt(DVE, CBm, pb[0][:, :].rearrange("p (g l) -> p g l", g=4), mask01.unsqueeze(1).to_broadcast([128, 4, 128]),
                        ALU.mult, [bpb[0], b_cm], [b_CBm])
                d4 = dec.rearrange("p (g j) l -> p g j l", g=4)
                self.tt(POOL, d4, d4, CBm.unsqueeze(2).to_broadcast([128, 4, 4, 128]), ALU.mult, [b_dec, b_CBm], [b_dec])
                pxb = pb[1][:].bitcast(BF16)
                for c in range(NCH):
                    self.tr(pxb[:, c * 128:(c + 1) * 128], xsT[:, c, osl], [b_xs], [bpb[1]])
                px3 = pxb.rearrange("p (h d) -> p h d", h=16)
                self.tt(DVE, xdt, px3, dt_all[:, kt * 16:(kt + 1) * 16].unsqueeze(2).to_broadcast([128, 16, 64]), ALU.mult,
                        [bpb[1], b_dt], [b_xdt])
                self.tt(DVE, ytok.rearrange("p (h d) -> p h d", h=16), px3, bc3[:, 32:48].unsqueeze(2).to_broadcast([128, 16, 64]),
                        ALU.mult, [bpb[1], b_bc3], [b_ytok])
                self.tt(POOL, xdd, xdt, E[:, 16:32].unsqueeze(2).to_broadcast([128, 16, 64]), ALU.mult, [b_xdt, b_E], [b_xdd])
                for h in range(HB):
                    self.mm(pb[4 + h // 8][:, (h % 8) * 64:(h % 8 + 1) * 64], dec[:, h, :], xdt[:, h, :], True, True,
                            [b_dec, b_xdt], [bpb[4 + h // 8]])
                if kt > 0:
                    for g in range(4):
                        self.mm(pb[6 + g // 2][:, (g % 2) * 256:(g % 2 + 1) * 256], CT[:, g, osl], state_bf[:, g * 256:(g + 1) * 256],
                                True, True, [b_C, b_sbf], [bpb[6 + g // 2]])
                for hb in range(2):
                    hs = slice(hb * 512, (hb + 1) * 512)
                    if kt > 0:
                        self.tt(DVE, y1[:, hs].rearrange("p (h d) -> p h d", h=8), pb[6 + hb][:, :].rearrange("p (h d) -> p h d", h=8),
                                E[:, 8 * hb:8 * hb + 8].unsqueeze(2).to_broadcast([128, 8, 64]), ALU.mult, [bpb[6 + hb], b_E], [b_y1])
                        self.tt(DVE, y1[:, hs], y1[:, hs], pb[4 + hb][:, :], ALU.add, [b_y1, bpb[4 + hb]], [b_y1])
                        self.tt(DVE, ytok[:, hs], y1[:, hs], ytok[:, hs], ALU.add, [b_y1, b_ytok], [b_ytok])
                    else:
                        self.tt(DVE, ytok[:, hs], pb[4 + hb][:, :], ytok[:, hs], ALU.add, [bpb[4 + hb], b_ytok], [b_ytok])
                if kt < NT - 1:
                    pbb = pb[0][:].bitcast(BF16)
                    for g in range(4):
                        self.tr(pbb[:, g * 128:(g + 1) * 128], BT[:, g, osl], [b_B], [bpb[0]])
                    self.act(Btok.rearrange("p g n -> p (g n)"), pbb[:, 0:512], AF.Copy, [bpb[0]], [b_Btok])
                    for g in range(4):
                        self.mm(pb[6 + g // 2][:, (g % 2) * 256:(g % 2 + 1) * 256], Btok[:, g, :],
                                xdd[:, 4 * g:4 * g + 4, :].rearrange("p h d -> p (h d)"), True, True, [b_Btok, b_xdd], [bpb[6 + g // 2]])
                    for hb in range(2):
                        hs = slice(hb * 512, (hb + 1) * 512)
                        if kt > 0:
                            s3 = state[:, hs].rearrange("p (h d) -> p h d", h=8)
                            self.tt(DVE, s3, s3, E[:, 32 + 8 * hb:32 + 8 * hb + 8].unsqueeze(2).to_broadcast([128, 8, 64]), ALU.mult,
                                    [b_state, b_E], [b_state])
                            self.tt(DVE, state[:, hs], state[:, hs], pb[6 + hb][:, :], ALU.add, [b_state, bpb[6 + hb]], [b_state])
                        else:
                            self.cp(DVE, state[:, hs], pb[6 + hb][:, :], [bpb[6 + hb]], [b_state])
                        self.cp(POOL, state_bf[:, hs], state[:, hs], [b_state], [b_sbf])
                pyb = pb[1][:].bitcast(BF16)
                for c in range(NCH):
                    self.tr(pyb[:, c * 128:(c + 1) * 128], ytok[:, c * 128:(c + 1) * 128], [b_ytok], [bpb[1]])
                self.tt(DVE, zsT[:, :, osl], pyb.rearrange("p (c t) -> p c t", c=8), zsT[:, :, osl], ALU.mult, [bpb[1], b_z], [b_z])
            for g in range(4):
                self.act(sq.rearrange("p c t -> p (c t)"), zsT[:, 2 * g:2 * g + 2, :].rearrange("p c t -> p (c t)"), AF.Square,
                         [b_z], [b_sq])
                self.mm(pb[2 + g % 2][:, 0:256], self.onesb[:], sq[:, 0, :], True, False, [self.b_const, b_sq], [bpb[2 + g % 2]])
                self.mm(pb[2 + g % 2][:, 0:256], self.onesb[:], sq[:, 1, :], False, True, [self.b_const, b_sq], [bpb[2 + g % 2]])
                self.act(rt, pb[2 + g % 2][:, 0:256], AF.Sqrt, [bpb[2 + g % 2]], [b_rt], scale=1.0 / 256, bias=1e-6)
                self.recip(rt, rt, [b_rt], [b_rt])
                for c2 in range(2):
                    c = 2 * g + c2
                    self.stt(self.ybr[:, c, tsl], zsT[:, c, :], snw[:, c:c + 1], rt, ALU.mult, ALU.mult, [b_z, b_snw, b_rt],
                             [self.b_ybr[c]])

    def diff(self, s, l):
        P = self.P
        lam_init = 0.8 - 0.6 * math.exp(-0.3 * l)
        cosE = self.carve(0, L)
        sinE = self.carve(2048, L)
        qT = self.carve(4096, L)
        kT = self.carve(6144, L)
        vaug = self.carve(8192, 2064).rearrange("p (t d) -> p t d", t=NT)
        sg = self.carve(10272, L)
        o = self.carve(12320, L, F32).rearrange("p (t d) -> p t d", t=NT)
        on = self.carve(16416, L).rearrange("p (t d) -> p t d", t=NT)
        pT = [self.carve(18464 + 512 * i, 512) for i in range(4)]
        xq = [self.carve(20512, 512), self.carve(21024, 512)]
        tf = [self.carve(21536, 512, F32), self.carve(22560, 512, F32)]
        pm = self.carve(23584, 128)
        b_tab = Buf("ropetab")
        b_q, b_k = Buf("dq"), Buf("dk")
        b_v = [Buf("dv%d" % i) for i in range(4)]
        b_sg = [Buf("dsg%d" % i) for i in range(4)]
        b_o = [Buf("o%d" % i) for i in range(NT)]
        b_on = [Buf("on%d" % i) for i in range(NT)]
        b_pT = [Buf("dpT%d" % i) for i in range(4)]
        b_xq = [Buf("xq0"), Buf("xq1")]
        b_tf = [Buf("tf0"), Buf("tf1")]
        b_pm = Buf("pm")
        b_rc = [Buf("drc%d" % i) for i in range(4)]
        b_lam = Buf("lam")
        b_ss2 = Buf("ss2")
        sm = self.small
        lamt = sm[:, 96:104]
        slw = sm[:, 104:106]
        ss2 = sm[:, 112:128]
        rt2 = sm[:, 128:144]
        rstd2 = sm[:, 144:160]
        P.dma(POOL, out=cosE, in_=self.rope_d[0], writes=[b_tab])
        P.dma(POOL, out=sinE, in_=self.rope_d[1], writes=[b_tab])
        self.memset(POOL, pm, 0.0, [b_pm])
        for a in range(2):
            P.op(POOL, lambda e, a=a: e.affine_select(out=pm[:, 64 * a:64 * a + 8], in_=self.onesb[:, 0:8], pattern=[[-1, 8]],
                                                      compare_op=ALU.is_equal, fill=0.0, base=-64 * a - 8, channel_multiplier=1),
                 [self.b_const], [b_pm])
            P.op(POOL, lambda e, a=a: e.affine_select(out=pm[:, 64 * a + 8:64 * a + 16], in_=self.onesb[:, 0:8], pattern=[[-1, 8]],
                                                      compare_op=ALU.is_equal, fill=0.0, base=-64 * a, channel_multiplier=1),
                 [self.b_const], [b_pm])
        dl = tf[0][:, 0:256]
        P.dma(SP, out=dl, in_=self.diff_lambda_d[l:l + 1].rearrange("o a d -> o (a d)").partition_broadcast(128), writes=[b_tf[0]])
        junk = tf[1][:, 0:64]
        self.stt(junk, dl[:, 0:64], 1.0, dl[:, 64:128], ALU.mult, ALU.mult, [b_tf[0]], [b_tf[1], b_lam], accum_out=lamt[:, 0:1])
        self.stt(junk, dl[:, 128:192], 1.0, dl[:, 192:256], ALU.mult, ALU.mult, [b_tf[0]], [b_tf[1], b_lam], accum_out=lamt[:, 1:2])
        self.act(lamt[:, 2:4], lamt[:, 0:2], AF.Exp, [b_lam], [b_lam])
        self.tt(DVE, lamt[:, 4:5], lamt[:, 2:3], lamt[:, 3:4], ALU.subtract, [b_lam], [b_lam])
        self.ts(DVE, lamt[:, 5:6], lamt[:, 4:5], -1.0, -lam_init, ALU.mult, ALU.add, [b_lam], [b_lam])
        neglam = lamt[:, 5:6]
        P.dma(SP, out=slw[:, 0:1], in_=self.subln_w_d[l:l + 1].rearrange("o p -> p o"), writes=[b_lam], allow_slow_non_contiguous=True)
        self.ts(DVE, slw[:, 0:1], slw[:, 0:1], (1.0 - lam_init) * 0.5, None, ALU.mult, None, [b_lam], [b_lam])
        for g4 in range(4):
            self.memset(DVE, vaug[:, 4 * g4:4 * g4 + 4, 128:129], 1.0, [b_v[g4]])
        pS = [self.pb[2], self.pb[3]]
        b_pS = [self.b_pb[2], self.b_pb[3]]
        pO = self.pb[4:8]
        b_pO = self.b_pb[4:8]
        sidx = 0
        k2 = 0
        for h in range(8):
            wq, bwq = self.wload(self.w_in_cols(l, C_DQ + 128 * h, 128), 128)
            wk, bwk = self.wload(self.w_in_cols(l, C_DK + 128 * h, 128), 128)
            wv, bwv = self.wload(self.w_in_cols(l, C_DV + 128 * h, 128), 128)
            wg, bwg = self.wload(self.w_in_cols(l, C_DG + 128 * h, 128), 128)
            for tb in range(4):
                sl = slice(tb * 512, (tb + 1) * 512)
                for (w_, bw_, dst, bdst, scale) in ((wq, bwq, qT, b_q, 0.125), (wk, bwk, kT, b_k, 1.0)):
                    ps, bps = self.pj()
                    self.proj_fm(w_, bw_, 128, tb, ps, bps)
                    x_, bx_ = xq[k2 % 2], b_xq[k2 % 2]
                    t1, bt1 = tf[k2 % 2], b_tf[k2 % 2]
                    k2 += 1
                    self.ts(DVE, x_, ps[:, :], scale, None, ALU.mult, None, [bps], [bx_])
                    S, bS = pS[sidx % 2], b_pS[sidx % 2]
                    sidx += 1
                    self.mm(S[:, :], pm, x_, True, True, [b_pm, bx_], [bS])
                    self.tt(DVE, t1, S[:, :], sinE[:, sl], ALU.mult, [bS, b_tab], [bt1])
                    self.tt(POOL, x_, x_, cosE[:, sl], ALU.mult, [bx_, b_tab], [bx_])
                    self.tt(DVE, dst[:, sl], t1, x_, ALU.add, [bt1, bx_], [bdst])
                ps, bps = self.pj()
                self.proj_fm(wg, bwg, 128, tb, ps, bps)
                t1, bt1 = tf[k2 % 2], b_tf[k2 % 2]
                k2 += 1
                self.act(t1, ps[:, :], AF.Tanh, [bps], [bt1], scale=0.5)
                self.stt(sg[:, sl], t1, 1.0, ps[:, :], ALU.add, ALU.mult, [bt1, bps], [b_sg[tb]])
            for g4 in range(4):
                ps, bps = self.pj()
                for u in range(4):
                    kt = 4 * g4 + u
                    for c in range(NCH):
                        self.mm(ps[:, u * 128:(u + 1) * 128], self.hT[:, c, kt * 128:(kt + 1) * 128], wv[:, c, :],
                                c == 0, c == NCH - 1, [bwv, self.b_hT[kt]], [bps])
                self.cp(DVE, vaug[:, 4 * g4:4 * g4 + 4, 0:128], ps[:, :].rearrange("p (t d) -> p t d", t=4), [bps], [b_v[g4]])
            tiles = [(comp, jq, i) for comp in range(2) for jq in range(4) for i in range(4 * jq + 4)]

            def emitS(tile, idx):
                comp, jq, i = tile
                pr = slice(64 * comp, 64 * comp + 64)
                r_ = i - 4 * jq
                qlo = max(512 * jq, 128 * i)
                w = 512 * (jq + 1) - qlo
                S, bS = pS[idx % 2], b_pS[idx % 2]
                pt, bpt = pT[idx % 4], b_pT[idx % 4]
                self.mm(S[:, 0:w], kT[pr, i * 128:(i + 1) * 128], qT[pr, qlo:qlo + w], True, r_ < 0, [b_k, b_q], [bS])
                if r_ >= 0:
                    self.mm(S[:, 0:128], self.ident[:], self.bmask[:], False, True, [self.b_const], [bS])
                self.act(pt[:, 0:w], S[:, 0:w], AF.Exp, [bS], [bpt])

            def emitPV(tile, idx):
                comp, jq, i = tile
                qlo = max(512 * jq, 128 * i)
                w = 512 * (jq + 1) - qlo
                pt, bpt = pT[idx % 4], b_pT[idx % 4]
                for cc in range(w // 128):
                    qc = qlo // 128 + cc
                    O, bO = pO[qc % 4], b_pO[qc % 4]
                    self.mm(O[:, 0:129], pt[:, cc * 128:(cc + 1) * 128], vaug[:, i, :], i == 0, i == qc,
                            [bpt, b_v[i // 4]], [bO])
                    if i == qc:
                        rcq = self.rc[:, qc:qc + 1]
                        self.recip(rcq, O[:, 128:129], [bO], [b_rc[qc % 4]])
                        if comp == 0:
                            self.ts(DVE, o[:, qc, :], O[:, 0:128], rcq, None, ALU.mult, None, [bO, b_rc[qc % 4]], [b_o[qc]])
                        else:
                            t1, bt1 = tf[qc % 2], b_tf[qc % 2]
                            self.ts(DVE, t1[:, 0:128], O[:, 0:128], rcq, neglam, ALU.mult, ALU.mult,
                                    [bO, b_rc[qc % 4], b_lam], [bt1])
                            self.tt(DVE, o[:, qc, :], t1[:, 0:128], o[:, qc, :], ALU.add, [bt1, b_o[qc]], [b_o[qc]])
                            self.stt(t1[:, 128:256], o[:, qc, :], 1.0, o[:, qc, :], ALU.mult, ALU.mult,
                                     [b_o[qc]], [bt1, b_ss2], accum_out=ss2[:, qc:qc + 1])

            for n_, tile in enumerate(tiles):
                emitS(tile, sidx + n_)
                if n_ > 0:
                    emitPV(tiles[n_ - 1], sidx + n_ - 1)
            emitPV(tiles[-1], sidx + len(tiles) - 1)
            sidx += len(tiles)
            self.act(rt2, ss2, AF.Sqrt, [b_ss2], [b_ss2], scale=1.0 / 128, bias=1e-6)
            self.recip(rstd2, rt2, [b_ss2], [b_ss2])
            for qc in range(NT):
                self.ts(POOL if qc % 2 else DVE, on[:, qc, :], o[:, qc, :], rstd2[:, qc:qc + 1], None, ALU.mult, None,
                        [b_o[qc], b_ss2], [b_on[qc]])
            for g4 in range(4):
                ps, bps = self.pj()
                psb = ps[:].bitcast(BF16)
                for u in range(4):
                    qc = 4 * g4 + u
                    self.tr(psb[:, u * 128:(u + 1) * 128], on[:, qc, :], [b_on[qc]], [bps])
                self.stt(self.ybr[:, h, g4 * 512:(g4 + 1) * 512], psb[:, 0:512], slw[:, 0:1], sg[:, g4 * 512:(g4 + 1) * 512],
                         ALU.mult, ALU.mult, [bps, b_sg[g4], b_lam], [self.b_ybr[h]])


def _rope_tables():
    pos = np.arange(L, dtype=np.float32)
    inv_freq = (np.float32(500000.0) ** (-np.arange(0, 16, 2, dtype=np.float32) / np.float32(16))).astype(np.float32)
    ang = (pos[:, None] * inv_freq[None, :]).astype(np.float32)
    cos = np.cos(ang).astype(np.float32)
    sin = np.sin(ang).astype(np.float32)
    cosE = np.ones((128, L), np.float32)
    sinE = np.zeros((128, L), np.float32)
    for m in range(128):
        i = m % 64
        if i < 8:
            cosE[m] = cos[:, i]
            sinE[m] = -sin[:, i]
        elif i < 16:
            cosE[m] = cos[:, i - 8]
            sinE[m] = sin[:, i - 8]
    return np.ascontiguousarray(np.stack([cosE, sinE]))


_NC_CACHE = {}


def kernel(x, norm_w, w_in, b_forget, conv_w, conv_b, dt_bias, a_log, d_skip, ssm_norm_w, diff_lambda, subln_w,
           w_branch, w_out, final_norm_w):
    ncores = 8
    nseq = x.shape[0] // ncores
    if 'nc' not in _NC_CACHE:
        _NC_CACHE['nc'] = Gen(nseq, 2, "abc", final=True).build()
    nc = _NC_CACHE['nc']
    f = lambda a: np.ascontiguousarray(np.asarray(a, dtype=np.float32))
    shared = dict(norm_w=f(norm_w), w_in=f(w_in), b_forget=f(b_forget), conv_w=f(conv_w), conv_b=f(conv_b),
                  dt_bias=f(dt_bias), a_log=f(a_log), d_skip=f(d_skip), ssm_norm_w=f(ssm_norm_w),
                  diff_lambda=f(diff_lambda), subln_w=f(subln_w), w_branch=f(w_branch), w_out=f(w_out),
                  final_norm_w=f(final_norm_w).reshape(1, -1), rope=_rope_tables())
    xs = f(x)
    in_maps = [dict(shared, x=np.ascontiguousarray(xs[i * nseq:(i + 1) * nseq])) for i in range(ncores)]
    res = run_bass_kernel_spmd(nc, in_maps, core_ids=list(range(ncores)))
    return np.concatenate([np.asarray(r["out"], dtype=np.float32) for r in res.results], axis=0)
```

```python
import numpy as np
import concourse.bass as bass
import concourse.mybir as mybir
from concourse.bass_utils import run_bass_kernel_spmd

F32 = mybir.dt.float32
BF16 = mybir.dt.bfloat16
AF = mybir.ActivationFunctionType
ALU = mybir.AluOpType
AX = mybir.AxisListType

PE, DVE, ACT, POOL, SP = 'tensor', 'vector', 'scalar', 'gpsimd', 'sync'
ENGS = [PE, DVE, ACT, POOL, SP]


class Buf:
    __slots__ = ('name', 'w', 'r', 'rd', 'dsem')

    def __init__(self, name):
        self.name = name
        self.w = None
        self.r = {}
        self.rd = []
        self.dsem = None


class Op:
    __slots__ = ('eng', 'fn', 'deps', 'dma_sem', 'tok', 'waited')

    def __init__(self, eng, fn):
        self.eng = eng
        self.fn = fn
        self.deps = []
        self.dma_sem = None
        self.tok = None
        self.waited = False


class DmaSem:
    def __init__(self, sem):
        self.sem = sem
        self.count = 0
        self.last = None
        self.q = None


class Prog:
    def __init__(self, nc, stack):
        self.nc = nc
        self.stack = stack
        self.ops = {e: [] for e in ENGS}
        self.esem = {}
        self.nops = 0
        self.dsems = []
        self.nsem = 0
        self.dpool = {}
        self.drr = {}

    def enter(self, cm):
        return self.stack.enter_context(cm)

    def sbuf(self, name, shape, dtype):
        return self.enter(self.nc.sbuf_tensor(name, list(shape), dtype))

    def psum(self, name, shape, dtype):
        return self.enter(self.nc.psum_tensor(name, list(shape), dtype))

    def dram(self, name, shape, dtype, kind):
        return self.nc.dram_tensor(name, list(shape), dtype, kind=kind).ap()

    def dsem(self, name=None):
        self.nsem += 1
        d = DmaSem(self.enter(self.nc.semaphore(name or ('ds%d' % self.nsem))))
        self.dsems.append(d)
        return d

    def op(self, eng, fn, reads=(), writes=(), dsem=None):
        o = Op(eng, fn)
        deps = []
        for b in reads:
            if b.w is not None:
                deps.append(b.w)
        for b in writes:
            if b.w is not None:
                deps.append(b.w)
            deps.extend(b.r.values())
            deps.extend(b.rd)
        seen = set()
        for d in deps:
            if id(d) in seen or d is o:
                continue
            seen.add(id(d))
            if d.dma_sem is None and d.eng == eng and eng == PE:
                continue
            o.deps.append(d)
        if dsem is not None:
            if dsem.last is not None and all(d is not dsem.last for d in o.deps):
                o.deps.append(dsem.last)
            o.dma_sem = dsem
            dsem.count += 16
            dsem.last = o
            o.tok = (dsem.sem, dsem.count)
        for b in reads:
            if dsem is not None:
                b.rd.append(o)
            else:
                b.r[eng] = o
        for b in writes:
            b.w = o
            b.r = {}
            b.rd = []
        self.ops[eng].append(o)
        self.nops += 1
        return o

    def dma(self, eng, out, in_, reads=(), writes=(), **kw):
        b = writes[0]
        if b.dsem is None or b.dsem.q != eng:
            pool = self.dpool.setdefault(eng, [])
            if len(pool) < 24:
                d = self.dsem()
                d.q = eng
                pool.append(d)
            else:
                d = pool[self.drr.get(eng, 0) % 24]
                self.drr[eng] = self.drr.get(eng, 0) + 1
            b.dsem = d
        return self.op(eng, lambda e: e.dma_start(out=out, in_=in_, **kw), reads=reads, writes=writes, dsem=b.dsem)

    def barrier(self):
        lasts = []
        for e in ENGS:
            if self.ops[e]:
                lasts.append(self.ops[e][-1])
        for d in self.dsems:
            if d.last is not None:
                lasts.append(d.last)
        new = []
        for e in ENGS:
            o = Op(e, lambda eng: eng.nop())
            for d in lasts:
                if d.dma_sem is None and d.eng == e:
                    continue
                o.deps.append(d)
            new.append(o)
        for o in new:
            self.ops[o.eng].append(o)

    def finish(self):
        o = Op(SP, lambda eng: eng.nop())
        for d in self.dsems:
            if d.last is not None:
                o.deps.append(d.last)
        self.ops[SP].append(o)
        self.emit()

    def emit(self):
        nc = self.nc
        for e in ENGS:
            self.esem[e] = self.enter(nc.semaphore('sem_' + e))
        for e in ENGS:
            for o in self.ops[e]:
                for d in o.deps:
                    d.waited = True
        self.nmile = {}
        for e in ENGS:
            c = 0
            for o in self.ops[e]:
                if o.dma_sem is None and o.waited:
                    c += 1
                    o.tok = (self.esem[e], c)
            self.nmile[e] = c
        prog = self

        def body(e):
            def run(eng):
                have = {}
                for o in prog.ops[e]:
                    need = {}
                    for d in o.deps:
                        s, v = d.tok
                        k = id(s)
                        if have.get(k, 0) >= v:
                            continue
                        if k not in need or need[k][1] < v:
                            need[k] = (s, v)
                    for k, (s, v) in need.items():
                        eng.wait_ge(s, v)
                        have[k] = v
                    ins = o.fn(eng)
                    if o.dma_sem is not None:
                        ins.then_inc(o.tok[0], 16)
                    elif o.waited:
                        ins.then_inc(o.tok[0], 1)
            return run

        with nc.Block() as block:
            block.tensor(body(PE))
            block.vector(body(DVE))
            block.scalar(body(ACT))
            block.gpsimd(body(POOL))
            block.sync(body(SP))


import math
from contextlib import ExitStack

L = 2048
D = 1024
NT = 16
NCH = 8
N_IN = 14368
C_FQ, C_FK, C_FV, C_FF, C_FG = 0, 1024, 2048, 3072, 3088
C_SZ, C_SX, C_SB, C_SC, C_SDT = 4112, 5136, 6160, 6672, 7184
C_DQ, C_DK, C_DV, C_DG, C_MG = 7200, 8224, 9248, 10272, 11296
NEG = -30000.0
ARENA = 24576
NSLOT = 6


class Gen:
    def __init__(self, nseq, nlayers, branches, final=True, dbg=None):
        self.nseq, self.nlayers, self.branches, self.final, self.dbg = nseq, nlayers, branches, final, dbg

    def mm(self, out, lhsT, rhs, start, stop, reads, writes):
        return self.P.op(PE, lambda e: e.matmul(out, lhsT=lhsT, rhs=rhs, start=start, stop=stop), reads, writes)

    def tr(self, out, in_, reads, writes):
        ident = self.ident
        return self.P.op(PE, lambda e: e.transpose(out=out, in_=in_, identity=ident[:]), list(reads) + [self.b_const], writes)

    def act(self, out, in_, func, reads, writes, **kw):
        return self.P.op(ACT, lambda e: e.activation(out=out, in_=in_, func=func, **kw), reads, writes)

    def tt(self, eng, out, in0, in1, op, reads, writes):
        return self.P.op(eng, lambda e: e.tensor_tensor(out=out, in0=in0, in1=in1, op=op), reads, writes)

    def ts(self, eng, out, in0, s1, s2, op0, op1, reads, writes, **kw):
        if op1 is None:
            return self.P.op(eng, lambda e: e.tensor_scalar(out=out, in0=in0, scalar1=s1, scalar2=None, op0=op0, **kw), reads, writes)
        return self.P.op(eng, lambda e: e.tensor_scalar(out=out, in0=in0, scalar1=s1, scalar2=s2, op0=op0, op1=op1, **kw), reads, writes)

    def stt(self, out, in0, scalar, in1, op0, op1, reads, writes, **kw):
        return self.P.op(DVE, lambda e: e.scalar_tensor_tensor(out=out, in0=in0, scalar=scalar, in1=in1, op0=op0, op1=op1, **kw), reads, writes)

    def cp(self, eng, out, in_, reads, writes):
        return self.P.op(eng, lambda e: e.tensor_copy(out=out, in_=in_), reads, writes)

    def memset(self, eng, ap, val, writes):
        return self.P.op(eng, lambda e: e.memset(ap, val), (), writes)

    def recip(self, out, in_, reads, writes):
        return self.P.op(DVE, lambda e: e.reciprocal(out=out, in_=in_), reads, writes)

    def carve(self, off, n, dtype=BF16):
        if dtype == F32:
            return self.arena[:, off:off + 2 * n].bitcast(F32)
        return self.arena[:, off:off + n]

    def wload(self, src, n):
        i = self.wrr % NSLOT
        self.wrr += 1
        ap = self.wslot[i][:, :, 0:n]
        buf = self.b_wslot[i]
        self.P.dma(POOL, out=ap, in_=src.rearrange("(c p) n -> p c n", p=128), writes=[buf])
        return ap, buf

    def w_in_cols(self, l, c0, n):
        return self.w_in_d[l, :, c0:c0 + n]

    def pj(self):
        i = self.pjrr % 2
        self.pjrr += 1
        return self.pb[i], self.b_pb[i]

    def hbufs(self, tb):
        return self.b_hT[4 * tb:4 * tb + 4]

    def proj_fm(self, w, bw, M, tb, ps, bps, rhs_src=None, rhs_bufs=None):
        src = self.hT if rhs_src is None else rhs_src
        rb = self.hbufs(tb) if rhs_bufs is None else rhs_bufs
        for c in range(NCH):
            self.mm(ps[0:M, :], w[:, c, 0:M], src[:, c, tb * 512:(tb + 1) * 512], c == 0, c == NCH - 1,
                    [bw] + list(rb), [bps])

    def build(self):
        nc = bass.Bass("TRN2", target_bir_lowering=False)
        self.nc = nc
        nseq = self.nseq
        di = lambda name, shape: nc.dram_tensor(name, list(shape), F32, kind="ExternalInput").ap()
        self.x_d = di("x", [nseq, L, D])
        self.norm_w_d = di("norm_w", [2, D])
        self.w_in_d = di("w_in", [2, D, N_IN])
        self.b_forget_d = di("b_forget", [2, 16])
        self.conv_w_d = di("conv_w", [2, 4, 2048])
        self.conv_b_d = di("conv_b", [2, 2048])
        self.dt_bias_d = di("dt_bias", [2, 16])
        self.a_log_d = di("a_log", [2, 16])
        self.d_skip_d = di("d_skip", [2, 16])
        self.ssm_norm_w_d = di("ssm_norm_w", [2, D])
        self.diff_lambda_d = di("diff_lambda", [2, 4, 64])
        self.subln_w_d = di("subln_w", [2, 128])
        self.w_branch_d = di("w_branch", [2, 3, D, D])
        self.w_out_d = di("w_out", [2, D, D])
        self.final_norm_w_d = di("final_norm_w", [1, D])
        self.rope_d = di("rope", [2, 128, L])
        self.out_d = nc.dram_tensor("out", [nseq, L, D], F32, kind="ExternalOutput").ap()
        self.gk_d = nc.dram_tensor("gk_scr", [16, 6, L], BF16, kind="Internal").ap()
        self.scr_d = nc.dram_tensor("ssd_scr", [3072, L], BF16, kind="ExternalOutput").ap()
        if self.dbg:
            self.dbg_d = nc.dram_tensor("dbg", [128, 8, L], BF16, kind="ExternalOutput").ap()
        with ExitStack() as st:
            P = Prog(nc, st)
            self.P = P
            self.xres = P.sbuf("xres", [128, NT, D], F32)
            self.hT = P.sbuf("hT", [128, NCH, L], BF16)
            self.ybr = P.sbuf("ybr", [128, NCH, L], BF16)
            self.wout = P.sbuf("wout", [128, NCH, D], BF16)
            self.arena = P.sbuf("arena", [128, ARENA], BF16)
            self.wslot = [P.sbuf("wslot%d" % i, [128, NCH, 128], BF16) for i in range(NSLOT)]
            self.ident = P.sbuf("ident", [128, 128], BF16)
            self.cmask = P.sbuf("cmask", [128, 128], BF16)
            self.bmask = P.sbuf("bmask", [128, 128], BF16)
            self.onesb = P.sbuf("onesb", [128, 128], BF16)
            self.zerob = P.sbuf("zerob", [128, 128], BF16)
            self.small = P.sbuf("small", [128, 256], F32)
            self.pb = [P.psum("pb%d" % i, [128, 512], F32) for i in range(8)]
            self.b_pb = [Buf("pb%d" % i) for i in range(8)]
            self.b_x = [Buf("x%d" % t) for t in range(NT)]
            self.b_hT = [Buf("hT%d" % t) for t in range(NT)]
            self.b_ybr = [Buf("ybr%d" % c) for c in range(NCH)]
            self.b_wout = Buf("wout")
            self.b_wslot = [Buf("ws%d" % i) for i in range(NSLOT)]
            self.b_const = Buf("const")
            self.b_small = Buf("small")
            self.b_gk = Buf("gk")
            self.b_out = [Buf("out%d" % t) for t in range(NT)]
            self.wrr = 0
            self.pjrr = 0
            sm = self.small
            self.nwT = sm[:, 0:16]
            self.negbf = sm[0:16, 16:18]
            self.ss = sm[:, 32:48]
            self.rt = sm[:, 48:64]
            self.rstd = sm[:, 64:80]
            self.rc = sm[:, 80:96]
            self.consts()
            for s in range(nseq):
                for t in range(NT):
                    P.dma(SP, out=self.xres[:, t, :], in_=self.x_d[s, t * 128:(t + 1) * 128, :], writes=[self.b_x[t]])
                for l in range(self.nlayers):
                    self.layer(s, l)
                self.final_out(s)
            P.finish()
        return nc

    def consts(self):
        P = self.P
        bc = self.b_const
        self.memset(POOL, self.onesb[:], 1.0, [bc])
        self.memset(POOL, self.zerob[:], 0.0, [bc])
        P.op(POOL, lambda e: e.affine_select(out=self.ident[:], in_=self.onesb[:], pattern=[[-1, 128]], compare_op=ALU.is_equal,
                                              fill=0.0, base=0, channel_multiplier=1), [bc], [bc])
        P.op(POOL, lambda e: e.affine_select(out=self.cmask[:], in_=self.zerob[:], pattern=[[1, 128]], compare_op=ALU.is_ge,
                                              fill=NEG, base=0, channel_multiplier=-1), [bc], [bc])
        self.memset(POOL, self.bmask[:], 0.0, [bc])
        self.memset(POOL, self.bmask[64:128, 0:64], NEG, [bc])
        P.dma(SP, out=self.nwT.rearrange("p (l c) -> p l c", l=2), in_=self.norm_w_d.rearrange("l (c p) -> p l c", p=128),
              writes=[self.b_small], allow_slow_non_contiguous=True)
        P.dma(SP, out=self.negbf, in_=self.b_forget_d.rearrange("l h -> h l"), writes=[self.b_small], allow_slow_non_contiguous=True)
        self.ts(DVE, self.negbf, self.negbf, -1.0, None, ALU.mult, None, [self.b_small], [self.b_small])

    def layer(self, s, l):
        P = self.P
        for eb in range(2):
            P.dma(POOL, out=self.wout[:, :, eb * 512:(eb + 1) * 512],
                  in_=self.w_out_d[l, :, eb * 512:(eb + 1) * 512].rearrange("(c p) n -> p c n", p=128), writes=[self.b_wout])
        P.barrier()
        self.rmsnorm_T(l)
        for n, br in enumerate("abc"):
            if br not in self.branches:
                continue
            P.barrier()
            if br == 'a':
                self.fox(s, l)
            elif br == 'b':
                self.ssd(s, l)
            else:
                self.diff(s, l)
            if self.dbg == br and l == self.nlayers - 1:
                for c in range(NCH):
                    P.dma(SP, out=self.dbg_d[:, c, :], in_=self.ybr[:, c, :], reads=[self.b_ybr[c]], writes=[Buf("dbg%d" % c)])
            P.barrier()
            self.merge(l, n)

    def rmsnorm_T(self, l):
        junk = self.carve(0, 1024)
        hn = [self.carve(1024, 1024), self.carve(2048, 1024)]
        b_junk = Buf("junk")
        b_hn = [Buf("hn0"), Buf("hn1")]
        b_ss = Buf("ss")
        for t in range(NT):
            self.act(junk, self.xres[:, t, :], AF.Square, [self.b_x[t]], [b_junk, b_ss], accum_out=self.ss[:, t:t + 1])
        self.act(self.rt, self.ss, AF.Sqrt, [b_ss], [b_ss], scale=1.0 / D, bias=1e-6)
        self.recip(self.rstd, self.rt, [b_ss], [b_ss])
        for t in range(NT):
            k = t % 2
            self.act(hn[k], self.xres[:, t, :], AF.Copy, [self.b_x[t], b_ss], [b_hn[k]], scale=self.rstd[:, t:t + 1])
            ps, bps = self.pj()
            psb = ps[:].bitcast(BF16)
            for c in range(NCH):
                self.tr(psb[:, c * 128:(c + 1) * 128], hn[k][:, c * 128:(c + 1) * 128], [b_hn[k]], [bps])
            self.tt(DVE, self.hT[:, :, t * 128:(t + 1) * 128], psb.rearrange("p (c n) -> p c n", c=NCH),
                    self.nwT[:, l * 8:(l + 1) * 8].unsqueeze(2).to_broadcast([128, NCH, 128]), ALU.mult,
                    [bps, self.b_small], [self.b_hT[t]])

    def merge(self, l, n):
        gp = self.carve(0, NCH * L).rearrange("p (c t) -> p c t", c=NCH)
        th = [self.carve(NCH * L, 512, F32), self.carve(NCH * L + 1024, 512, F32)]
        b_gp = [Buf("gp%d" % t) for t in range(4)]
        b_th = [Buf("th0"), Buf("th1")]
        k = 0
        for co in range(NCH):
            wb, bwb = self.wload(self.w_branch_d[l, n, :, co * 128:(co + 1) * 128], 128)
            wg, bwg = self.wload(self.w_in_cols(l, C_MG + n * 1024 + co * 128, 128), 128)
            for tb in range(4):
                ps1, bps1 = self.pj()
                self.proj_fm(wb, bwb, 128, tb, ps1, bps1, rhs_src=self.ybr, rhs_bufs=self.b_ybr)
                ps2, bps2 = self.pb[2 + k % 2], self.b_pb[2 + k % 2]
                self.proj_fm(wg, bwg, 128, tb, ps2, bps2)
                t_ = th[k % 2]
                self.act(t_, ps2[:, :], AF.Tanh, [bps2], [b_th[k % 2]], scale=0.5)
                self.stt(gp[:, co, tb * 512:(tb + 1) * 512], t_, 1.0, ps1[:, :], ALU.add, ALU.mult,
                         [b_th[k % 2], bps1], [b_gp[tb]])
                k += 1
        for t in range(NT):
            for eb in range(2):
                ps, bps = self.pb[4 + (2 * t + eb) % 4], self.b_pb[4 + (2 * t + eb) % 4]
                for co in range(NCH):
                    self.mm(ps[:, :], gp[:, co, t * 128:(t + 1) * 128], self.wout[:, co, eb * 512:(eb + 1) * 512],
                            co == 0, co == NCH - 1, [b_gp[t // 4], self.b_wout], [bps])
                xs = self.xres[:, t, eb * 512:(eb + 1) * 512]
                self.stt(xs, ps[:, :], 0.5, xs, ALU.mult, ALU.add, [bps, self.b_x[t]], [self.b_x[t]])

    def final_out(self, s):
        P = self.P
        P.barrier()
        b_ss = Buf("fss")
        if self.final:
            junk = self.carve(0, 1024)
            fnw = self.carve(1024, 1024, F32)
            b_junk, b_fnw = Buf("fjunk"), Buf("fnw")
            P.dma(SP, out=fnw, in_=self.final_norm_w_d.partition_broadcast(128), writes=[b_fnw])
            for t in range(NT):
                self.act(junk, self.xres[:, t, :], AF.Square, [self.b_x[t]], [b_junk, b_ss], accum_out=self.ss[:, t:t + 1])
            self.act(self.rt, self.ss, AF.Sqrt, [b_ss], [b_ss], scale=1.0 / D, bias=1e-6)
            self.recip(self.rstd, self.rt, [b_ss], [b_ss])
            for t in range(NT):
                xs = self.xres[:, t, :]
                self.stt(xs, xs, self.rstd[:, t:t + 1], fnw, ALU.mult, ALU.mult, [self.b_x[t], b_ss, b_fnw], [self.b_x[t]])
        for t in range(NT):
            P.dma(SP, out=self.out_d[s, t * 128:(t + 1) * 128, :], in_=self.xres[:, t, :], reads=[self.b_x[t]], writes=[self.b_out[t]])

    def fox(self, s, l):
        P = self.P
        lf = self.carve(0, L, F32)[0:16, :]
        G = self.carve(4096, L, F32)[0:16, :]
        r = self.carve(8192, L, F32)[0:16, :]
        parts = [self.carve(12288 + 2048 * j, L)[0:16, :] for j in range(6)]
        b_lf, b_G, b_r = Buf("lf"), Buf("G"), Buf("r")
        b_parts = [Buf("part%d" % j) for j in range(6)]
        wff, bwff = self.wload(self.w_in_cols(l, C_FF, 16), 16)
        for tb in range(4):
            ps, bps = self.pj()
            self.proj_fm(wff, bwff, 16, tb, ps, bps)
            self.act(lf[:, tb * 512:(tb + 1) * 512], ps[0:16, :], AF.Exp, [bps, self.b_small], [b_lf],
                     scale=-1.0, bias=self.negbf[:, l:l + 1])
        self.act(lf, lf, AF.Ln, [b_lf], [b_lf], bias=1.0)
        P.op(DVE, lambda e: e.tensor_tensor_scan(out=G, data0=lf, data1=lf, initial=0.0, op0=ALU.add, op1=ALU.max), [b_lf], [b_G])
        self.cp(DVE, parts[0], G, [b_G], [b_parts[0]])
        self.tt(DVE, r, G, parts[0], ALU.subtract, [b_G, b_parts[0]], [b_r])
        self.cp(DVE, parts[1], r, [b_r], [b_parts[1]])
        self.tt(DVE, r, r, parts[1], ALU.subtract, [b_r, b_parts[1]], [b_r])
        self.cp(DVE, parts[2], r, [b_r], [b_parts[2]])
        for j in range(3):
            self.ts(DVE, parts[3 + j], parts[j], -1.0, None, ALU.mult, None, [b_parts[j]], [b_parts[3 + j]])
        for j in range(6):
            P.dma(SP, out=self.gk_d[:, j, :], in_=parts[j], reads=[b_parts[j]], writes=[self.b_gk])
        P.barrier()
        qaug = [self.carve(0, L), self.carve(2048, L)]
        kaug = [self.carve(4096, L), self.carve(6144, L)]
        vaug = self.carve(8192, 2080).rearrange("p (t h d) -> p t h d", t=NT, h=2)
        sg = self.carve(10368, L)
        opair = self.carve(12416, L).rearrange("p (t d) -> p t d", t=NT)
        pT = [self.carve(14464 + 512 * i, 512) for i in range(4)]
        th = [self.carve(16512, 512, F32), self.carve(17536, 512, F32)]
        b_q = [Buf("qaug0"), Buf("qaug1")]
        b_k = [Buf("kaug0"), Buf("kaug1")]
        b_v = [Buf("vaug%d" % i) for i in range(4)]
        b_sg = [Buf("sg%d" % i) for i in range(4)]
        b_op = [Buf("op%d" % i) for i in range(NT)]
        b_pT = [Buf("pT%d" % i) for i in range(4)]
        b_th = [Buf("fth0"), Buf("fth1")]
        b_rc = [Buf("rc%d" % i) for i in range(4)]
        for hh in range(2):
            self.memset(DVE, kaug[hh][64:67, :], 1.0, [b_k[hh]])
            self.memset(DVE, qaug[hh][64:67, :], 1.0, [b_q[hh]])
            P.dma(SP, out=qaug[hh][67:70, :], in_=qaug[hh][64:67, :], reads=[b_q[hh]], writes=[b_q[hh]])
        for g4 in range(4):
            self.memset(DVE, vaug[:, 4 * g4:4 * g4 + 4, :, 64:65], 1.0, [b_v[g4]])
        pS = [self.pb[2], self.pb[3]]
        b_pS = [self.b_pb[2], self.b_pb[3]]
        pO = self.pb[4:8]
        b_pO = self.b_pb[4:8]
        sidx = 0
        thk = 0
        for j in range(8):
            wq, bwq = self.wload(self.w_in_cols(l, C_FQ + 128 * j, 128), 128)
            wk, bwk = self.wload(self.w_in_cols(l, C_FK + 128 * j, 128), 128)
            wv, bwv = self.wload(self.w_in_cols(l, C_FV + 128 * j, 128), 128)
            wg, bwg = self.wload(self.w_in_cols(l, C_FG + 128 * j, 128), 128)
            for hh in range(2):
                h = 2 * j + hh
                P.dma(SP, out=kaug[hh][67:70, :], in_=self.gk_d[h, 0:3, :], reads=[self.b_gk], writes=[b_k[hh]])
                P.dma(SP, out=qaug[hh][64:67, :], in_=self.gk_d[h, 3:6, :], reads=[self.b_gk], writes=[b_q[hh]])
            for tb in range(4):
                sl = slice(tb * 512, (tb + 1) * 512)
                ps, bps = self.pj()
                self.proj_fm(wq, bwq, 128, tb, ps, bps)
                for hh in range(2):
                    self.ts(DVE, qaug[hh][0:64, sl], ps[64 * hh:64 * hh + 64, :], 0.125, None, ALU.mult, None, [bps], [b_q[hh]])
                ps, bps = self.pj()
                self.proj_fm(wk, bwk, 128, tb, ps, bps)
                for hh in range(2):
                    self.cp(DVE, kaug[hh][0:64, sl], ps[64 * hh:64 * hh + 64, :], [bps], [b_k[hh]])
                ps, bps = self.pj()
                self.proj_fm(wg, bwg, 128, tb, ps, bps)
                t_ = th[thk % 2]
                self.act(t_, ps[:, :], AF.Tanh, [bps], [b_th[thk % 2]], scale=0.5)
                self.stt(sg[:, sl], t_, 1.0, ps[:, :], ALU.add, ALU.mult, [b_th[thk % 2], bps], [b_sg[tb]])
                thk += 1
            for g4 in range(4):
                ps, bps = self.pj()
                for u in range(4):
                    kt = 4 * g4 + u
                    for c in range(NCH):
                        self.mm(ps[:, u * 128:(u + 1) * 128], self.hT[:, c, kt * 128:(kt + 1) * 128], wv[:, c, :],
                                c == 0, c == NCH - 1, [bwv, self.b_hT[kt]], [bps])
                self.cp(DVE, vaug[:, 4 * g4:4 * g4 + 4, :, 0:64], ps[:, :].rearrange("p (t h d) -> p t h d", t=4, h=2),
                        [bps], [b_v[g4]])
            tiles = [(hh, jq, i) for hh in range(2) for jq in range(4) for i in range(4 * jq + 4)]

            def emitS(tile, idx):
                hh, jq, i = tile
                r_ = i - 4 * jq
                qlo = max(512 * jq, 128 * i)
                w = 512 * (jq + 1) - qlo
                S, bS = pS[idx % 2], b_pS[idx % 2]
                pt, bpt = pT[idx % 4], b_pT[idx % 4]
                self.mm(S[:, 0:w], kaug[hh][0:70, i * 128:(i + 1) * 128], qaug[hh][0:70, qlo:qlo + w],
                        True, r_ < 0, [b_k[hh], b_q[hh]], [bS])
                if r_ >= 0:
                    self.mm(S[:, 0:128], self.ident[:], self.cmask[:], False, True, [self.b_const], [bS])
                self.act(pt[:, 0:w], S[:, 0:w], AF.Exp, [bS], [bpt])

            def emitPV(tile, idx):
                hh, jq, i = tile
                qlo = max(512 * jq, 128 * i)
                w = 512 * (jq + 1) - qlo
                pt, bpt = pT[idx % 4], b_pT[idx % 4]
                for cc in range(w // 128):
                    qc = qlo // 128 + cc
                    O, bO = pO[qc % 4], b_pO[qc % 4]
                    self.mm(O[:, 0:65], pt[:, cc * 128:(cc + 1) * 128], vaug[:, i, hh, :], i == 0, i == qc,
                            [bpt, b_v[i // 4]], [bO])
                    if i == qc:
                        self.recip(self.rc[:, qc:qc + 1], O[:, 64:65], [bO], [b_rc[qc % 4]])
                        self.ts(DVE, opair[:, qc, 64 * hh:64 * hh + 64], O[:, 0:64], self.rc[:, qc:qc + 1], 0.5,
                                ALU.mult, ALU.mult, [bO, b_rc[qc % 4]], [b_op[qc]])

            for n_, tile in enumerate(tiles):
                emitS(tile, sidx + n_)
                if n_ > 0:
                    emitPV(tiles[n_ - 1], sidx + n_ - 1)
            emitPV(tiles[-1], sidx + len(tiles) - 1)
            sidx += len(tiles)
            for g4 in range(4):
                ps, bps = self.pj()
                psb = ps[:].bitcast(BF16)
                for u in range(4):
                    qc = 4 * g4 + u
                    self.tr(psb[:, u * 128:(u + 1) * 128], opair[:, qc, :], [b_op[qc]], [bps])
                self.tt(DVE, self.ybr[:, j, g4 * 512:(g4 + 1) * 512], psb[:, 0:512], sg[:, g4 * 512:(g4 + 1) * 512], ALU.mult,
                        [bps, b_sg[g4]], [self.b_ybr[j]])

    def ssd(self, s, l):
        P = self.P
        scr = self.scr_d
        HB = 16
        dt_all = self.carve(0, 256, F32)
        dA_all = self.carve(512, 256, F32)
        bc3 = self.carve(1024, 48, F32)
        cwT = self.carve(1120, 80, F32).rearrange("p (c j) -> p c j", c=16)
        snw = self.carve(1280, 8, F32)
        Lm = self.carve(1296, 128, F32)
        U = self.carve(1552, 128, F32)
        onesf = self.carve(1808, 128, F32)
        mask01 = self.carve(2064, 128)
        E = self.carve(2192, 48, F32)
        b_dt, b_bc3, b_cw, b_snw, b_cm, b_E = Buf("dt_all"), Buf("bc3"), Buf("cwT"), Buf("snw"), Buf("ssdconst"), Buf("E")
        self.memset(POOL, onesf, 1.0, [b_cm])
        P.op(POOL, lambda e: e.affine_select(out=Lm, in_=self.onesb[:], pattern=[[1, 128]], compare_op=ALU.is_ge, fill=0.0,
                                              base=0, channel_multiplier=-1), [self.b_const], [b_cm])
        P.op(POOL, lambda e: e.affine_select(out=mask01, in_=self.onesb[:], pattern=[[1, 128]], compare_op=ALU.is_ge, fill=0.0,
                                              base=0, channel_multiplier=-1), [self.b_const], [b_cm])
        P.op(POOL, lambda e: e.affine_select(out=U, in_=self.onesb[:], pattern=[[-1, 128]], compare_op=ALU.is_gt, fill=0.0,
                                              base=0, channel_multiplier=1), [self.b_const], [b_cm])
        P.dma(SP, out=bc3[:, 0:16], in_=self.dt_bias_d[l:l + 1].partition_broadcast(128), writes=[b_bc3])
        P.dma(SP, out=bc3[:, 16:32], in_=self.a_log_d[l:l + 1].partition_broadcast(128), writes=[b_bc3])
        P.dma(SP, out=bc3[:, 32:48], in_=self.d_skip_d[l:l + 1].partition_broadcast(128), writes=[b_bc3])
        self.act(bc3[:, 16:32], bc3[:, 16:32], AF.Exp, [b_bc3], [b_bc3])
        self.ts(DVE, bc3[:, 16:32], bc3[:, 16:32], -1.0, None, ALU.mult, None, [b_bc3], [b_bc3])
        for j in range(4):
            P.dma(SP, out=cwT[:, :, j:j + 1], in_=self.conv_w_d[l, j:j + 1, :].rearrange("o (c p) -> p c o", p=128), writes=[b_cw],
                  allow_slow_non_contiguous=True)
        P.dma(SP, out=cwT[:, :, 4:5], in_=self.conv_b_d[l:l + 1, :].rearrange("o (c p) -> p c o", p=128), writes=[b_cw],
              allow_slow_non_contiguous=True)
        P.dma(SP, out=snw, in_=self.ssm_norm_w_d[l:l + 1, :].rearrange("o (c p) -> p (o c)", p=128), writes=[b_snw],
              allow_slow_non_contiguous=True)
        wdt, bwdt = self.wload(self.w_in_cols(l, C_SDT, 16), 16)
        for g4 in range(4):
            ps, bps = self.pj()
            for u_ in range(4):
                kt = 4 * g4 + u_
                for c in range(NCH):
                    self.mm(ps[:, u_ * 16:(u_ + 1) * 16], self.hT[:, c, kt * 128:(kt + 1) * 128], wdt[:, c, 0:16],
                            c == 0, c == NCH - 1, [bwdt, self.b_hT[kt]], [bps])
            self.tt(DVE, dt_all[:, g4 * 64:(g4 + 1) * 64].rearrange("p (t h) -> p t h", t=4),
                    ps[:, 0:64].rearrange("p (t h) -> p t h", t=4), bc3[:, 0:16].unsqueeze(1).to_broadcast([128, 4, 16]), ALU.add,
                    [bps, b_bc3], [b_dt])
        self.act(dt_all, dt_all, AF.Exp, [b_dt], [b_dt])
        self.act(dt_all, dt_all, AF.Ln, [b_dt], [b_dt], bias=1.0)
        self.tt(DVE, dA_all.rearrange("p (t h) -> p t h", t=NT), dt_all.rearrange("p (t h) -> p t h", t=NT),
                bc3[:, 16:32].unsqueeze(1).to_broadcast([128, NT, 16]), ALU.mult, [b_dt, b_bc3], [b_dt])
        if getattr(self, 'ssd_level', 9) < 1:
            return
        cins = [self.carve(2304, 2080, F32), self.carve(6464, 2080, F32)]
        cout = [self.carve(10624, L), self.carve(12672, L)]
        pa = [self.carve(14720 + 1024 * i_, 512, F32) for i_ in range(4)]
        b_cin, b_cout = [Buf("cin0"), Buf("cin1")], [Buf("cout0"), Buf("cout1")]
        b_pa = [Buf("pa%d" % i_) for i_ in range(4)]
        b_scr = [Buf("scr%d" % i) for i in range(24)]
        PAD = 4
        for i_ in range(2):
            self.memset(DVE, cins[i_][:, 0:PAD], 0.0, [b_cin[i_]])
        kk = [0]
        wl = {}

        def emit_proj(cc, tb):
            if tb == 0:
                wl[cc] = self.wload(self.w_in_cols(l, C_SX + 128 * cc, 128), 128)
            w_, bw_ = wl[cc]
            cin, b_u = cins[cc % 2], b_cin[cc % 2]
            ps, bps = self.pj()
            self.proj_fm(w_, bw_, 128, tb, ps, bps)
            self.act(cin[:, PAD + tb * 512:PAD + (tb + 1) * 512], ps[:, :], AF.Copy, [bps], [b_u])

        def emit_conv(cc, tb):
            k = kk[0]
            kk[0] += 1
            cin, b_u = cins[cc % 2], b_cin[cc % 2]
            co, bco = cout[cc % 2], b_cout[cc % 2]
            a_, ba_ = self.pb[2 + k % 4], self.b_pb[2 + k % 4]
            p_, bp_ = pa[k % 4], b_pa[k % 4]
            o0 = tb * 512 + PAD - 3
            self.act(a_[:, :], cin[:, o0 + 3:o0 + 3 + 512], AF.Identity, [b_u, b_cw], [ba_], scale=cwT[:, cc, 3:4], bias=cwT[:, cc, 4:5])
            self.ts(POOL, p_, cin[:, o0 + 1:o0 + 1 + 512], cwT[:, cc, 1:2], 0.0, ALU.mult, ALU.add, [b_u, b_cw], [bp_])
            self.stt(a_[:, :], cin[:, o0 + 2:o0 + 2 + 512], cwT[:, cc, 2:3], a_[:, :], ALU.mult, ALU.add, [b_u, b_cw, ba_], [ba_])
            self.stt(a_[:, :], cin[:, o0:o0 + 512], cwT[:, cc, 0:1], a_[:, :], ALU.mult, ALU.add, [b_u, b_cw, ba_], [ba_])
            self.tt(DVE, a_[:, :], a_[:, :], p_, ALU.add, [ba_, bp_], [ba_])
            pend.append((cc, tb, a_, ba_))

        def emit_back():
            cc, tb, a_, ba_ = pend.pop(0)
            co, bco = cout[cc % 2], b_cout[cc % 2]
            self.act(co[:, tb * 512:(tb + 1) * 512], a_[:, :], AF.Silu, [ba_], [bco])
            if tb == 3:
                P.dma(SP, out=scr[cc * 128:(cc + 1) * 128, :], in_=co, reads=[bco], writes=[b_scr[cc]])

        pend = []
        for tb in range(4):
            emit_proj(0, tb)
        for cc in range(16):
            for tb in range(4):
                if cc + 1 < 16:
                    emit_proj(cc + 1, tb)
                emit_conv(cc, tb)
                if len(pend) > 2:
                    emit_back()
        while pend:
            emit_back()
        for c in range(NCH):
            w_, bw_ = self.wload(self.w_in_cols(l, C_SZ + 128 * c, 128), 128)
            co, bco = cout[c % 2], b_cout[c % 2]
            for tb in range(4):
                ps, bps = self.pj()
                self.proj_fm(w_, bw_, 128, tb, ps, bps)
                self.act(co[:, tb * 512:(tb + 1) * 512], ps[:, :], AF.Silu, [bps], [bco])
            P.dma(SP, out=scr[2048 + c * 128:2048 + (c + 1) * 128, :], in_=co, reads=[bco], writes=[b_scr[16 + c]])
        if getattr(self, 'ssd_level', 9) < 2:
            return
        P.barrier()
        xsT = self.carve(2304, 1024).rearrange("p (c t) -> p c t", c=8)
        BT = self.carve(3328, 512).rearrange("p (c t) -> p c t", c=4)
        CTs = [self.carve(3840 + 512 * i_, 512).rearrange("p (c t) -> p c t", c=4) for i_ in range(2)]
        zsT = self.carve(4864, 1024).rearrange("p (c t) -> p c t", c=8)
        rhsseg = self.carve(5888, 1024, F32).rearrange("p (h l) -> p h l", h=8)
        decs = [self.carve(7936 + 2048 * i_, 2048).rearrange("p (h l) -> p h l", h=16) for i_ in range(2)]
        CBm = self.carve(12032, 512).rearrange("p (g l) -> p g l", g=4)
        xdts = [self.carve(12544 + 1024 * i_, 1024).rearrange("p (h d) -> p h d", h=16) for i_ in range(2)]
        xdd = self.carve(14592, 1024).rearrange("p (h d) -> p h d", h=16)
        ytoks = [self.carve(15616 + 1024 * i_, 1024) for i_ in range(2)]
        y1 = self.carve(17664, 1024, F32)
        Btoks = [self.carve(19712 + 512 * i_, 512).rearrange("p (g n) -> p g n", g=4) for i_ in range(2)]
        state = self.carve(20736, 1024, F32)
        state_bf = self.carve(22784, 1024)
        sq = self.carve(23808, 256).rearrange("p (c t) -> p c t", c=2)
        rt = self.carve(24064, 128, F32)
        Es = [self.carve(24320 + 96 * i_, 48, F32) for i_ in range(2)]
        b_xs, b_B, b_z = Buf("xsT"), Buf("BT"), Buf("zsT")
        b_Cs = [Buf("CT0"), Buf("CT1")]
        b_rs, b_CBm, b_xdd, b_y1 = [Buf(n) for n in "rhsseg CBm xdd y1".split()]
        b_decs, b_xdts, b_ytoks, b_Btoks, b_Es = [[Buf("%s%d" % (n, i_)) for i_ in range(2)] for n in "dec xdt ytok Btok E".split()]
        b_state, b_sbf, b_sq, b_rt = [Buf(n) for n in "state state_bf sq rt".split()]
        pb, bpb = self.pb, self.b_pb

        def load_front_tiles(kt):
            tsl = slice(kt * 128, (kt + 1) * 128)
            P.dma(SP, out=xsT, in_=scr[0:1024, tsl].rearrange("(c p) t -> p c t", p=128), reads=b_scr[0:8], writes=[b_xs])
            P.dma(SP, out=BT, in_=scr[1024:1536, tsl].rearrange("(c p) t -> p c t", p=128), reads=b_scr[8:12], writes=[b_B])
            P.dma(SP, out=CTs[kt % 2], in_=scr[1536:2048, tsl].rearrange("(c p) t -> p c t", p=128), reads=b_scr[12:16],
                  writes=[b_Cs[kt % 2]])

        def front(kt):
            E, b_E = Es[kt % 2], b_Es[kt % 2]
            dec, b_dec = decs[kt % 2], b_decs[kt % 2]
            xdt, b_xdt = xdts[kt % 2], b_xdts[kt % 2]
            ytok, b_ytok = ytoks[kt % 2], b_ytoks[kt % 2]
            Btok, b_Btok = Btoks[kt % 2], b_Btoks[kt % 2]
            CT, b_C = CTs[kt % 2], b_Cs[kt % 2]
            dAk = dA_all[:, kt * 16:(kt + 1) * 16]
            self.mm(pb[0][:, 0:16], Lm, dAk, True, True, [b_cm, b_dt], [bpb[0]])
            self.mm(pb[0][:, 16:32], U, dAk, True, True, [b_cm, b_dt], [bpb[0]])
            self.mm(pb[0][:, 32:48], onesf, dAk, True, True, [b_cm, b_dt], [bpb[0]])
            self.act(E, pb[0][:, 0:48], AF.Exp, [bpb[0]], [b_E])
            yield
            for half in range(2):
                self.tt(POOL, rhsseg, Lm.unsqueeze(1).to_broadcast([128, 8, 128]),
                        dAk[:, 8 * half:8 * half + 8].unsqueeze(2).to_broadcast([128, 8, 128]), ALU.mult, [b_cm, b_dt], [b_rs])
                yield
                for q in range(2):
                    self.mm(pb[2 + q][:, :], U, rhsseg[:, 4 * q:4 * q + 4, :].rearrange("p h l -> p (h l)"), True, True,
                            [b_cm, b_rs], [bpb[2 + q]])
                    self.act(dec[:, 8 * half + 4 * q:8 * half + 4 * q + 4, :].rearrange("p h l -> p (h l)"), pb[2 + q][:, :],
                             AF.Exp, [bpb[2 + q]], [b_dec])
                yield
            for g in range(4):
                self.mm(pb[0][:, g * 128:(g + 1) * 128], BT[:, g, :], CT[:, g, :], True, True, [b_B, b_C], [bpb[0]])
            self.tt(DVE, CBm, pb[0][:, :].rearrange("p (g l) -> p g l", g=4), mask01.unsqueeze(1).to_broadcast([128, 4, 128]),
                    ALU.mult, [bpb[0], b_cm], [b_CBm])
            yield
            d4 = dec.rearrange("p (g j) l -> p g j l", g=4)
            self.tt(POOL, d4, d4, CBm.unsqueeze(2).to_broadcast([128, 4, 4, 128]), ALU.mult, [b_dec, b_CBm], [b_dec])
            yield
            pxb = pb[1][:].bitcast(BF16)
            for c in range(NCH):
                self.tr(pxb[:, c * 128:(c + 1) * 128], xsT[:, c, :], [b_xs], [bpb[1]])
            px3 = pxb.rearrange("p (h d) -> p h d", h=16)
            self.tt(DVE, xdt, px3, dt_all[:, kt * 16:(kt + 1) * 16].unsqueeze(2).to_broadcast([128, 16, 64]), ALU.mult,
                    [bpb[1], b_dt], [b_xdt])
            self.tt(DVE, ytok.rearrange("p (h d) -> p h d", h=16), px3, bc3[:, 32:48].unsqueeze(2).to_broadcast([128, 16, 64]),
                    ALU.mult, [bpb[1], b_bc3], [b_ytok])
            yield
            if kt < NT - 1:
                pbb = pb[0][:].bitcast(BF16)
                for g in range(4):
                    self.tr(pbb[:, g * 128:(g + 1) * 128], BT[:, g, :], [b_B], [bpb[0]])
                self.act(Btok.rearrange("p g n -> p (g n)"), pbb[:, 0:512], AF.Copy, [bpb[0]], [b_Btok])
            yield

        def back(kt):
            tsl = slice(kt * 128, (kt + 1) * 128)
            E, b_E = Es[kt % 2], b_Es[kt % 2]
            dec, b_dec = decs[kt % 2], b_decs[kt % 2]
            xdt, b_xdt = xdts[kt % 2], b_xdts[kt % 2]
            ytok, b_ytok = ytoks[kt % 2], b_ytoks[kt % 2]
            Btok, b_Btok = Btoks[kt % 2], b_Btoks[kt % 2]
            CT, b_C = CTs[kt % 2], b_Cs[kt % 2]
            P.dma(SP, out=zsT, in_=scr[2048:3072, tsl].rearrange("(c p) t -> p c t", p=128), reads=b_scr[16:24], writes=[b_z])
            if kt < NT - 1:
                self.tt(POOL, xdd, xdt, E[:, 16:32].unsqueeze(2).to_broadcast([128, 16, 64]), ALU.mult, [b_xdt, b_E], [b_xdd])
            for h in range(HB):
                self.mm(pb[4 + h // 8][:, (h % 8) * 64:(h % 8 + 1) * 64], dec[:, h, :], xdt[:, h, :], True, True,
                        [b_dec, b_xdt], [bpb[4 + h // 8]])
            yield
            if kt > 0:
                for g in range(4):
                    self.mm(pb[6 + g // 2][:, (g % 2) * 256:(g % 2 + 1) * 256], CT[:, g, :], state_bf[:, g * 256:(g + 1) * 256],
                            True, True, [b_C, b_sbf], [bpb[6 + g // 2]])
            yield
            for hb in range(2):
                hs = slice(hb * 512, (hb + 1) * 512)
                if kt > 0:
                    self.tt(DVE, y1[:, hs].rearrange("p (h d) -> p h d", h=8), pb[6 + hb][:, :].rearrange("p (h d) -> p h d", h=8),
                            E[:, 8 * hb:8 * hb + 8].unsqueeze(2).to_broadcast([128, 8, 64]), ALU.mult, [bpb[6 + hb], b_E], [b_y1])
                    self.tt(DVE, y1[:, hs], y1[:, hs], pb[4 + hb][:, :], ALU.add, [b_y1, bpb[4 + hb]], [b_y1])
                    self.tt(DVE, ytok[:, hs], y1[:, hs], ytok[:, hs], ALU.add, [b_y1, b_ytok], [b_ytok])
                else:
                    self.tt(DVE, ytok[:, hs], pb[4 + hb][:, :], ytok[:, hs], ALU.add, [bpb[4 + hb], b_ytok], [b_ytok])
                yield
            if kt < NT - 1:
                for g in range(4):
                    self.mm(pb[6 + g // 2][:, (g % 2) * 256:(g % 2 + 1) * 256], Btok[:, g, :],
                            xdd[:, 4 * g:4 * g + 4, :].rearrange("p h d -> p (h d)"), True, True, [b_Btok, b_xdd], [bpb[6 + g // 2]])
                yield
                for hb in range(2):
                    hs = slice(hb * 512, (hb + 1) * 512)
                    if kt > 0:
                        s3 = state[:, hs].rearrange("p (h d) -> p h d", h=8)
                        self.tt(POOL, s3, s3, E[:, 32 + 8 * hb:32 + 8 * hb + 8].unsqueeze(2).to_broadcast([128, 8, 64]), ALU.mult,
                                [b_state, b_E], [b_state])
                        self.tt(DVE, state[:, hs], state[:, hs], pb[6 + hb][:, :], ALU.add, [b_state, bpb[6 + hb]], [b_state])
                    else:
                        self.cp(DVE, state[:, hs], pb[6 + hb][:, :], [bpb[6 + hb]], [b_state])
                    self.cp(POOL, state_bf[:, hs], state[:, hs], [b_state], [b_sbf])
                    yield
            pyb = pb[4][:].bitcast(BF16)
            for c in range(NCH):
                self.tr(pyb[:, c * 128:(c + 1) * 128], ytok[:, c * 128:(c + 1) * 128], [b_ytok], [bpb[4]])
            self.tt(DVE, zsT, pyb.rearrange("p (c t) -> p c t", c=8), zsT, ALU.mult, [bpb[4], b_z], [b_z])
            yield
            for g in range(4):
                self.act(sq.rearrange("p c t -> p (c t)"), zsT[:, 2 * g:2 * g + 2, :].rearrange("p c t -> p (c t)"), AF.Square,
                         [b_z], [b_sq])
                self.mm(pb[5][:, 0:128], self.onesb[:], sq[:, 0, :], True, False, [self.b_const, b_sq], [bpb[5]])
                self.mm(pb[5][:, 0:128], self.onesb[:], sq[:, 1, :], False, True, [self.b_const, b_sq], [bpb[5]])
                self.act(rt, pb[5][:, 0:128], AF.Sqrt, [bpb[5]], [b_rt], scale=1.0 / 256, bias=1e-6)
                self.recip(rt, rt, [b_rt], [b_rt])
                for c2 in range(2):
                    c = 2 * g + c2
                    self.stt(self.ybr[:, c, tsl], zsT[:, c, :], snw[:, c:c + 1], rt, ALU.mult, ALU.mult, [b_z, b_snw, b_rt],
                             [self.b_ybr[c]])
                yield

        def interleave(*gens):
            gens = [g for g in gens if g is not None]
            while gens:
                for g in list(gens):
                    try:
                        next(g)
                    except StopIteration:
                        gens.remove(g)

        load_front_tiles(0)
        interleave(front(0))
        for kt in range(NT):
            nxt = None
            if kt + 1 < NT:
                load_front_tiles(kt + 1)
                nxt = front(kt + 1)
            interleave(nxt, back(kt))

    def diff(self, s, l):
        P = self.P
        lam_init = 0.8 - 0.6 * math.exp(-0.3 * l)
        cosE = self.carve(0, L)
        sinE = self.carve(2048, L)
        qT = self.carve(4096, L)
        kT = self.carve(6144, L)
        vaug = self.carve(8192, 2064).rearrange("p (t d) -> p t d", t=NT)
        sg = self.carve(10272, L)
        o = self.carve(12320, L, F32).rearrange("p (t d) -> p t d", t=NT)
        on = self.carve(16416, L).rearrange("p (t d) -> p t d", t=NT)
        pT = [self.carve(18464 + 512 * i, 512) for i in range(4)]
        xq = [self.carve(20512, 512), self.carve(21024, 512)]
        tf = [self.carve(21536, 512, F32), self.carve(22560, 512, F32)]
        pm = self.carve(23584, 128)
        b_tab = Buf("ropetab")
        b_q, b_k = Buf("dq"), Buf("dk")
        b_v = [Buf("dv%d" % i) for i in range(4)]
        b_sg = [Buf("dsg%d" % i) for i in range(4)]
        b_o = [Buf("o%d" % i) for i in range(NT)]
        b_on = [Buf("on%d" % i) for i in range(NT)]
        b_pT = [Buf("dpT%d" % i) for i in range(4)]
        b_xq = [Buf("xq0"), Buf("xq1")]
        b_tf = [Buf("tf0"), Buf("tf1")]
        b_pm = Buf("pm")
        b_rc = [Buf("drc%d" % i) for i in range(4)]
        b_lam = Buf("lam")
        b_ss2 = Buf("ss2")
        sm = self.small
        lamt = sm[:, 96:104]
        slw = sm[:, 104:106]
        ss2 = sm[:, 112:128]
        rt2 = sm[:, 128:144]
        rstd2 = sm[:, 144:160]
        P.dma(POOL, out=cosE, in_=self.rope_d[0], writes=[b_tab])
        P.dma(POOL, out=sinE, in_=self.rope_d[1], writes=[b_tab])
        self.memset(POOL, pm, 0.0, [b_pm])
        for a in range(2):
            P.op(POOL, lambda e, a=a: e.affine_select(out=pm[:, 64 * a:64 * a + 8], in_=self.onesb[:, 0:8], pattern=[[-1, 8]],
                                                      compare_op=ALU.is_equal, fill=0.0, base=-64 * a - 8, channel_multiplier=1),
                 [self.b_const], [b_pm])
            P.op(POOL, lambda e, a=a: e.affine_select(out=pm[:, 64 * a + 8:64 * a + 16], in_=self.onesb[:, 0:8], pattern=[[-1, 8]],
                                                      compare_op=ALU.is_equal, fill=0.0, base=-64 * a, channel_multiplier=1),
                 [self.b_const], [b_pm])
        dl = tf[0][:, 0:256]
        P.dma(SP, out=dl, in_=self.diff_lambda_d[l:l + 1].rearrange("o a d -> o (a d)").partition_broadcast(128), writes=[b_tf[0]])
        junk = tf[1][:, 0:64]
        self.stt(junk, dl[:, 0:64], 1.0, dl[:, 64:128], ALU.mult, ALU.mult, [b_tf[0]], [b_tf[1], b_lam], accum_out=lamt[:, 0:1])
        self.stt(junk, dl[:, 128:192], 1.0, dl[:, 192:256], ALU.mult, ALU.mult, [b_tf[0]], [b_tf[1], b_lam], accum_out=lamt[:, 1:2])
        self.act(lamt[:, 2:4], lamt[:, 0:2], AF.Exp, [b_lam], [b_lam])
        self.tt(DVE, lamt[:, 4:5], lamt[:, 2:3], lamt[:, 3:4], ALU.subtract, [b_lam], [b_lam])
        self.ts(DVE, lamt[:, 5:6], lamt[:, 4:5], -1.0, -lam_init, ALU.mult, ALU.add, [b_lam], [b_lam])
        neglam = lamt[:, 5:6]
        P.dma(SP, out=slw[:, 0:1], in_=self.subln_w_d[l:l + 1].rearrange("o p -> p o"), writes=[b_lam], allow_slow_non_contiguous=True)
        self.ts(DVE, slw[:, 0:1], slw[:, 0:1], (1.0 - lam_init) * 0.5, None, ALU.mult, None, [b_lam], [b_lam])
        for g4 in range(4):
            self.memset(DVE, vaug[:, 4 * g4:4 * g4 + 4, 128:129], 1.0, [b_v[g4]])
        pS = [self.pb[2], self.pb[3]]
        b_pS = [self.b_pb[2], self.b_pb[3]]
        pO = self.pb[4:8]
        b_pO = self.b_pb[4:8]
        sidx = 0
        k2 = 0
        for h in range(8):
            wq, bwq = self.wload(self.w_in_cols(l, C_DQ + 128 * h, 128), 128)
            wk, bwk = self.wload(self.w_in_cols(l, C_DK + 128 * h, 128), 128)
            wv, bwv = self.wload(self.w_in_cols(l, C_DV + 128 * h, 128), 128)
            wg, bwg = self.wload(self.w_in_cols(l, C_DG + 128 * h, 128), 128)
            for tb in range(4):
                sl = slice(tb * 512, (tb + 1) * 512)
                for (w_, bw_, dst, bdst, scale) in ((wq, bwq, qT, b_q, 0.125), (wk, bwk, kT, b_k, 1.0)):
                    ps, bps = self.pj()
                    self.proj_fm(w_, bw_, 128, tb, ps, bps)
                    x_, bx_ = xq[k2 % 2], b_xq[k2 % 2]
                    t1, bt1 = tf[k2 % 2], b_tf[k2 % 2]
                    k2 += 1
                    self.ts(DVE, x_, ps[:, :], scale, None, ALU.mult, None, [bps], [bx_])
                    S, bS = pS[sidx % 2], b_pS[sidx % 2]
                    sidx += 1
                    self.mm(S[:, :], pm, x_, True, True, [b_pm, bx_], [bS])
                    self.tt(DVE, t1, S[:, :], sinE[:, sl], ALU.mult, [bS, b_tab], [bt1])
                    self.tt(POOL, x_, x_, cosE[:, sl], ALU.mult, [bx_, b_tab], [bx_])
                    self.tt(DVE, dst[:, sl], t1, x_, ALU.add, [bt1, bx_], [bdst])
                ps, bps = self.pj()
                self.proj_fm(wg, bwg, 128, tb, ps, bps)
                t1, bt1 = tf[k2 % 2], b_tf[k2 % 2]
                k2 += 1
                self.act(t1, ps[:, :], AF.Tanh, [bps], [bt1], scale=0.5)
                self.stt(sg[:, sl], t1, 1.0, ps[:, :], ALU.add, ALU.mult, [bt1, bps], [b_sg[tb]])
            for g4 in range(4):
                ps, bps = self.pj()
                for u in range(4):
                    kt = 4 * g4 + u
                    for c in range(NCH):
                        self.mm(ps[:, u * 128:(u + 1) * 128], self.hT[:, c, kt * 128:(kt + 1) * 128], wv[:, c, :],
                                c == 0, c == NCH - 1, [bwv, self.b_hT[kt]], [bps])
                self.cp(DVE, vaug[:, 4 * g4:4 * g4 + 4, 0:128], ps[:, :].rearrange("p (t d) -> p t d", t=4), [bps], [b_v[g4]])
            tiles = [(comp, jq, i) for comp in range(2) for jq in range(4) for i in range(4 * jq + 4)]

            def emitS(tile, idx):
                comp, jq, i = tile
                pr = slice(64 * comp, 64 * comp + 64)
                r_ = i - 4 * jq
                qlo = max(512 * jq, 128 * i)
                w = 512 * (jq + 1) - qlo
                S, bS = pS[idx % 2], b_pS[idx % 2]
                pt, bpt = pT[idx % 4], b_pT[idx % 4]
                self.mm(S[:, 0:w], kT[pr, i * 128:(i + 1) * 128], qT[pr, qlo:qlo + w], True, r_ < 0, [b_k, b_q], [bS])
                if r_ >= 0:
                    self.mm(S[:, 0:128], self.ident[:], self.bmask[:], False, True, [self.b_const], [bS])
                self.act(pt[:, 0:w], S[:, 0:w], AF.Exp, [bS], [bpt])

            def emitPV(tile, idx):
                comp, jq, i = tile
                qlo = max(512 * jq, 128 * i)
                w = 512 * (jq + 1) - qlo
                pt, bpt = pT[idx % 4], b_pT[idx % 4]
                for cc in range(w // 128):
                    qc = qlo // 128 + cc
                    O, bO = pO[qc % 4], b_pO[qc % 4]
                    self.mm(O[:, 0:129], pt[:, cc * 128:(cc + 1) * 128], vaug[:, i, :], i == 0, i == qc,
                            [bpt, b_v[i // 4]], [bO])
                    if i == qc:
                        rcq = self.rc[:, qc:qc + 1]
                        self.recip(rcq, O[:, 128:129], [bO], [b_rc[qc % 4]])
                        if comp == 0:
                            self.ts(DVE, o[:, qc, :], O[:, 0:128], rcq, None, ALU.mult, None, [bO, b_rc[qc % 4]], [b_o[qc]])
                        else:
                            t1, bt1 = tf[qc % 2], b_tf[qc % 2]
                            self.ts(DVE, t1[:, 0:128], O[:, 0:128], rcq, neglam, ALU.mult, ALU.mult,
                                    [bO, b_rc[qc % 4], b_lam], [bt1])
                            self.tt(DVE, o[:, qc, :], t1[:, 0:128], o[:, qc, :], ALU.add, [bt1, b_o[qc]], [b_o[qc]])
                            self.stt(t1[:, 128:256], o[:, qc, :], 1.0, o[:, qc, :], ALU.mult, ALU.mult,
                                     [b_o[qc]], [bt1, b_ss2], accum_out=ss2[:, qc:qc + 1])

            for n_, tile in enumerate(tiles):
                emitS(tile, sidx + n_)
                if n_ > 0:
                    emitPV(tiles[n_ - 1], sidx + n_ - 1)
            emitPV(tiles[-1], sidx + len(tiles) - 1)
            sidx += len(tiles)
            self.act(rt2, ss2, AF.Sqrt, [b_ss2], [b_ss2], scale=1.0 / 128, bias=1e-6)
            self.recip(rstd2, rt2, [b_ss2], [b_ss2])
            for qc in range(NT):
                self.ts(POOL if qc % 2 else DVE, on[:, qc, :], o[:, qc, :], rstd2[:, qc:qc + 1], None, ALU.mult, None,
                        [b_o[qc], b_ss2], [b_on[qc]])
            for g4 in range(4):
                ps, bps = self.pj()
                psb = ps[:].bitcast(BF16)
                for u in range(4):
                    qc = 4 * g4 + u
                    self.tr(psb[:, u * 128:(u + 1) * 128], on[:, qc, :], [b_on[qc]], [bps])
                self.stt(self.ybr[:, h, g4 * 512:(g4 + 1) * 512], psb[:, 0:512], slw[:, 0:1], sg[:, g4 * 512:(g4 + 1) * 512],
                         ALU.mult, ALU.mult, [bps, b_sg[g4], b_lam], [self.b_ybr[h]])


def _rope_tables():
    pos = np.arange(L, dtype=np.float32)
    inv_freq = (np.float32(500000.0) ** (-np.arange(0, 16, 2, dtype=np.float32) / np.float32(16))).astype(np.float32)
    ang = (pos[:, None] * inv_freq[None, :]).astype(np.float32)
    cos = np.cos(ang).astype(np.float32)
    sin = np.sin(ang).astype(np.float32)
    cosE = np.ones((128, L), np.float32)
    sinE = np.zeros((128, L), np.float32)
    for m in range(128):
        i = m % 64
        if i < 8:
            cosE[m] = cos[:, i]
            sinE[m] = -sin[:, i]
        elif i < 16:
            cosE[m] = cos[:, i - 8]
            sinE[m] = sin[:, i - 8]
    return np.ascontiguousarray(np.stack([cosE, sinE]))


_NC_CACHE = {}


def kernel(x, norm_w, w_in, b_forget, conv_w, conv_b, dt_bias, a_log, d_skip, ssm_norm_w, diff_lambda, subln_w,
           w_branch, w_out, final_norm_w):
    ncores = 8
    nseq = x.shape[0] // ncores
    if 'nc' not in _NC_CACHE:
        _NC_CACHE['nc'] = Gen(nseq, 2, "abc", final=True).build()
    nc = _NC_CACHE['nc']
    f = lambda a: np.ascontiguousarray(np.asarray(a, dtype=np.float32))
    shared = dict(norm_w=f(norm_w), w_in=f(w_in), b_forget=f(b_forget), conv_w=f(conv_w), conv_b=f(conv_b),
                  dt_bias=f(dt_bias), a_log=f(a_log), d_skip=f(d_skip), ssm_norm_w=f(ssm_norm_w),
                  diff_lambda=f(diff_lambda), subln_w=f(subln_w), w_branch=f(w_branch), w_out=f(w_out),
                  final_norm_w=f(final_norm_w).reshape(1, -1), rope=_rope_tables())
    xs = f(x)
    in_maps = [dict(shared, x=np.ascontiguousarray(xs[i * nseq:(i + 1) * nseq])) for i in range(ncores)]
    res = run_bass_kernel_spmd(nc, in_maps, core_ids=list(range(ncores)))
    return np.concatenate([np.asarray(r["out"], dtype=np.float32) for r in res.results], axis=0)
```

```python
import numpy as np
import concourse.bass as bass
import concourse.mybir as mybir
from concourse.bass_utils import run_bass_kernel_spmd

F32 = mybir.dt.float32
BF16 = mybir.dt.bfloat16
AF = mybir.ActivationFunctionType
ALU = mybir.AluOpType
AX = mybir.AxisListType

PE, DVE, ACT, POOL, SP = 'tensor', 'vector', 'scalar', 'gpsimd', 'sync'
ENGS = [PE, DVE, ACT, POOL, SP]


class Buf:
    __slots__ = ('name', 'w', 'r', 'rd', 'dsem')

    def __init__(self, name):
        self.name = name
        self.w = None
        self.r = {}
        self.rd = []
        self.dsem = None


class Op:
    __slots__ = ('eng', 'fn', 'deps', 'dma_sem', 'tok', 'waited')

    def __init__(self, eng, fn):
        self.eng = eng
        self.fn = fn
        self.deps = []
        self.dma_sem = None
        self.tok = None
        self.waited = False


class DmaSem:
    def __init__(self, sem):
        self.sem = sem
        self.count = 0
        self.last = None
        self.q = None


class Prog:
    def __init__(self, nc, stack):
        self.nc = nc
        self.stack = stack
        self.ops = {e: [] for e in ENGS}
        self.esem = {}
        self.nops = 0
        self.dsems = []
        self.nsem = 0
        self.dpool = {}
        self.drr = {}

    def enter(self, cm):
        return self.stack.enter_context(cm)

    def sbuf(self, name, shape, dtype):
        return self.enter(self.nc.sbuf_tensor(name, list(shape), dtype))

    def psum(self, name, shape, dtype):
        return self.enter(self.nc.psum_tensor(name, list(shape), dtype))

    def dram(self, name, shape, dtype, kind):
        return self.nc.dram_tensor(name, list(shape), dtype, kind=kind).ap()

    def dsem(self, name=None):
        self.nsem += 1
        d = DmaSem(self.enter(self.nc.semaphore(name or ('ds%d' % self.nsem))))
        self.dsems.append(d)
        return d

    def op(self, eng, fn, reads=(), writes=(), dsem=None):
        o = Op(eng, fn)
        deps = []
        for b in reads:
            if b.w is not None:
                deps.append(b.w)
        for b in writes:
            if b.w is not None:
                deps.append(b.w)
            deps.extend(b.r.values())
            deps.extend(b.rd)
        seen = set()
        for d in deps:
            if id(d) in seen or d is o:
                continue
            seen.add(id(d))
            if d.dma_sem is None and d.eng == eng and eng == PE:
                continue
            o.deps.append(d)
        if dsem is not None:
            if dsem.last is not None and all(d is not dsem.last for d in o.deps):
                o.deps.append(dsem.last)
            o.dma_sem = dsem
            dsem.count += 16
            dsem.last = o
            o.tok = (dsem.sem, dsem.count)
        for b in reads:
            if dsem is not None:
                b.rd.append(o)
            else:
                b.r[eng] = o
        for b in writes:
            b.w = o
            b.r = {}
            b.rd = []
        self.ops[eng].append(o)
        self.nops += 1
        return o

    def dma(self, eng, out, in_, reads=(), writes=(), **kw):
        b = writes[0]
        if b.dsem is None or b.dsem.q != eng:
            pool = self.dpool.setdefault(eng, [])
            if len(pool) < 24:
                d = self.dsem()
                d.q = eng
                pool.append(d)
            else:
                d = pool[self.drr.get(eng, 0) % 24]
                self.drr[eng] = self.drr.get(eng, 0) + 1
            b.dsem = d
        return self.op(eng, lambda e: e.dma_start(out=out, in_=in_, **kw), reads=reads, writes=writes, dsem=b.dsem)

    def barrier(self):
        lasts = []
        for e in ENGS:
            if self.ops[e]:
                lasts.append(self.ops[e][-1])
        for d in self.dsems:
            if d.last is not None:
                lasts.append(d.last)
        new = []
        for e in ENGS:
            o = Op(e, lambda eng: eng.nop())
            for d in lasts:
                if d.dma_sem is None and d.eng == e:
                    continue
                o.deps.append(d)
            new.append(o)
        for o in new:
            self.ops[o.eng].append(o)

    def finish(self):
        o = Op(SP, lambda eng: eng.nop())
        for d in self.dsems:
            if d.last is not None:
                o.deps.append(d.last)
        self.ops[SP].append(o)
        self.emit()

    def emit(self):
        nc = self.nc
        for e in ENGS:
            self.esem[e] = self.enter(nc.semaphore('sem_' + e))
        for e in ENGS:
            for o in self.ops[e]:
                for d in o.deps:
                    d.waited = True
        self.nmile = {}
        for e in ENGS:
            c = 0
            for o in self.ops[e]:
                if o.dma_sem is None and o.waited:
                    c += 1
                    o.tok = (self.esem[e], c)
            self.nmile[e] = c
        prog = self

        def body(e):
            def run(eng):
                have = {}
                for o in prog.ops[e]:
                    need = {}
                    for d in o.deps:
                        s, v = d.tok
                        k = id(s)
                        if have.get(k, 0) >= v:
                            continue
                        if k not in need or need[k][1] < v:
                            need[k] = (s, v)
                    for k, (s, v) in need.items():
                        eng.wait_ge(s, v)
                        have[k] = v
                    ins = o.fn(eng)
                    if o.dma_sem is not None:
                        ins.then_inc(o.tok[0], 16)
                    elif o.waited:
                        ins.then_inc(o.tok[0], 1)
            return run

        with nc.Block() as block:
            block.tensor(body(PE))
            block.vector(body(DVE))
            block.scalar(body(ACT))
            block.gpsimd(body(POOL))
            block.sync(body(SP))


import math
from contextlib import ExitStack

L = 2048
D = 1024
NT = 16
NCH = 8
N_IN = 14368
C_FQ, C_FK, C_FV, C_FF, C_FG = 0, 1024, 2048, 3072, 3088
C_SZ, C_SX, C_SB, C_SC, C_SDT = 4112, 5136, 6160, 6672, 7184
C_DQ, C_DK, C_DV, C_DG, C_MG = 7200, 8224, 9248, 10272, 11296
NEG = -30000.0
ARENA = 24576
NSLOT = 6


class Gen:
    def __init__(self, nseq, nlayers, branches, final=True, dbg=None):
        self.nseq, self.nlayers, self.branches, self.final, self.dbg = nseq, nlayers, branches, final, dbg

    def mm(self, out, lhsT, rhs, start, stop, reads, writes):
        return self.P.op(PE, lambda e: e.matmul(out, lhsT=lhsT, rhs=rhs, start=start, stop=stop), reads, writes)

    def tr(self, out, in_, reads, writes):
        ident = self.ident
        return self.P.op(PE, lambda e: e.transpose(out=out, in_=in_, identity=ident[:]), list(reads) + [self.b_const], writes)

    def act(self, out, in_, func, reads, writes, **kw):
        return self.P.op(ACT, lambda e: e.activation(out=out, in_=in_, func=func, **kw), reads, writes)

    def tt(self, eng, out, in0, in1, op, reads, writes):
        return self.P.op(eng, lambda e: e.tensor_tensor(out=out, in0=in0, in1=in1, op=op), reads, writes)

    def ts(self, eng, out, in0, s1, s2, op0, op1, reads, writes, **kw):
        if op1 is None:
            return self.P.op(eng, lambda e: e.tensor_scalar(out=out, in0=in0, scalar1=s1, scalar2=None, op0=op0, **kw), reads, writes)
        return self.P.op(eng, lambda e: e.tensor_scalar(out=out, in0=in0, scalar1=s1, scalar2=s2, op0=op0, op1=op1, **kw), reads, writes)

    def stt(self, out, in0, scalar, in1, op0, op1, reads, writes, **kw):
        return self.P.op(DVE, lambda e: e.scalar_tensor_tensor(out=out, in0=in0, scalar=scalar, in1=in1, op0=op0, op1=op1, **kw), reads, writes)

    def cp(self, eng, out, in_, reads, writes):
        return self.P.op(eng, lambda e: e.tensor_copy(out=out, in_=in_), reads, writes)

    def memset(self, eng, ap, val, writes):
        return self.P.op(eng, lambda e: e.memset(ap, val), (), writes)

    def recip(self, out, in_, reads, writes):
        return self.P.op(DVE, lambda e: e.reciprocal(out=out, in_=in_), reads, writes)

    def carve(self, off, n, dtype=BF16):
        if dtype == F32:
            return self.arena[:, off:off + 2 * n].bitcast(F32)
        return self.arena[:, off:off + n]

    def wload(self, src, n):
        i = self.wrr % NSLOT
        self.wrr += 1
        ap = self.wslot[i][:, :, 0:n]
        buf = self.b_wslot[i]
        self.P.dma(POOL, out=ap, in_=src.rearrange("(c p) n -> p c n", p=128), writes=[buf])
        return ap, buf

    def w_in_cols(self, l, c0, n):
        return self.w_in_d[l, :, c0:c0 + n]

    def pj(self):
        i = self.pjrr % 2
        self.pjrr += 1
        return self.pb[i], self.b_pb[i]

    def hbufs(self, tb):
        return self.b_hT[4 * tb:4 * tb + 4]

    def proj_fm(self, w, bw, M, tb, ps, bps, rhs_src=None, rhs_bufs=None):
        src = self.hT if rhs_src is None else rhs_src
        rb = self.hbufs(tb) if rhs_bufs is None else rhs_bufs
        for c in range(NCH):
            self.mm(ps[0:M, :], w[:, c, 0:M], src[:, c, tb * 512:(tb + 1) * 512], c == 0, c == NCH - 1,
                    [bw] + list(rb), [bps])

    def build(self):
        nc = bass.Bass("TRN2", target_bir_lowering=False)
        self.nc = nc
        nseq = self.nseq
        di = lambda name, shape: nc.dram_tensor(name, list(shape), F32, kind="ExternalInput").ap()
        self.x_d = di("x", [nseq, L, D])
        self.norm_w_d = di("norm_w", [2, D])
        self.w_in_d = di("w_in", [2, D, N_IN])
        self.b_forget_d = di("b_forget", [2, 16])
        self.conv_w_d = di("conv_w", [2, 4, 2048])
        self.conv_b_d = di("conv_b", [2, 2048])
        self.dt_bias_d = di("dt_bias", [2, 16])
        self.a_log_d = di("a_log", [2, 16])
        self.d_skip_d = di("d_skip", [2, 16])
        self.ssm_norm_w_d = di("ssm_norm_w", [2, D])
        self.diff_lambda_d = di("diff_lambda", [2, 4, 64])
        self.subln_w_d = di("subln_w", [2, 128])
        self.w_branch_d = di("w_branch", [2, 3, D, D])
        self.w_out_d = di("w_out", [2, D, D])
        self.final_norm_w_d = di("final_norm_w", [1, D])
        self.rope_d = di("rope", [2, 128, L])
        self.out_d = nc.dram_tensor("out", [nseq, L, D], F32, kind="ExternalOutput").ap()
        self.gk_d = nc.dram_tensor("gk_scr", [16, 6, L], BF16, kind="Internal").ap()
        self.scr_d = nc.dram_tensor("ssd_scr", [3072, L], BF16, kind="ExternalOutput").ap()
        if self.dbg:
            self.dbg_d = nc.dram_tensor("dbg", [128, 8, L], BF16, kind="ExternalOutput").ap()
        with ExitStack() as st:
            P = Prog(nc, st)
            self.P = P
            self.xres = P.sbuf("xres", [128, NT, D], F32)
            self.hT = P.sbuf("hT", [128, NCH, L], BF16)
            self.ybr = P.sbuf("ybr", [128, NCH, L], BF16)
            self.wout = P.sbuf("wout", [128, NCH, D], BF16)
            self.arena = P.sbuf("arena", [128, ARENA], BF16)
            self.wslot = [P.sbuf("wslot%d" % i, [128, NCH, 128], BF16) for i in range(NSLOT)]
            self.ident = P.sbuf("ident", [128, 128], BF16)
            self.cmask = P.sbuf("cmask", [128, 128], BF16)
            self.bmask = P.sbuf("bmask", [128, 128], BF16)
            self.onesb = P.sbuf("onesb", [128, 128], BF16)
            self.zerob = P.sbuf("zerob", [128, 128], BF16)
            self.small = P.sbuf("small", [128, 256], F32)
            self.pb = [P.psum("pb%d" % i, [128, 512], F32) for i in range(8)]
            self.b_pb = [Buf("pb%d" % i) for i in range(8)]
            self.b_x = [Buf("x%d" % t) for t in range(NT)]
            self.b_hT = [Buf("hT%d" % t) for t in range(NT)]
            self.b_ybr = [Buf("ybr%d" % c) for c in range(NCH)]
            self.b_wout = Buf("wout")
            self.b_wslot = [Buf("ws%d" % i) for i in range(NSLOT)]
            self.b_const = Buf("const")
            self.b_small = Buf("small")
            self.b_gk = Buf("gk")
            self.b_out = [Buf("out%d" % t) for t in range(NT)]
            self.wrr = 0
            self.pjrr = 0
            sm = self.small
            self.nwT = sm[:, 0:16]
            self.negbf = sm[0:16, 16:18]
            self.ss = sm[:, 32:48]
            self.rt = sm[:, 48:64]
            self.rstd = sm[:, 64:80]
            self.rc = sm[:, 80:96]
            self.consts()
            for s in range(nseq):
                for t in range(NT):
                    P.dma(SP, out=self.xres[:, t, :], in_=self.x_d[s, t * 128:(t + 1) * 128, :], writes=[self.b_x[t]])
                for l in range(self.nlayers):
                    self.layer(s, l)
                self.final_out(s)
            P.finish()
        return nc

    def consts(self):
        P = self.P
        bc = self.b_const
        self.memset(POOL, self.onesb[:], 1.0, [bc])
        self.memset(POOL, self.zerob[:], 0.0, [bc])
        P.op(POOL, lambda e: e.affine_select(out=self.ident[:], in_=self.onesb[:], pattern=[[-1, 128]], compare_op=ALU.is_equal,
                                              fill=0.0, base=0, channel_multiplier=1), [bc], [bc])
        P.op(POOL, lambda e: e.affine_select(out=self.cmask[:], in_=self.zerob[:], pattern=[[1, 128]], compare_op=ALU.is_ge,
                                              fill=NEG, base=0, channel_multiplier=-1), [bc], [bc])
        self.memset(POOL, self.bmask[:], 0.0, [bc])
        self.memset(POOL, self.bmask[64:128, 0:64], NEG, [bc])
        P.dma(SP, out=self.nwT.rearrange("p (l c) -> p l c", l=2), in_=self.norm_w_d.rearrange("l (c p) -> p l c", p=128),
              writes=[self.b_small], allow_slow_non_contiguous=True)
        P.dma(SP, out=self.negbf, in_=self.b_forget_d.rearrange("l h -> h l"), writes=[self.b_small], allow_slow_non_contiguous=True)
        self.ts(DVE, self.negbf, self.negbf, -1.0, None, ALU.mult, None, [self.b_small], [self.b_small])

    def layer(self, s, l):
        P = self.P
        for eb in range(2):
            P.dma(POOL, out=self.wout[:, :, eb * 512:(eb + 1) * 512],
                  in_=self.w_out_d[l, :, eb * 512:(eb + 1) * 512].rearrange("(c p) n -> p c n", p=128), writes=[self.b_wout])
        P.barrier()
        self.rmsnorm_T(l)
        for n, br in enumerate("abc"):
            if br not in self.branches:
                continue
            P.barrier()
            if br == 'a':
                self.fox(s, l)
            elif br == 'b':
                self.ssd(s, l)
            else:
                self.diff(s, l)
            if self.dbg == br and l == self.nlayers - 1:
                for c in range(NCH):
                    P.dma(SP, out=self.dbg_d[:, c, :], in_=self.ybr[:, c, :], reads=[self.b_ybr[c]], writes=[Buf("dbg%d" % c)])
            P.barrier()
            self.merge(l, n)

    def rmsnorm_T(self, l):
        junk = self.carve(0, 1024)
        hn = [self.carve(1024, 1024), self.carve(2048, 1024)]
        b_junk = Buf("junk")
        b_hn = [Buf("hn0"), Buf("hn1")]
        b_ss = Buf("ss")
        for t in range(NT):
            self.act(junk, self.xres[:, t, :], AF.Square, [self.b_x[t]], [b_junk, b_ss], accum_out=self.ss[:, t:t + 1])
        self.act(self.rt, self.ss, AF.Sqrt, [b_ss], [b_ss], scale=1.0 / D, bias=1e-6)
        self.recip(self.rstd, self.rt, [b_ss], [b_ss])
        for t in range(NT):
            k = t % 2
            self.act(hn[k], self.xres[:, t, :], AF.Copy, [self.b_x[t], b_ss], [b_hn[k]], scale=self.rstd[:, t:t + 1])
            ps, bps = self.pj()
            psb = ps[:].bitcast(BF16)
            for c in range(NCH):
                self.tr(psb[:, c * 128:(c + 1) * 128], hn[k][:, c * 128:(c + 1) * 128], [b_hn[k]], [bps])
            self.tt(DVE, self.hT[:, :, t * 128:(t + 1) * 128], psb.rearrange("p (c n) -> p c n", c=NCH),
                    self.nwT[:, l * 8:(l + 1) * 8].unsqueeze(2).to_broadcast([128, NCH, 128]), ALU.mult,
                    [bps, self.b_small], [self.b_hT[t]])

    def merge(self, l, n):
        gp = self.carve(0, NCH * L).rearrange("p (c t) -> p c t", c=NCH)
        th = [self.carve(NCH * L, 512, F32), self.carve(NCH * L + 1024, 512, F32)]
        b_gp = [Buf("gp%d" % t) for t in range(4)]
        b_th = [Buf("th0"), Buf("th1")]
        k = 0
        for co in range(NCH):
            wb, bwb = self.wload(self.w_branch_d[l, n, :, co * 128:(co + 1) * 128], 128)
            wg, bwg = self.wload(self.w_in_cols(l, C_MG + n * 1024 + co * 128, 128), 128)
            for tb in range(4):
                ps1, bps1 = self.pj()
                self.proj_fm(wb, bwb, 128, tb, ps1, bps1, rhs_src=self.ybr, rhs_bufs=self.b_ybr)
                ps2, bps2 = self.pb[2 + k % 2], self.b_pb[2 + k % 2]
                self.proj_fm(wg, bwg, 128, tb, ps2, bps2)
                t_ = th[k % 2]
                self.act(t_, ps2[:, :], AF.Tanh, [bps2], [b_th[k % 2]], scale=0.5)
                self.stt(gp[:, co, tb * 512:(tb + 1) * 512], t_, 1.0, ps1[:, :], ALU.add, ALU.mult,
                         [b_th[k % 2], bps1], [b_gp[tb]])
                k += 1
        for t in range(NT):
            for eb in range(2):
                ps, bps = self.pb[4 + (2 * t + eb) % 4], self.b_pb[4 + (2 * t + eb) % 4]
                for co in range(NCH):
                    self.mm(ps[:, :], gp[:, co, t * 128:(t + 1) * 128], self.wout[:, co, eb * 512:(eb + 1) * 512],
                            co == 0, co == NCH - 1, [b_gp[t // 4], self.b_wout], [bps])
                xs = self.xres[:, t, eb * 512:(eb + 1) * 512]
                self.stt(xs, ps[:, :], 0.5, xs, ALU.mult, ALU.add, [bps, self.b_x[t]], [self.b_x[t]])

    def final_out(self, s):
        P = self.P
        P.barrier()
        b_ss = Buf("fss")
        if self.final:
            junk = self.carve(0, 1024)
            fnw = self.carve(1024, 1024, F32)
            b_junk, b_fnw = Buf("fjunk"), Buf("fnw")
            P.dma(SP, out=fnw, in_=self.final_norm_w_d.partition_broadcast(128), writes=[b_fnw])
            for t in range(NT):
                self.act(junk, self.xres[:, t, :], AF.Square, [self.b_x[t]], [b_junk, b_ss], accum_out=self.ss[:, t:t + 1])
            self.act(self.rt, self.ss, AF.Sqrt, [b_ss], [b_ss], scale=1.0 / D, bias=1e-6)
            self.recip(self.rstd, self.rt, [b_ss], [b_ss])
            for t in range(NT):
                xs = self.xres[:, t, :]
                self.stt(xs, xs, self.rstd[:, t:t + 1], fnw, ALU.mult, ALU.mult, [self.b_x[t], b_ss, b_fnw], [self.b_x[t]])
        for t in range(NT):
            P.dma(SP, out=self.out_d[s, t * 128:(t + 1) * 128, :], in_=self.xres[:, t, :], reads=[self.b_x[t]], writes=[self.b_out[t]])

    def fox(self, s, l):
        P = self.P
        lf = self.carve(0, L, F32)[0:16, :]
        G = self.carve(4096, L, F32)[0:16, :]
        r = self.carve(8192, L, F32)[0:16, :]
        parts = [self.carve(12288 + 2048 * j, L)[0:16, :] for j in range(6)]
        b_lf, b_G, b_r = Buf("lf"), Buf("G"), Buf("r")
        b_parts = [Buf("part%d" % j) for j in range(6)]
        wff, bwff = self.wload(self.w_in_cols(l, C_FF, 16), 16)
        for tb in range(4):
            ps, bps = self.pj()
            self.proj_fm(wff, bwff, 16, tb, ps, bps)
            self.act(lf[:, tb * 512:(tb + 1) * 512], ps[0:16, :], AF.Exp, [bps, self.b_small], [b_lf],
                     scale=-1.0, bias=self.negbf[:, l:l + 1])
        self.act(lf, lf, AF.Ln, [b_lf], [b_lf], bias=1.0)
        P.op(DVE, lambda e: e.tensor_tensor_scan(out=G, data0=lf, data1=lf, initial=0.0, op0=ALU.add, op1=ALU.max), [b_lf], [b_G])
        self.cp(DVE, parts[0], G, [b_G], [b_parts[0]])
        self.tt(DVE, r, G, parts[0], ALU.subtract, [b_G, b_parts[0]], [b_r])
        self.cp(DVE, parts[1], r, [b_r], [b_parts[1]])
        self.tt(DVE, r, r, parts[1], ALU.subtract, [b_r, b_parts[1]], [b_r])
        self.cp(DVE, parts[2], r, [b_r], [b_parts[2]])
        for j in range(3):
            self.ts(DVE, parts[3 + j], parts[j], -1.0, None, ALU.mult, None, [b_parts[j]], [b_parts[3 + j]])
        for j in range(6):
            P.dma(SP, out=self.gk_d[:, j, :], in_=parts[j], reads=[b_parts[j]], writes=[self.b_gk])
        P.barrier()
        qaug = [self.carve(0, L), self.carve(2048, L)]
        kaug = [self.carve(4096, L), self.carve(6144, L)]
        vaug = self.carve(8192, 4096).rearrange("p (t h d) -> p t h d", t=NT, h=2)
        sg = self.carve(12288, L)
        pT = [self.carve(14336 + 512 * i, 512) for i in range(4)]
        th = [self.carve(16384, 512, F32), self.carve(17408, 512, F32)]
        rinv = [self.carve(18432, 512, F32), self.carve(19456, 512, F32)]
        tmpn = [self.carve(20480, 512, F32), self.carve(21504, 512, F32)]
        b_q = [Buf("qaug0"), Buf("qaug1")]
        b_k = [Buf("kaug0"), Buf("kaug1")]
        b_v = [Buf("vaug%d" % i) for i in range(4)]
        b_sg = [Buf("sg%d" % i) for i in range(4)]
        b_pT = [Buf("pT%d" % i) for i in range(4)]
        b_th = [Buf("fth0"), Buf("fth1")]
        b_rinv = [Buf("rinv0"), Buf("rinv1")]
        b_tmpn = [Buf("tmpn0"), Buf("tmpn1")]
        for hh in range(2):
            self.memset(DVE, kaug[hh][64:67, :], 1.0, [b_k[hh]])
            self.memset(DVE, qaug[hh][64:67, :], 1.0, [b_q[hh]])
            P.dma(SP, out=qaug[hh][67:70, :], in_=qaug[hh][64:67, :], reads=[b_q[hh]], writes=[b_q[hh]])
        for g4 in range(4):
            self.memset(DVE, vaug[:, 4 * g4:4 * g4 + 4, 0, 64:128], 1.0, [b_v[g4]])
            self.memset(DVE, vaug[:, 4 * g4:4 * g4 + 4, 1, 0:64], 1.0, [b_v[g4]])
        pS = self.pb[2:6]
        b_pS = self.b_pb[2:6]
        pO = self.pb[6:8]
        b_pO = self.b_pb[6:8]
        sidx = 0
        thk = 0
        blk = 0
        for j in range(8):
            wq, bwq = self.wload(self.w_in_cols(l, C_FQ + 128 * j, 128), 128)
            wk, bwk = self.wload(self.w_in_cols(l, C_FK + 128 * j, 128), 128)
            wv, bwv = self.wload(self.w_in_cols(l, C_FV + 128 * j, 128), 128)
            wg, bwg = self.wload(self.w_in_cols(l, C_FG + 128 * j, 128), 128)
            for hh in range(2):
                h = 2 * j + hh
                P.dma(SP, out=kaug[hh][67:70, :], in_=self.gk_d[h, 0:3, :], reads=[self.b_gk], writes=[b_k[hh]])
                P.dma(SP, out=qaug[hh][64:67, :], in_=self.gk_d[h, 3:6, :], reads=[self.b_gk], writes=[b_q[hh]])
            for tb in range(4):
                sl = slice(tb * 512, (tb + 1) * 512)
                ps, bps = self.pj()
                self.proj_fm(wq, bwq, 128, tb, ps, bps)
                for hh in range(2):
                    self.ts(DVE, qaug[hh][0:64, sl], ps[64 * hh:64 * hh + 64, :], 0.125, None, ALU.mult, None, [bps], [b_q[hh]])
                ps, bps = self.pj()
                self.proj_fm(wk, bwk, 128, tb, ps, bps)
                for hh in range(2):
                    self.cp(DVE, kaug[hh][0:64, sl], ps[64 * hh:64 * hh + 64, :], [bps], [b_k[hh]])
                ps, bps = self.pj()
                self.proj_fm(wg, bwg, 128, tb, ps, bps)
                t_ = th[thk % 2]
                self.act(t_, ps[:, :], AF.Tanh, [bps], [b_th[thk % 2]], scale=0.5)
                self.stt(sg[:, sl], t_, 1.0, ps[:, :], ALU.add, ALU.mult, [b_th[thk % 2], bps], [b_sg[tb]])
                thk += 1
            for g4 in range(4):
                ps, bps = self.pj()
                for u in range(4):
                    kt = 4 * g4 + u
                    for c in range(NCH):
                        self.mm(ps[:, u * 128:(u + 1) * 128], self.hT[:, c, kt * 128:(kt + 1) * 128], wv[:, c, :],
                                c == 0, c == NCH - 1, [bwv, self.b_hT[kt]], [bps])
                ps4 = ps[:, :].rearrange("p (t h d) -> p t h d", t=4, h=2)
                self.cp(DVE, vaug[:, 4 * g4:4 * g4 + 4, 0, 0:64], ps4[:, :, 0, :], [bps], [b_v[g4]])
                self.cp(DVE, vaug[:, 4 * g4:4 * g4 + 4, 1, 64:128], ps4[:, :, 1, :], [bps], [b_v[g4]])
            tiles = [(hh, jq, i) for hh in range(2) for jq in range(4) for i in range(4 * jq + 4)]

            def emitS(tile, idx):
                hh, jq, i = tile
                r_ = i - 4 * jq
                qlo = max(512 * jq, 128 * i)
                w = 512 * (jq + 1) - qlo
                S, bS = pS[idx % 4], b_pS[idx % 4]
                pt, bpt = pT[idx % 4], b_pT[idx % 4]
                self.mm(S[:, 0:w], kaug[hh][0:70, i * 128:(i + 1) * 128], qaug[hh][0:70, qlo:qlo + w],
                        True, r_ < 0, [b_k[hh], b_q[hh]], [bS])
                if r_ >= 0:
                    self.mm(S[:, 0:128], self.ident[:], self.cmask[:], False, True, [self.b_const], [bS])
                self.act(pt[:, 0:w], S[:, 0:w], AF.Exp, [bS], [bpt])

            def emitPV(tile, idx, bi):
                hh, jq, i = tile
                qlo = max(512 * jq, 128 * i)
                w = 512 * (jq + 1) - qlo
                qoff = qlo - 512 * jq
                pt, bpt = pT[idx % 4], b_pT[idx % 4]
                O, bO = pO[bi % 2], b_pO[bi % 2]
                last = (i == 4 * jq + 3)
                self.mm(O[:, qoff:qoff + w], vaug[:, i, hh, :], pt[:, 0:w], i == 0, last, [bpt, b_v[i // 4]], [bO])
                if last:
                    pr = slice(0, 64) if hh == 0 else slice(64, 128)
                    sr = slice(64, 128) if hh == 0 else slice(0, 64)
                    ri, bri = rinv[bi % 2], b_rinv[bi % 2]
                    tm, btm = tmpn[bi % 2], b_tmpn[bi % 2]
                    self.recip(ri[pr, :], O[sr, :], [bO], [bri])
                    self.tt(DVE, tm[pr, :], O[pr, :], ri[pr, :], ALU.mult, [bO, bri], [btm])
                    qs = slice(jq * 512, (jq + 1) * 512)
                    self.stt(self.ybr[pr, j, qs], tm[pr, :], 0.5, sg[pr, qs], ALU.mult, ALU.mult, [btm, b_sg[jq]], [self.b_ybr[j]])

            bis = []
            for (hh, jq, i) in tiles:
                bis.append(blk + hh * 4 + jq)
            LAG = 2
            for n_, tile in enumerate(tiles):
                emitS(tile, sidx + n_)
                if n_ >= LAG:
                    emitPV(tiles[n_ - LAG], sidx + n_ - LAG, bis[n_ - LAG])
            for n_ in range(len(tiles) - LAG, len(tiles)):
                emitPV(tiles[n_], sidx + n_, bis[n_])
            sidx += len(tiles)
            blk += 8

    def ssd(self, s, l):
        P = self.P
        scr = self.scr_d
        HB = 16
        dt_all = self.carve(0, 256, F32)
        dA_all = self.carve(512, 256, F32)
        bc3 = self.carve(1024, 48, F32)
        cwT = self.carve(1120, 80, F32).rearrange("p (c j) -> p c j", c=16)
        snw = self.carve(1280, 8, F32)
        Lm = self.carve(1296, 128, F32)
        U = self.carve(1552, 128, F32)
        onesf = self.carve(1808, 128, F32)
        mask01 = self.carve(2064, 128)
        E = self.carve(2192, 48, F32)
        b_dt, b_bc3, b_cw, b_snw, b_cm, b_E = Buf("dt_all"), Buf("bc3"), Buf("cwT"), Buf("snw"), Buf("ssdconst"), Buf("E")
        self.memset(POOL, onesf, 1.0, [b_cm])
        P.op(POOL, lambda e: e.affine_select(out=Lm, in_=self.onesb[:], pattern=[[1, 128]], compare_op=ALU.is_ge, fill=0.0,
                                              base=0, channel_multiplier=-1), [self.b_const], [b_cm])
        P.op(POOL, lambda e: e.affine_select(out=mask01, in_=self.onesb[:], pattern=[[1, 128]], compare_op=ALU.is_ge, fill=0.0,
                                              base=0, channel_multiplier=-1), [self.b_const], [b_cm])
        P.op(POOL, lambda e: e.affine_select(out=U, in_=self.onesb[:], pattern=[[-1, 128]], compare_op=ALU.is_gt, fill=0.0,
                                              base=0, channel_multiplier=1), [self.b_const], [b_cm])
        P.dma(SP, out=bc3[:, 0:16], in_=self.dt_bias_d[l:l + 1].partition_broadcast(128), writes=[b_bc3])
        P.dma(SP, out=bc3[:, 16:32], in_=self.a_log_d[l:l + 1].partition_broadcast(128), writes=[b_bc3])
        P.dma(SP, out=bc3[:, 32:48], in_=self.d_skip_d[l:l + 1].partition_broadcast(128), writes=[b_bc3])
        self.act(bc3[:, 16:32], bc3[:, 16:32], AF.Exp, [b_bc3], [b_bc3])
        self.ts(DVE, bc3[:, 16:32], bc3[:, 16:32], -1.0, None, ALU.mult, None, [b_bc3], [b_bc3])
        for j in range(4):
            P.dma(SP, out=cwT[:, :, j:j + 1], in_=self.conv_w_d[l, j:j + 1, :].rearrange("o (c p) -> p c o", p=128), writes=[b_cw],
                  allow_slow_non_contiguous=True)
        P.dma(SP, out=cwT[:, :, 4:5], in_=self.conv_b_d[l:l + 1, :].rearrange("o (c p) -> p c o", p=128), writes=[b_cw],
              allow_slow_non_contiguous=True)
        P.dma(SP, out=snw, in_=self.ssm_norm_w_d[l:l + 1, :].rearrange("o (c p) -> p (o c)", p=128), writes=[b_snw],
              allow_slow_non_contiguous=True)
        wdt, bwdt = self.wload(self.w_in_cols(l, C_SDT, 16), 16)
        for g4 in range(4):
            ps, bps = self.pj()
            for u_ in range(4):
                kt = 4 * g4 + u_
                for c in range(NCH):
                    self.mm(ps[:, u_ * 16:(u_ + 1) * 16], self.hT[:, c, kt * 128:(kt + 1) * 128], wdt[:, c, 0:16],
                            c == 0, c == NCH - 1, [bwdt, self.b_hT[kt]], [bps])
            self.tt(DVE, dt_all[:, g4 * 64:(g4 + 1) * 64].rearrange("p (t h) -> p t h", t=4),
                    ps[:, 0:64].rearrange("p (t h) -> p t h", t=4), bc3[:, 0:16].unsqueeze(1).to_broadcast([128, 4, 16]), ALU.add,
                    [bps, b_bc3], [b_dt])
        self.act(dt_all, dt_all, AF.Exp, [b_dt], [b_dt])
        self.act(dt_all, dt_all, AF.Ln, [b_dt], [b_dt], bias=1.0)
        self.tt(DVE, dA_all.rearrange("p (t h) -> p t h", t=NT), dt_all.rearrange("p (t h) -> p t h", t=NT),
                bc3[:, 16:32].unsqueeze(1).to_broadcast([128, NT, 16]), ALU.mult, [b_dt, b_bc3], [b_dt])
        if getattr(self, 'ssd_level', 9) < 1:
            return
        cins = [self.carve(2304, 2080, F32), self.carve(6464, 2080, F32)]
        cout = [self.carve(10624, L), self.carve(12672, L)]
        pa = [self.carve(14720 + 1024 * i_, 512, F32) for i_ in range(4)]
        b_cin, b_cout = [Buf("cin0"), Buf("cin1")], [Buf("cout0"), Buf("cout1")]
        b_pa = [Buf("pa%d" % i_) for i_ in range(4)]
        b_scr = [Buf("scr%d" % i) for i in range(24)]
        PAD = 4
        for i_ in range(2):
            self.memset(DVE, cins[i_][:, 0:PAD], 0.0, [b_cin[i_]])
        kk = [0]
        wl = {}

        def emit_proj(cc, tb):
            if tb == 0:
                wl[cc] = self.wload(self.w_in_cols(l, C_SX + 128 * cc, 128), 128)
            w_, bw_ = wl[cc]
            cin, b_u = cins[cc % 2], b_cin[cc % 2]
            ps, bps = self.pj()
            self.proj_fm(w_, bw_, 128, tb, ps, bps)
            self.act(cin[:, PAD + tb * 512:PAD + (tb + 1) * 512], ps[:, :], AF.Copy, [bps], [b_u])

        def emit_conv(cc, tb):
            k = kk[0]
            kk[0] += 1
            cin, b_u = cins[cc % 2], b_cin[cc % 2]
            co, bco = cout[cc % 2], b_cout[cc % 2]
            a_, ba_ = self.pb[2 + k % 4], self.b_pb[2 + k % 4]
            p_, bp_ = pa[k % 4], b_pa[k % 4]
            o0 = tb * 512 + PAD - 3
            self.act(a_[:, :], cin[:, o0 + 3:o0 + 3 + 512], AF.Identity, [b_u, b_cw], [ba_], scale=cwT[:, cc, 3:4], bias=cwT[:, cc, 4:5])
            self.ts(POOL, p_, cin[:, o0 + 1:o0 + 1 + 512], cwT[:, cc, 1:2], 0.0, ALU.mult, ALU.add, [b_u, b_cw], [bp_])
            self.stt(a_[:, :], cin[:, o0 + 2:o0 + 2 + 512], cwT[:, cc, 2:3], a_[:, :], ALU.mult, ALU.add, [b_u, b_cw, ba_], [ba_])
            self.stt(a_[:, :], cin[:, o0:o0 + 512], cwT[:, cc, 0:1], a_[:, :], ALU.mult, ALU.add, [b_u, b_cw, ba_], [ba_])
            self.tt(DVE, a_[:, :], a_[:, :], p_, ALU.add, [ba_, bp_], [ba_])
            pend.append((cc, tb, a_, ba_))

        def emit_back():
            cc, tb, a_, ba_ = pend.pop(0)
            co, bco = cout[cc % 2], b_cout[cc % 2]
            self.act(co[:, tb * 512:(tb + 1) * 512], a_[:, :], AF.Silu, [ba_], [bco])
            if tb == 3:
                P.dma(SP, out=scr[cc * 128:(cc + 1) * 128, :], in_=co, reads=[bco], writes=[b_scr[cc]])

        pend = []
        for tb in range(4):
            emit_proj(0, tb)
        for cc in range(16):
            for tb in range(4):
                if cc + 1 < 16:
                    emit_proj(cc + 1, tb)
                emit_conv(cc, tb)
                if len(pend) > 2:
                    emit_back()
        while pend:
            emit_back()
        for c in range(NCH):
            w_, bw_ = self.wload(self.w_in_cols(l, C_SZ + 128 * c, 128), 128)
            co, bco = cout[c % 2], b_cout[c % 2]
            for tb in range(4):
                ps, bps = self.pj()
                self.proj_fm(w_, bw_, 128, tb, ps, bps)
                self.act(co[:, tb * 512:(tb + 1) * 512], ps[:, :], AF.Silu, [bps], [bco])
            P.dma(SP, out=scr[2048 + c * 128:2048 + (c + 1) * 128, :], in_=co, reads=[bco], writes=[b_scr[16 + c]])
        if getattr(self, 'ssd_level', 9) < 2:
            return
        P.barrier()
        xsT = self.carve(2304, 1024).rearrange("p (c t) -> p c t", c=8)
        BT = self.carve(3328, 512).rearrange("p (c t) -> p c t", c=4)
        CTs = [self.carve(3840 + 512 * i_, 512).rearrange("p (c t) -> p c t", c=4) for i_ in range(2)]
        zsT = self.carve(4864, 1024).rearrange("p (c t) -> p c t", c=8)
        rhsseg = self.carve(5888, 1024, F32).rearrange("p (h l) -> p h l", h=8)
        decs = [self.carve(7936 + 2048 * i_, 2048).rearrange("p (h l) -> p h l", h=16) for i_ in range(2)]
        CBm = self.carve(12032, 512).rearrange("p (g l) -> p g l", g=4)
        xdts = [self.carve(12544 + 1024 * i_, 1024).rearrange("p (h d) -> p h d", h=16) for i_ in range(2)]
        xdd = self.carve(14592, 1024).rearrange("p (h d) -> p h d", h=16)
        ytoks = [self.carve(15616 + 1024 * i_, 1024) for i_ in range(2)]
        y1 = self.carve(17664, 1024, F32)
        Btoks = [self.carve(19712 + 512 * i_, 512).rearrange("p (g n) -> p g n", g=4) for i_ in range(2)]
        state = self.carve(20736, 1024, F32)
        state_bf = self.carve(22784, 1024)
        sq = self.carve(23808, 256).rearrange("p (c t) -> p c t", c=2)
        rt = self.carve(24064, 128, F32)
        Es = [self.carve(24320 + 96 * i_, 48, F32) for i_ in range(2)]
        b_xs, b_B, b_z = Buf("xsT"), Buf("BT"), Buf("zsT")
        b_Cs = [Buf("CT0"), Buf("CT1")]
        b_rs, b_CBm, b_xdd, b_y1 = [Buf(n) for n in "rhsseg CBm xdd y1".split()]
        b_decs, b_xdts, b_ytoks, b_Btoks, b_Es = [[Buf("%s%d" % (n, i_)) for i_ in range(2)] for n in "dec xdt ytok Btok E".split()]
        b_state, b_sbf, b_sq, b_rt = [Buf(n) for n in "state state_bf sq rt".split()]
        pb, bpb = self.pb, self.b_pb

        def load_front_tiles(kt):
            tsl = slice(kt * 128, (kt + 1) * 128)
            P.dma(SP, out=xsT, in_=scr[0:1024, tsl].rearrange("(c p) t -> p c t", p=128), reads=b_scr[0:8], writes=[b_xs])
            P.dma(SP, out=BT, in_=scr[1024:1536, tsl].rearrange("(c p) t -> p c t", p=128), reads=b_scr[8:12], writes=[b_B])
            P.dma(SP, out=CTs[kt % 2], in_=scr[1536:2048, tsl].rearrange("(c p) t -> p c t", p=128), reads=b_scr[12:16],
                  writes=[b_Cs[kt % 2]])

        def front(kt):
            E, b_E = Es[kt % 2], b_Es[kt % 2]
            dec, b_dec = decs[kt % 2], b_decs[kt % 2]
            xdt, b_xdt = xdts[kt % 2], b_xdts[kt % 2]
            ytok, b_ytok = ytoks[kt % 2], b_ytoks[kt % 2]
            Btok, b_Btok = Btoks[kt % 2], b_Btoks[kt % 2]
            CT, b_C = CTs[kt % 2], b_Cs[kt % 2]
            dAk = dA_all[:, kt * 16:(kt + 1) * 16]
            self.mm(pb[0][:, 0:16], Lm, dAk, True, True, [b_cm, b_dt], [bpb[0]])
            self.mm(pb[0][:, 16:32], U, dAk, True, True, [b_cm, b_dt], [bpb[0]])
            self.mm(pb[0][:, 32:48], onesf, dAk, True, True, [b_cm, b_dt], [bpb[0]])
            self.act(E, pb[0][:, 0:48], AF.Exp, [bpb[0]], [b_E])
            yield
            for half in range(2):
                self.tt(POOL, rhsseg, Lm.unsqueeze(1).to_broadcast([128, 8, 128]),
                        dAk[:, 8 * half:8 * half + 8].unsqueeze(2).to_broadcast([128, 8, 128]), ALU.mult, [b_cm, b_dt], [b_rs])
                yield
                for q in range(2):
                    self.mm(pb[2 + q][:, :], U, rhsseg[:, 4 * q:4 * q + 4, :].rearrange("p h l -> p (h l)"), True, True,
                            [b_cm, b_rs], [bpb[2 + q]])
                    self.act(dec[:, 8 * half + 4 * q:8 * half + 4 * q + 4, :].rearrange("p h l -> p (h l)"), pb[2 + q][:, :],
                             AF.Exp, [bpb[2 + q]], [b_dec])
                yield
            for g in range(4):
                self.mm(pb[0][:, g * 128:(g + 1) * 128], BT[:, g, :], CT[:, g, :], True, True, [b_B, b_C], [bpb[0]])
            self.tt(DVE, CBm, pb[0][:, :].rearrange("p (g l) -> p g l", g=4), mask01.unsqueeze(1).to_broadcast([128, 4, 128]),
                    ALU.mult, [bpb[0], b_cm], [b_CBm])
            yield
            d4 = dec.rearrange("p (g j) l -> p g j l", g=4)
            self.tt(POOL, d4, d4, CBm.unsqueeze(2).to_broadcast([128, 4, 4, 128]), ALU.mult, [b_dec, b_CBm], [b_dec])
            yield
            pxb = pb[1][:].bitcast(BF16)
            for c in range(NCH):
                self.tr(pxb[:, c * 128:(c + 1) * 128], xsT[:, c, :], [b_xs], [bpb[1]])
            px3 = pxb.rearrange("p (h d) -> p h d", h=16)
            self.tt(DVE, xdt, px3, dt_all[:, kt * 16:(kt + 1) * 16].unsqueeze(2).to_broadcast([128, 16, 64]), ALU.mult,
                    [bpb[1], b_dt], [b_xdt])
            self.tt(DVE, ytok.rearrange("p (h d) -> p h d", h=16), px3, bc3[:, 32:48].unsqueeze(2).to_broadcast([128, 16, 64]),
                    ALU.mult, [bpb[1], b_bc3], [b_ytok])
            yield
            if kt < NT - 1:
                pbb = pb[0][:].bitcast(BF16)
                for g in range(4):
                    self.tr(pbb[:, g * 128:(g + 1) * 128], BT[:, g, :], [b_B], [bpb[0]])
                self.act(Btok.rearrange("p g n -> p (g n)"), pbb[:, 0:512], AF.Copy, [bpb[0]], [b_Btok])
            yield

        def back(kt):
            tsl = slice(kt * 128, (kt + 1) * 128)
            E, b_E = Es[kt % 2], b_Es[kt % 2]
            dec, b_dec = decs[kt % 2], b_decs[kt % 2]
            xdt, b_xdt = xdts[kt % 2], b_xdts[kt % 2]
            ytok, b_ytok = ytoks[kt % 2], b_ytoks[kt % 2]
            Btok, b_Btok = Btoks[kt % 2], b_Btoks[kt % 2]
            CT, b_C = CTs[kt % 2], b_Cs[kt % 2]
            P.dma(SP, out=zsT, in_=scr[2048:3072, tsl].rearrange("(c p) t -> p c t", p=128), reads=b_scr[16:24], writes=[b_z])
            if kt < NT - 1:
                self.tt(POOL, xdd, xdt, E[:, 16:32].unsqueeze(2).to_broadcast([128, 16, 64]), ALU.mult, [b_xdt, b_E], [b_xdd])
            for h in range(HB):
                self.mm(pb[4 + h // 8][:, (h % 8) * 64:(h % 8 + 1) * 64], dec[:, h, :], xdt[:, h, :], True, True,
                        [b_dec, b_xdt], [bpb[4 + h // 8]])
            yield
            if kt > 0:
                for g in range(4):
                    self.mm(pb[6 + g // 2][:, (g % 2) * 256:(g % 2 + 1) * 256], CT[:, g, :], state_bf[:, g * 256:(g + 1) * 256],
                            True, True, [b_C, b_sbf], [bpb[6 + g // 2]])
            yield
            for hb in range(2):
                hs = slice(hb * 512, (hb + 1) * 512)
                if kt > 0:
                    self.tt(DVE, y1[:, hs].rearrange("p (h d) -> p h d", h=8), pb[6 + hb][:, :].rearrange("p (h d) -> p h d", h=8),
                            E[:, 8 * hb:8 * hb + 8].unsqueeze(2).to_broadcast([128, 8, 64]), ALU.mult, [bpb[6 + hb], b_E], [b_y1])
                    self.tt(DVE, y1[:, hs], y1[:, hs], pb[4 + hb][:, :], ALU.add, [b_y1, bpb[4 + hb]], [b_y1])
                    self.tt(DVE, ytok[:, hs], y1[:, hs], ytok[:, hs], ALU.add, [b_y1, b_ytok], [b_ytok])
                else:
                    self.tt(DVE, ytok[:, hs], pb[4 + hb][:, :], ytok[:, hs], ALU.add, [bpb[4 + hb], b_ytok], [b_ytok])
                yield
            if kt < NT - 1:
                for g in range(4):
                    self.mm(pb[6 + g // 2][:, (g % 2) * 256:(g % 2 + 1) * 256], Btok[:, g, :],
                            xdd[:, 4 * g:4 * g + 4, :].rearrange("p h d -> p (h d)"), True, True, [b_Btok, b_xdd], [bpb[6 + g // 2]])
                yield
                for hb in range(2):
                    hs = slice(hb * 512, (hb + 1) * 512)
                    if kt > 0:
                        s3 = state[:, hs].rearrange("p (h d) -> p h d", h=8)
                        self.tt(POOL, s3, s3, E[:, 32 + 8 * hb:32 + 8 * hb + 8].unsqueeze(2).to_broadcast([128, 8, 64]), ALU.mult,
                                [b_state, b_E], [b_state])
                        self.tt(DVE, state[:, hs], state[:, hs], pb[6 + hb][:, :], ALU.add, [b_state, bpb[6 + hb]], [b_state])
                    else:
                        self.cp(DVE, state[:, hs], pb[6 + hb][:, :], [bpb[6 + hb]], [b_state])
                    self.cp(POOL, state_bf[:, hs], state[:, hs], [b_state], [b_sbf])
                    yield
            pyb = pb[4][:].bitcast(BF16)
            for c in range(NCH):
                self.tr(pyb[:, c * 128:(c + 1) * 128], ytok[:, c * 128:(c + 1) * 128], [b_ytok], [bpb[4]])
            self.tt(DVE, zsT, pyb.rearrange("p (c t) -> p c t", c=8), zsT, ALU.mult, [bpb[4], b_z], [b_z])
            yield
            for g in range(4):
                self.act(sq.rearrange("p c t -> p (c t)"), zsT[:, 2 * g:2 * g + 2, :].rearrange("p c t -> p (c t)"), AF.Square,
                         [b_z], [b_sq])
                self.mm(pb[5][:, 0:128], self.onesb[:], sq[:, 0, :], True, False, [self.b_const, b_sq], [bpb[5]])
                self.mm(pb[5][:, 0:128], self.onesb[:], sq[:, 1, :], False, True, [self.b_const, b_sq], [bpb[5]])
                self.act(rt, pb[5][:, 0:128], AF.Sqrt, [bpb[5]], [b_rt], scale=1.0 / 256, bias=1e-6)
                self.recip(rt, rt, [b_rt], [b_rt])
                for c2 in range(2):
                    c = 2 * g + c2
                    self.stt(self.ybr[:, c, tsl], zsT[:, c, :], snw[:, c:c + 1], rt, ALU.mult, ALU.mult, [b_z, b_snw, b_rt],
                             [self.b_ybr[c]])
                yield

        def interleave(*gens):
            gens = [g for g in gens if g is not None]
            while gens:
                for g in list(gens):
                    try:
                        next(g)
                    except StopIteration:
                        gens.remove(g)

        load_front_tiles(0)
        interleave(front(0))
        for kt in range(NT):
            nxt = None
            if kt + 1 < NT:
                load_front_tiles(kt + 1)
                nxt = front(kt + 1)
            interleave(nxt, back(kt))

    def diff(self, s, l):
        P = self.P
        lam_init = 0.8 - 0.6 * math.exp(-0.3 * l)
        cosE = self.carve(0, L)
        sinE = self.carve(2048, L)
        qT = self.carve(4096, L)
        kT = self.carve(6144, L)
        vaug = self.carve(8192, 2064).rearrange("p (t d) -> p t d", t=NT)
        sg = self.carve(10272, L)
        o = self.carve(12320, L, F32).rearrange("p (t d) -> p t d", t=NT)
        on = self.carve(16416, L).rearrange("p (t d) -> p t d", t=NT)
        pT = [self.carve(18464 + 512 * i, 512) for i in range(4)]
        xq = [self.carve(20512, 512), self.carve(21024, 512)]
        tf = [self.carve(21536, 512, F32), self.carve(22560, 512, F32)]
        pm = self.carve(23584, 128)
        b_tab = Buf("ropetab")
        b_q, b_k = Buf("dq"), Buf("dk")
        b_v = [Buf("dv%d" % i) for i in range(4)]
        b_sg = [Buf("dsg%d" % i) for i in range(4)]
        b_o = [Buf("o%d" % i) for i in range(NT)]
        b_on = [Buf("on%d" % i) for i in range(NT)]
        b_pT = [Buf("dpT%d" % i) for i in range(4)]
        b_xq = [Buf("xq0"), Buf("xq1")]
        b_tf = [Buf("tf0"), Buf("tf1")]
        b_pm = Buf("pm")
        b_rc = [Buf("drc%d" % i) for i in range(4)]
        b_lam = Buf("lam")
        b_ss2 = Buf("ss2")
        sm = self.small
        lamt = sm[:, 96:104]
        slw = sm[:, 104:106]
        ss2 = sm[:, 112:128]
        rt2 = sm[:, 128:144]
        rstd2 = sm[:, 144:160]
        P.dma(POOL, out=cosE, in_=self.rope_d[0], writes=[b_tab])
        P.dma(POOL, out=sinE, in_=self.rope_d[1], writes=[b_tab])
        self.memset(POOL, pm, 0.0, [b_pm])
        for a in range(2):
            P.op(POOL, lambda e, a=a: e.affine_select(out=pm[:, 64 * a:64 * a + 8], in_=self.onesb[:, 0:8], pattern=[[-1, 8]],
                                                      compare_op=ALU.is_equal, fill=0.0, base=-64 * a - 8, channel_multiplier=1),
                 [self.b_const], [b_pm])
            P.op(POOL, lambda e, a=a: e.affine_select(out=pm[:, 64 * a + 8:64 * a + 16], in_=self.onesb[:, 0:8], pattern=[[-1, 8]],
                                                      compare_op=ALU.is_equal, fill=0.0, base=-64 * a, channel_multiplier=1),
                 [self.b_const], [b_pm])
        dl = tf[0][:, 0:256]
        P.dma(SP, out=dl, in_=self.diff_lambda_d[l:l + 1].rearrange("o a d -> o (a d)").partition_broadcast(128), writes=[b_tf[0]])
        junk = tf[1][:, 0:64]
        self.stt(junk, dl[:, 0:64], 1.0, dl[:, 64:128], ALU.mult, ALU.mult, [b_tf[0]], [b_tf[1], b_lam], accum_out=lamt[:, 0:1])
        self.stt(junk, dl[:, 128:192], 1.0, dl[:, 192:256], ALU.mult, ALU.mult, [b_tf[0]], [b_tf[1], b_lam], accum_out=lamt[:, 1:2])
        self.act(lamt[:, 2:4], lamt[:, 0:2], AF.Exp, [b_lam], [b_lam])
        self.tt(DVE, lamt[:, 4:5], lamt[:, 2:3], lamt[:, 3:4], ALU.subtract, [b_lam], [b_lam])
        self.ts(DVE, lamt[:, 5:6], lamt[:, 4:5], -1.0, -lam_init, ALU.mult, ALU.add, [b_lam], [b_lam])
        neglam = lamt[:, 5:6]
        P.dma(SP, out=slw[:, 0:1], in_=self.subln_w_d[l:l + 1].rearrange("o p -> p o"), writes=[b_lam], allow_slow_non_contiguous=True)
        self.ts(DVE, slw[:, 0:1], slw[:, 0:1], (1.0 - lam_init) * 0.5, None, ALU.mult, None, [b_lam], [b_lam])
        for g4 in range(4):
            self.memset(DVE, vaug[:, 4 * g4:4 * g4 + 4, 128:129], 1.0, [b_v[g4]])
        pS = [self.pb[2], self.pb[3]]
        b_pS = [self.b_pb[2], self.b_pb[3]]
        pO = self.pb[4:8]
        b_pO = self.b_pb[4:8]
        sidx = 0
        k2 = 0
        for h in range(8):
            wq, bwq = self.wload(self.w_in_cols(l, C_DQ + 128 * h, 128), 128)
            wk, bwk = self.wload(self.w_in_cols(l, C_DK + 128 * h, 128), 128)
            wv, bwv = self.wload(self.w_in_cols(l, C_DV + 128 * h, 128), 128)
            wg, bwg = self.wload(self.w_in_cols(l, C_DG + 128 * h, 128), 128)
            for tb in range(4):
                sl = slice(tb * 512, (tb + 1) * 512)
                for (w_, bw_, dst, bdst, scale) in ((wq, bwq, qT, b_q, 0.125), (wk, bwk, kT, b_k, 1.0)):
                    ps, bps = self.pj()
                    self.proj_fm(w_, bw_, 128, tb, ps, bps)
                    x_, bx_ = xq[k2 % 2], b_xq[k2 % 2]
                    t1, bt1 = tf[k2 % 2], b_tf[k2 % 2]
                    k2 += 1
                    self.ts(DVE, x_, ps[:, :], scale, None, ALU.mult, None, [bps], [bx_])
                    S, bS = pS[sidx % 2], b_pS[sidx % 2]
                    sidx += 1
                    self.mm(S[:, :], pm, x_, True, True, [b_pm, bx_], [bS])
                    self.tt(DVE, t1, S[:, :], sinE[:, sl], ALU.mult, [bS, b_tab], [bt1])
                    self.tt(POOL, x_, x_, cosE[:, sl], ALU.mult, [bx_, b_tab], [bx_])
                    self.tt(DVE, dst[:, sl], t1, x_, ALU.add, [bt1, bx_], [bdst])
                ps, bps = self.pj()
                self.proj_fm(wg, bwg, 128, tb, ps, bps)
                t1, bt1 = tf[k2 % 2], b_tf[k2 % 2]
                k2 += 1
                self.act(t1, ps[:, :], AF.Tanh, [bps], [bt1], scale=0.5)
                self.stt(sg[:, sl], t1, 1.0, ps[:, :], ALU.add, ALU.mult, [bt1, bps], [b_sg[tb]])
            for g4 in range(4):
                ps, bps = self.pj()
                for u in range(4):
                    kt = 4 * g4 + u
                    for c in range(NCH):
                        self.mm(ps[:, u * 128:(u + 1) * 128], self.hT[:, c, kt * 128:(kt + 1) * 128], wv[:, c, :],
                                c == 0, c == NCH - 1, [bwv, self.b_hT[kt]], [bps])
                self.cp(DVE, vaug[:, 4 * g4:4 * g4 + 4, 0:128], ps[:, :].rearrange("p (t d) -> p t d", t=4), [bps], [b_v[g4]])
            tiles = [(comp, jq, i) for comp in range(2) for jq in range(4) for i in range(4 * jq + 4)]

            def emitS(tile, idx):
                comp, jq, i = tile
                pr = slice(64 * comp, 64 * comp + 64)
                r_ = i - 4 * jq
                qlo = max(512 * jq, 128 * i)
                w = 512 * (jq + 1) - qlo
                S, bS = pS[idx % 2], b_pS[idx % 2]
                pt, bpt = pT[idx % 4], b_pT[idx % 4]
                self.mm(S[:, 0:w], kT[pr, i * 128:(i + 1) * 128], qT[pr, qlo:qlo + w], True, r_ < 0, [b_k, b_q], [bS])
                if r_ >= 0:
                    self.mm(S[:, 0:128], self.ident[:], self.bmask[:], False, True, [self.b_const], [bS])
                self.act(pt[:, 0:w], S[:, 0:w], AF.Exp, [bS], [bpt])

            def emitPV(tile, idx):
                comp, jq, i = tile
                qlo = max(512 * jq, 128 * i)
                w = 512 * (jq + 1) - qlo
                pt, bpt = pT[idx % 4], b_pT[idx % 4]
                for cc in range(w // 128):
                    qc = qlo // 128 + cc
                    O, bO = pO[qc % 4], b_pO[qc % 4]
                    self.mm(O[:, 0:129], pt[:, cc * 128:(cc + 1) * 128], vaug[:, i, :], i == 0, i == qc,
                            [bpt, b_v[i // 4]], [bO])
                    if i == qc:
                        rcq = self.rc[:, qc:qc + 1]
                        self.recip(rcq, O[:, 128:129], [bO], [b_rc[qc % 4]])
                        if comp == 0:
                            self.ts(DVE, o[:, qc, :], O[:, 0:128], rcq, None, ALU.mult, None, [bO, b_rc[qc % 4]], [b_o[qc]])
                        else:
                            t1, bt1 = tf[qc % 2], b_tf[qc % 2]
                            self.ts(DVE, t1[:, 0:128], O[:, 0:128], rcq, neglam, ALU.mult, ALU.mult,
                                    [bO, b_rc[qc % 4], b_lam], [bt1])
                            self.tt(DVE, o[:, qc, :], t1[:, 0:128], o[:, qc, :], ALU.add, [bt1, b_o[qc]], [b_o[qc]])
                            self.stt(t1[:, 128:256], o[:, qc, :], 1.0, o[:, qc, :], ALU.mult, ALU.mult,
                                     [b_o[qc]], [bt1, b_ss2], accum_out=ss2[:, qc:qc + 1])

            for n_, tile in enumerate(tiles):
                emitS(tile, sidx + n_)
                if n_ > 0:
                    emitPV(tiles[n_ - 1], sidx + n_ - 1)
            emitPV(tiles[-1], sidx + len(tiles) - 1)
            sidx += len(tiles)
            self.act(rt2, ss2, AF.Sqrt, [b_ss2], [b_ss2], scale=1.0 / 128, bias=1e-6)
            self.recip(rstd2, rt2, [b_ss2], [b_ss2])
            for qc in range(NT):
                self.ts(POOL if qc % 2 else DVE, on[:, qc, :], o[:, qc, :], rstd2[:, qc:qc + 1], None, ALU.mult, None,
                        [b_o[qc], b_ss2], [b_on[qc]])
            for g4 in range(4):
                ps, bps = self.pj()
                psb = ps[:].bitcast(BF16)
                for u in range(4):
                    qc = 4 * g4 + u
                    self.tr(psb[:, u * 128:(u + 1) * 128], on[:, qc, :], [b_on[qc]], [bps])
                self.stt(self.ybr[:, h, g4 * 512:(g4 + 1) * 512], psb[:, 0:512], slw[:, 0:1], sg[:, g4 * 512:(g4 + 1) * 512],
                         ALU.mult, ALU.mult, [bps, b_sg[g4], b_lam], [self.b_ybr[h]])


def _rope_tables():
    pos = np.arange(L, dtype=np.float32)
    inv_freq = (np.float32(500000.0) ** (-np.arange(0, 16, 2, dtype=np.float32) / np.float32(16))).astype(np.float32)
    ang = (pos[:, None] * inv_freq[None, :]).astype(np.float32)
    cos = np.cos(ang).astype(np.float32)
    sin = np.sin(ang).astype(np.float32)
    cosE = np.ones((128, L), np.float32)
    sinE = np.zeros((128, L), np.float32)
    for m in range(128):
        i = m % 64
        if i < 8:
            cosE[m] = cos[:, i]
            sinE[m] = -sin[:, i]
        elif i < 16:
            cosE[m] = cos[:, i - 8]
            sinE[m] = sin[:, i - 8]
    return np.ascontiguousarray(np.stack([cosE, sinE]))


_NC_CACHE = {}


def kernel(x, norm_w, w_in, b_forget, conv_w, conv_b, dt_bias, a_log, d_skip, ssm_norm_w, diff_lambda, subln_w,
           w_branch, w_out, final_norm_w):
    ncores = 8
    nseq = x.shape[0] // ncores
    if 'nc' not in _NC_CACHE:
        _NC_CACHE['nc'] = Gen(nseq, 2, "abc", final=True).build()
    nc = _NC_CACHE['nc']
    f = lambda a: np.ascontiguousarray(np.asarray(a, dtype=np.float32))
    shared = dict(norm_w=f(norm_w), w_in=f(w_in), b_forget=f(b_forget), conv_w=f(conv_w), conv_b=f(conv_b),
                  dt_bias=f(dt_bias), a_log=f(a_log), d_skip=f(d_skip), ssm_norm_w=f(ssm_norm_w),
                  diff_lambda=f(diff_lambda), subln_w=f(subln_w), w_branch=f(w_branch), w_out=f(w_out),
                  final_norm_w=f(final_norm_w).reshape(1, -1), rope=_rope_tables())
    xs = f(x)
    in_maps = [dict(shared, x=np.ascontiguousarray(xs[i * nseq:(i + 1) * nseq])) for i in range(ncores)]
    res = run_bass_kernel_spmd(nc, in_maps, core_ids=list(range(ncores)))
    return np.concatenate([np.asarray(r["out"], dtype=np.float32) for r in res.results], axis=0)
```

```python
import numpy as np
import concourse.bass as bass
import concourse.mybir as mybir
from concourse.bass_utils import run_bass_kernel_spmd

F32 = mybir.dt.float32
BF16 = mybir.dt.bfloat16
AF = mybir.ActivationFunctionType
ALU = mybir.AluOpType
AX = mybir.AxisListType

PE, DVE, ACT, POOL, SP = 'tensor', 'vector', 'scalar', 'gpsimd', 'sync'
ENGS = [PE, DVE, ACT, POOL, SP]


class Buf:
    __slots__ = ('name', 'w', 'r', 'rd', 'dsem')

    def __init__(self, name):
        self.name = name
        self.w = None
        self.r = {}
        self.rd = []
        self.dsem = None


class Op:
    __slots__ = ('eng', 'fn', 'deps', 'dma_sem', 'tok', 'waited')

    def __init__(self, eng, fn):
        self.eng = eng
        self.fn = fn
        self.deps = []
        self.dma_sem = None
        self.tok = None
        self.waited = False


class DmaSem:
    def __init__(self, sem):
        self.sem = sem
        self.count = 0
        self.last = None
        self.q = None


class Prog:
    def __init__(self, nc, stack):
        self.nc = nc
        self.stack = stack
        self.ops = {e: [] for e in ENGS}
        self.esem = {}
        self.nops = 0
        self.dsems = []
        self.nsem = 0
        self.dpool = {}
        self.drr = {}

    def enter(self, cm):
        return self.stack.enter_context(cm)

    def sbuf(self, name, shape, dtype):
        return self.enter(self.nc.sbuf_tensor(name, list(shape), dtype))

    def psum(self, name, shape, dtype):
        return self.enter(self.nc.psum_tensor(name, list(shape), dtype))

    def dram(self, name, shape, dtype, kind):
        return self.nc.dram_tensor(name, list(shape), dtype, kind=kind).ap()

    def dsem(self, name=None):
        self.nsem += 1
        d = DmaSem(self.enter(self.nc.semaphore(name or ('ds%d' % self.nsem))))
        self.dsems.append(d)
        return d

    def op(self, eng, fn, reads=(), writes=(), dsem=None):
        o = Op(eng, fn)
        deps = []
        for b in reads:
            if b.w is not None:
                deps.append(b.w)
        for b in writes:
            if b.w is not None:
                deps.append(b.w)
            deps.extend(b.r.values())
            deps.extend(b.rd)
        seen = set()
        for d in deps:
            if id(d) in seen or d is o:
                continue
            seen.add(id(d))
            if d.dma_sem is None and d.eng == eng and eng == PE:
                continue
            o.deps.append(d)
        if dsem is not None:
            if dsem.last is not None and all(d is not dsem.last for d in o.deps):
                o.deps.append(dsem.last)
            o.dma_sem = dsem
            dsem.count += 16
            dsem.last = o
            o.tok = (dsem.sem, dsem.count)
        for b in reads:
            if dsem is not None:
                b.rd.append(o)
            else:
                b.r[eng] = o
        for b in writes:
            b.w = o
            b.r = {}
            b.rd = []
        self.ops[eng].append(o)
        self.nops += 1
        return o

    def dma(self, eng, out, in_, reads=(), writes=(), **kw):
        b = writes[0]
        if b.dsem is None or b.dsem.q != eng:
            pool = self.dpool.setdefault(eng, [])
            if len(pool) < 24:
                d = self.dsem()
                d.q = eng
                pool.append(d)
            else:
                d = pool[self.drr.get(eng, 0) % 24]
                self.drr[eng] = self.drr.get(eng, 0) + 1
            b.dsem = d
        return self.op(eng, lambda e: e.dma_start(out=out, in_=in_, **kw), reads=reads, writes=writes, dsem=b.dsem)

    def barrier(self):
        lasts = []
        for e in ENGS:
            if self.ops[e]:
                lasts.append(self.ops[e][-1])
        for d in self.dsems:
            if d.last is not None:
                lasts.append(d.last)
        new = []
        for e in ENGS:
            o = Op(e, lambda eng: eng.nop())
            for d in lasts:
                if d.dma_sem is None and d.eng == e:
                    continue
                o.deps.append(d)
            new.append(o)
        for o in new:
            self.ops[o.eng].append(o)

    def finish(self):
        o = Op(SP, lambda eng: eng.nop())
        for d in self.dsems:
            if d.last is not None:
                o.deps.append(d.last)
        self.ops[SP].append(o)
        self.emit()

    def emit(self):
        nc = self.nc
        for e in ENGS:
            self.esem[e] = self.enter(nc.semaphore('sem_' + e))
        for e in ENGS:
            for o in self.ops[e]:
                for d in o.deps:
                    d.waited = True
        self.nmile = {}
        for e in ENGS:
            c = 0
            for o in self.ops[e]:
                if o.dma_sem is None and o.waited:
                    c += 1
                    o.tok = (self.esem[e], c)
            self.nmile[e] = c
        prog = self

        def body(e):
            def run(eng):
                have = {}
                for o in prog.ops[e]:
                    need = {}
                    for d in o.deps:
                        s, v = d.tok
                        k = id(s)
                        if have.get(k, 0) >= v:
                            continue
                        if k not in need or need[k][1] < v:
                            need[k] = (s, v)
                    for k, (s, v) in need.items():
                        eng.wait_ge(s, v)
                        have[k] = v
                    ins = o.fn(eng)
                    if o.dma_sem is not None:
                        ins.then_inc(o.tok[0], 16)
                    elif o.waited:
                        ins.then_inc(o.tok[0], 1)
            return run

        with nc.Block() as block:
            block.tensor(body(PE))
            block.vector(body(DVE))
            block.scalar(body(ACT))
            block.gpsimd(body(POOL))
            block.sync(body(SP))


import math
import numpy as np
from contextlib import ExitStack

L = 2048
D = 1024
NT = 16
NCH = 8
N_IN = 14368
C_FQ, C_FK, C_FV, C_FF, C_FG = 0, 1024, 2048, 3072, 3088
C_SZ, C_SX, C_SB, C_SC, C_SDT = 4112, 5136, 6160, 6672, 7184
C_DQ, C_DK, C_DV, C_DG, C_MG = 7200, 8224, 9248, 10272, 11296
NEG = -30000.0
ARENA = 24576
NSLOT = 6


class Gen:
    def __init__(self, nseq, nlayers, branches, final=True, dbg=None):
        self.nseq, self.nlayers, self.branches, self.final, self.dbg = nseq, nlayers, branches, final, dbg

    def mm(self, out, lhsT, rhs, start, stop, reads, writes):
        return self.P.op(PE, lambda e: e.matmul(out, lhsT=lhsT, rhs=rhs, start=start, stop=stop), reads, writes)

    def tr(self, out, in_, reads, writes):
        ident = self.ident
        return self.P.op(PE, lambda e: e.transpose(out=out, in_=in_, identity=ident[:]), list(reads) + [self.b_const], writes)

    def act(self, out, in_, func, reads, writes, **kw):
        return self.P.op(ACT, lambda e: e.activation(out=out, in_=in_, func=func, **kw), reads, writes)

    def tt(self, eng, out, in0, in1, op, reads, writes):
        return self.P.op(eng, lambda e: e.tensor_tensor(out=out, in0=in0, in1=in1, op=op), reads, writes)

    def ts(self, eng, out, in0, s1, s2, op0, op1, reads, writes, **kw):
        if op1 is None:
            return self.P.op(eng, lambda e: e.tensor_scalar(out=out, in0=in0, scalar1=s1, scalar2=None, op0=op0, **kw), reads, writes)
        return self.P.op(eng, lambda e: e.tensor_scalar(out=out, in0=in0, scalar1=s1, scalar2=s2, op0=op0, op1=op1, **kw), reads, writes)

    def stt(self, out, in0, scalar, in1, op0, op1, reads, writes, **kw):
        return self.P.op(DVE, lambda e: e.scalar_tensor_tensor(out=out, in0=in0, scalar=scalar, in1=in1, op0=op0, op1=op1, **kw), reads, writes)

    def cp(self, eng, out, in_, reads, writes):
        return self.P.op(eng, lambda e: e.tensor_copy(out=out, in_=in_), reads, writes)

    def memset(self, eng, ap, val, writes):
        return self.P.op(eng, lambda e: e.memset(ap, val), (), writes)

    def recip(self, out, in_, reads, writes):
        return self.P.op(DVE, lambda e: e.reciprocal(out=out, in_=in_), reads, writes)

    def carve(self, off, n, dtype=BF16):
        if dtype == F32:
            return self.arena[:, off:off + 2 * n].bitcast(F32)
        return self.arena[:, off:off + n]

    def wload(self, src, n):
        i = self.wrr % NSLOT
        self.wrr += 1
        ap = self.wslot[i][:, :, 0:n]
        buf = self.b_wslot[i]
        self.P.dma(POOL, out=ap, in_=src.rearrange("(c p) n -> p c n", p=128), writes=[buf])
        return ap, buf

    def w_in_cols(self, l, c0, n):
        return self.w_in_d[l, :, c0:c0 + n]

    def pj(self):
        i = self.pjrr % 2
        self.pjrr += 1
        return self.pb[i], self.b_pb[i]

    def hbufs(self, tb):
        return self.b_hT[4 * tb:4 * tb + 4]

    def proj_fm(self, w, bw, M, tb, ps, bps, rhs_src=None, rhs_bufs=None):
        src = self.hT if rhs_src is None else rhs_src
        rb = self.hbufs(tb) if rhs_bufs is None else rhs_bufs
        for c in range(NCH):
            self.mm(ps[0:M, :], w[:, c, 0:M], src[:, c, tb * 512:(tb + 1) * 512], c == 0, c == NCH - 1,
                    [bw] + list(rb), [bps])

    def build(self):
        nc = bass.Bass("TRN2", target_bir_lowering=False)
        self.nc = nc
        nseq = self.nseq
        di = lambda name, shape: nc.dram_tensor(name, list(shape), F32, kind="ExternalInput").ap()
        self.x_d = di("x", [nseq, L, D])
        self.norm_w_d = di("norm_w", [2, D])
        self.w_in_d = di("w_in", [2, D, N_IN])
        self.b_forget_d = di("b_forget", [2, 16])
        self.conv_w_d = di("conv_w", [2, 4, 2048])
        self.conv_b_d = di("conv_b", [2, 2048])
        self.dt_bias_d = di("dt_bias", [2, 16])
        self.a_log_d = di("a_log", [2, 16])
        self.d_skip_d = di("d_skip", [2, 16])
        self.ssm_norm_w_d = di("ssm_norm_w", [2, D])
        self.diff_lambda_d = di("diff_lambda", [2, 4, 64])
        self.subln_w_d = di("subln_w", [2, 128])
        self.w_branch_d = di("w_branch", [2, 3, D, D])
        self.w_out_d = di("w_out", [2, D, D])
        self.final_norm_w_d = di("final_norm_w", [1, D])
        self.rope_d = nc.dram_tensor("rope_scr", [2, 128, L], F32, kind="Internal").ap()
        self.out_d = nc.dram_tensor("out", [nseq, L, D], F32, kind="ExternalOutput").ap()
        self.gk_d = nc.dram_tensor("gk_scr", [16, 6, L], BF16, kind="Internal").ap()
        self.scr_d = nc.dram_tensor("ssd_scr", [3072, L], BF16, kind="ExternalOutput").ap()
        if self.dbg:
            self.dbg_d = nc.dram_tensor("dbg", [128, 8, L], BF16, kind="ExternalOutput").ap()
        with ExitStack() as st:
            P = Prog(nc, st)
            self.P = P
            self.xres = P.sbuf("xres", [128, NT, D], F32)
            self.hT = P.sbuf("hT", [128, NCH, L], BF16)
            self.ybr = P.sbuf("ybr", [128, NCH, L], BF16)
            self.wout = P.sbuf("wout", [128, NCH, D], BF16)
            self.arena = P.sbuf("arena", [128, ARENA], BF16)
            self.wslot = [P.sbuf("wslot%d" % i, [128, NCH, 128], BF16) for i in range(NSLOT)]
            self.ident = P.sbuf("ident", [128, 128], BF16)
            self.cmask = P.sbuf("cmask", [128, 128], BF16)
            self.bmask = P.sbuf("bmask", [128, 128], BF16)
            self.onesb = P.sbuf("onesb", [128, 128], BF16)
            self.zerob = P.sbuf("zerob", [128, 128], BF16)
            self.small = P.sbuf("small", [128, 256], F32)
            self.pb = [P.psum("pb%d" % i, [128, 512], F32) for i in range(8)]
            self.b_pb = [Buf("pb%d" % i) for i in range(8)]
            self.b_x = [Buf("x%d" % t) for t in range(NT)]
            self.b_hT = [Buf("hT%d" % t) for t in range(NT)]
            self.b_ybr = [Buf("ybr%d" % c) for c in range(NCH)]
            self.b_wout = Buf("wout")
            self.b_wslot = [Buf("ws%d" % i) for i in range(NSLOT)]
            self.b_const = Buf("const")
            self.b_small = Buf("small")
            self.b_gk = Buf("gk")
            self.b_out = [Buf("out%d" % t) for t in range(NT)]
            self.wrr = 0
            self.pjrr = 0
            sm = self.small
            self.nwT = sm[:, 0:16]
            self.negbf = sm[0:16, 16:18]
            self.ss = sm[:, 32:48]
            self.rt = sm[:, 48:64]
            self.rstd = sm[:, 64:80]
            self.rc = sm[:, 80:96]
            self.consts()
            self.rope_tables()
            for s in range(nseq):
                for t in range(NT):
                    P.dma(SP, out=self.xres[:, t, :], in_=self.x_d[s, t * 128:(t + 1) * 128, :], writes=[self.b_x[t]])
                for l in range(self.nlayers):
                    self.layer(s, l)
                self.final_out(s)
            P.finish()
        return nc

    def consts(self):
        P = self.P
        bc = self.b_const
        self.memset(POOL, self.onesb[:], 1.0, [bc])
        self.memset(POOL, self.zerob[:], 0.0, [bc])
        P.op(POOL, lambda e: e.affine_select(out=self.ident[:], in_=self.onesb[:], pattern=[[-1, 128]], compare_op=ALU.is_equal,
                                              fill=0.0, base=0, channel_multiplier=1), [bc], [bc])
        P.op(POOL, lambda e: e.affine_select(out=self.cmask[:], in_=self.zerob[:], pattern=[[1, 128]], compare_op=ALU.is_ge,
                                              fill=NEG, base=0, channel_multiplier=-1), [bc], [bc])
        self.memset(POOL, self.bmask[:], 0.0, [bc])
        self.memset(POOL, self.bmask[64:128, 0:64], NEG, [bc])
        P.dma(SP, out=self.nwT.rearrange("p (l c) -> p l c", l=2), in_=self.norm_w_d.rearrange("l (c p) -> p l c", p=128),
              writes=[self.b_small], allow_slow_non_contiguous=True)
        P.dma(SP, out=self.negbf, in_=self.b_forget_d.rearrange("l h -> h l"), writes=[self.b_small], allow_slow_non_contiguous=True)
        self.ts(DVE, self.negbf, self.negbf, -1.0, None, ALU.mult, None, [self.b_small], [self.b_small])


    def rope_tables(self):
        P = self.P
        I32 = mybir.dt.int32
        TWO_PI = 2.0 * math.pi
        C1 = 6.28125
        C2 = TWO_PI - C1
        MAGIC = 12582912.0
        wrow = self.carve(0, 128, F32)[0:1, :]
        one2 = self.carve(256, 2, F32)[0:1, :]
        wcol = self.small[:, 160:161]
        pos_i = self.arena[:, 4096:8192].bitcast(I32)
        pos = self.carve(8192, L, F32)
        ang = self.carve(12288, L, F32)
        kr = self.carve(16384, L, F32)
        r = self.carve(20480, L, F32)
        b_w, b_pi, b_pos, b_ang, b_kr, b_r = [Buf(n) for n in "wrow posi pos ang kr r".split()]
        self.memset(DVE, wrow, 0.0, [b_w])
        self.memset(DVE, one2, 1.0, [b_w])
        for m in range(128):
            i = m % 64
            if i < 16:
                f = i % 8
                wf = float(np.float32(500000.0) ** np.float32(-(2.0 * f) / 16.0))
                self.memset(DVE, wrow[:, m:m + 1], -wf if i < 8 else wf, [b_w])
        ps, bps = self.pj()
        self.mm(ps[:, 0:2], wrow, one2, True, True, [b_w], [bps])
        self.cp(DVE, wcol, ps[:, 0:1], [bps], [self.b_small])
        P.op(POOL, lambda e: e.iota(pos_i, pattern=[[1, L]], base=0, channel_multiplier=0), (), [b_pi])
        self.cp(DVE, pos, pos_i, [b_pi], [b_pos])
        self.ts(DVE, ang, pos, wcol, None, ALU.mult, None, [b_pos, self.b_small], [b_ang])
        self.ts(DVE, kr, ang, 1.0 / TWO_PI, MAGIC, ALU.mult, ALU.add, [b_ang], [b_kr])
        self.ts(DVE, kr, kr, -MAGIC, None, ALU.add, None, [b_kr], [b_kr])
        self.stt(r, kr, -C1, ang, ALU.mult, ALU.add, [b_kr, b_ang], [b_r])
        self.stt(r, kr, -C2, r, ALU.mult, ALU.add, [b_kr, b_r], [b_r])
        self.ts(DVE, r, r, 3.1415925, -3.1415925, ALU.min, ALU.max, [b_r], [b_r])
        b_rope = Buf("rope_scr")
        self.b_rope = b_rope
        self.act(ang, r, AF.Sin, [b_r], [b_ang])
        self.act(kr, r, AF.Abs, [b_r], [b_kr])
        self.act(pos, kr, AF.Sin, [b_kr], [b_pos], scale=-1.0, bias=math.pi / 2.0)
        P.dma(SP, out=self.rope_d[0], in_=pos, reads=[b_pos], writes=[b_rope])
        P.dma(SP, out=self.rope_d[1], in_=ang, reads=[b_ang], writes=[b_rope])
        P.barrier()

    def layer(self, s, l):
        P = self.P
        for eb in range(2):
            P.dma(POOL, out=self.wout[:, :, eb * 512:(eb + 1) * 512],
                  in_=self.w_out_d[l, :, eb * 512:(eb + 1) * 512].rearrange("(c p) n -> p c n", p=128), writes=[self.b_wout])
        P.barrier()
        self.rmsnorm_T(l)
        for n, br in enumerate("abc"):
            if br not in self.branches:
                continue
            P.barrier()
            if br == 'a':
                self.fox(s, l)
            elif br == 'b':
                self.ssd(s, l)
            else:
                self.diff(s, l)
            if self.dbg == br and l == self.nlayers - 1:
                for c in range(NCH):
                    P.dma(SP, out=self.dbg_d[:, c, :], in_=self.ybr[:, c, :], reads=[self.b_ybr[c]], writes=[Buf("dbg%d" % c)])
            P.barrier()
            self.merge(l, n)

    def rmsnorm_T(self, l):
        junk = self.carve(0, 1024)
        hn = [self.carve(1024, 1024), self.carve(2048, 1024)]
        b_junk = Buf("junk")
        b_hn = [Buf("hn0"), Buf("hn1")]
        b_ss = Buf("ss")
        for t in range(NT):
            self.act(junk, self.xres[:, t, :], AF.Square, [self.b_x[t]], [b_junk, b_ss], accum_out=self.ss[:, t:t + 1])
        self.act(self.rt, self.ss, AF.Sqrt, [b_ss], [b_ss], scale=1.0 / D, bias=1e-6)
        self.recip(self.rstd, self.rt, [b_ss], [b_ss])
        for t in range(NT):
            k = t % 2
            self.act(hn[k], self.xres[:, t, :], AF.Copy, [self.b_x[t], b_ss], [b_hn[k]], scale=self.rstd[:, t:t + 1])
            ps, bps = self.pj()
            psb = ps[:].bitcast(BF16)
            for c in range(NCH):
                self.tr(psb[:, c * 128:(c + 1) * 128], hn[k][:, c * 128:(c + 1) * 128], [b_hn[k]], [bps])
            self.tt(DVE, self.hT[:, :, t * 128:(t + 1) * 128], psb.rearrange("p (c n) -> p c n", c=NCH),
                    self.nwT[:, l * 8:(l + 1) * 8].unsqueeze(2).to_broadcast([128, NCH, 128]), ALU.mult,
                    [bps, self.b_small], [self.b_hT[t]])

    def merge(self, l, n):
        gp = self.carve(0, NCH * L).rearrange("p (c t) -> p c t", c=NCH)
        th = [self.carve(NCH * L, 512, F32), self.carve(NCH * L + 1024, 512, F32)]
        b_gp = [Buf("gp%d" % t) for t in range(4)]
        b_th = [Buf("th0"), Buf("th1")]
        k = 0
        for co in range(NCH):
            wb, bwb = self.wload(self.w_branch_d[l, n, :, co * 128:(co + 1) * 128], 128)
            wg, bwg = self.wload(self.w_in_cols(l, C_MG + n * 1024 + co * 128, 128), 128)
            for tb in range(4):
                ps1, bps1 = self.pj()
                self.proj_fm(wb, bwb, 128, tb, ps1, bps1, rhs_src=self.ybr, rhs_bufs=self.b_ybr)
                ps2, bps2 = self.pb[2 + k % 2], self.b_pb[2 + k % 2]
                self.proj_fm(wg, bwg, 128, tb, ps2, bps2)
                t_ = th[k % 2]
                self.act(t_, ps2[:, :], AF.Tanh, [bps2], [b_th[k % 2]], scale=0.5)
                self.stt(gp[:, co, tb * 512:(tb + 1) * 512], t_, 1.0, ps1[:, :], ALU.add, ALU.mult,
                         [b_th[k % 2], bps1], [b_gp[tb]])
                k += 1
        for t in range(NT):
            for eb in range(2):
                ps, bps = self.pb[4 + (2 * t + eb) % 4], self.b_pb[4 + (2 * t + eb) % 4]
                for co in range(NCH):
                    self.mm(ps[:, :], gp[:, co, t * 128:(t + 1) * 128], self.wout[:, co, eb * 512:(eb + 1) * 512],
                            co == 0, co == NCH - 1, [b_gp[t // 4], self.b_wout], [bps])
                xs = self.xres[:, t, eb * 512:(eb + 1) * 512]
                self.stt(xs, ps[:, :], 0.5, xs, ALU.mult, ALU.add, [bps, self.b_x[t]], [self.b_x[t]])

    def final_out(self, s):
        P = self.P
        P.barrier()
        b_ss = Buf("fss")
        if self.final:
            junk = self.carve(0, 1024)
            fnw = self.carve(1024, 1024, F32)
            b_junk, b_fnw = Buf("fjunk"), Buf("fnw")
            P.dma(SP, out=fnw, in_=self.final_norm_w_d.partition_broadcast(128), writes=[b_fnw])
            for t in range(NT):
                self.act(junk, self.xres[:, t, :], AF.Square, [self.b_x[t]], [b_junk, b_ss], accum_out=self.ss[:, t:t + 1])
            self.act(self.rt, self.ss, AF.Sqrt, [b_ss], [b_ss], scale=1.0 / D, bias=1e-6)
            self.recip(self.rstd, self.rt, [b_ss], [b_ss])
            for t in range(NT):
                xs = self.xres[:, t, :]
                self.stt(xs, xs, self.rstd[:, t:t + 1], fnw, ALU.mult, ALU.mult, [self.b_x[t], b_ss, b_fnw], [self.b_x[t]])
        for t in range(NT):
            P.dma(SP, out=self.out_d[s, t * 128:(t + 1) * 128, :], in_=self.xres[:, t, :], reads=[self.b_x[t]], writes=[self.b_out[t]])

    def fox(self, s, l):
        P = self.P
        lf = self.carve(0, L, F32)[0:16, :]
        G = self.carve(4096, L, F32)[0:16, :]
        r = self.carve(8192, L, F32)[0:16, :]
        parts = [self.carve(12288 + 2048 * j, L)[0:16, :] for j in range(6)]
        b_lf, b_G, b_r = Buf("lf"), Buf("G"), Buf("r")
        b_parts = [Buf("part%d" % j) for j in range(6)]
        wff, bwff = self.wload(self.w_in_cols(l, C_FF, 16), 16)
        for tb in range(4):
            ps, bps = self.pj()
            self.proj_fm(wff, bwff, 16, tb, ps, bps)
            self.act(lf[:, tb * 512:(tb + 1) * 512], ps[0:16, :], AF.Exp, [bps, self.b_small], [b_lf],
                     scale=-1.0, bias=self.negbf[:, l:l + 1])
        self.act(lf, lf, AF.Ln, [b_lf], [b_lf], bias=1.0)
        P.op(DVE, lambda e: e.tensor_tensor_scan(out=G, data0=lf, data1=lf, initial=0.0, op0=ALU.add, op1=ALU.max), [b_lf], [b_G])
        self.cp(DVE, parts[0], G, [b_G], [b_parts[0]])
        self.tt(DVE, r, G, parts[0], ALU.subtract, [b_G, b_parts[0]], [b_r])
        self.cp(DVE, parts[1], r, [b_r], [b_parts[1]])
        self.tt(DVE, r, r, parts[1], ALU.subtract, [b_r, b_parts[1]], [b_r])
        self.cp(DVE, parts[2], r, [b_r], [b_parts[2]])
        for j in range(3):
            self.ts(DVE, parts[3 + j], parts[j], -1.0, None, ALU.mult, None, [b_parts[j]], [b_parts[3 + j]])
        for j in range(6):
            P.dma(SP, out=self.gk_d[:, j, :], in_=parts[j], reads=[b_parts[j]], writes=[self.b_gk])
        P.barrier()
        qaug = [self.carve(0, L), self.carve(2048, L)]
        kaug = [self.carve(4096, L), self.carve(6144, L)]
        vaug = self.carve(8192, 4096).rearrange("p (t h d) -> p t h d", t=NT, h=2)
        sg = self.carve(12288, L)
        pT = [self.carve(14336 + 512 * i, 512) for i in range(4)]
        th = [self.carve(16384, 512, F32), self.carve(17408, 512, F32)]
        rinv = [self.carve(18432, 512, F32), self.carve(19456, 512, F32)]
        tmpn = [self.carve(20480, 512, F32), self.carve(21504, 512, F32)]
        b_q = [Buf("qaug0"), Buf("qaug1")]
        b_k = [Buf("kaug0"), Buf("kaug1")]
        b_v = [Buf("vaug%d" % i) for i in range(4)]
        b_sg = [Buf("sg%d" % i) for i in range(4)]
        b_pT = [Buf("pT%d" % i) for i in range(4)]
        b_th = [Buf("fth0"), Buf("fth1")]
        b_rinv = [Buf("rinv0"), Buf("rinv1")]
        b_tmpn = [Buf("tmpn0"), Buf("tmpn1")]
        for hh in range(2):
            self.memset(DVE, kaug[hh][64:67, :], 1.0, [b_k[hh]])
            self.memset(DVE, qaug[hh][64:67, :], 1.0, [b_q[hh]])
            P.dma(SP, out=qaug[hh][67:70, :], in_=qaug[hh][64:67, :], reads=[b_q[hh]], writes=[b_q[hh]])
        for g4 in range(4):
            self.memset(DVE, vaug[:, 4 * g4:4 * g4 + 4, 0, 64:128], 1.0, [b_v[g4]])
            self.memset(DVE, vaug[:, 4 * g4:4 * g4 + 4, 1, 0:64], 1.0, [b_v[g4]])
        pS = self.pb[2:6]
        b_pS = self.b_pb[2:6]
        pO = self.pb[6:8]
        b_pO = self.b_pb[6:8]
        sidx = 0
        thk = 0
        blk = 0
        for j in range(8):
            wq, bwq = self.wload(self.w_in_cols(l, C_FQ + 128 * j, 128), 128)
            wk, bwk = self.wload(self.w_in_cols(l, C_FK + 128 * j, 128), 128)
            wv, bwv = self.wload(self.w_in_cols(l, C_FV + 128 * j, 128), 128)
            wg, bwg = self.wload(self.w_in_cols(l, C_FG + 128 * j, 128), 128)
            for hh in range(2):
                h = 2 * j + hh
                P.dma(SP, out=kaug[hh][67:70, :], in_=self.gk_d[h, 0:3, :], reads=[self.b_gk], writes=[b_k[hh]])
                P.dma(SP, out=qaug[hh][64:67, :], in_=self.gk_d[h, 3:6, :], reads=[self.b_gk], writes=[b_q[hh]])
            for tb in range(4):
                sl = slice(tb * 512, (tb + 1) * 512)
                ps, bps = self.pj()
                self.proj_fm(wq, bwq, 128, tb, ps, bps)
                for hh in range(2):
                    self.ts(DVE, qaug[hh][0:64, sl], ps[64 * hh:64 * hh + 64, :], 0.125, None, ALU.mult, None, [bps], [b_q[hh]])
                ps, bps = self.pj()
                self.proj_fm(wk, bwk, 128, tb, ps, bps)
                for hh in range(2):
                    self.cp(DVE, kaug[hh][0:64, sl], ps[64 * hh:64 * hh + 64, :], [bps], [b_k[hh]])
                ps, bps = self.pj()
                self.proj_fm(wg, bwg, 128, tb, ps, bps)
                t_ = th[thk % 2]
                self.act(t_, ps[:, :], AF.Tanh, [bps], [b_th[thk % 2]], scale=0.5)
                self.stt(sg[:, sl], t_, 1.0, ps[:, :], ALU.add, ALU.mult, [b_th[thk % 2], bps], [b_sg[tb]])
                thk += 1
            for g4 in range(4):
                ps, bps = self.pj()
                for u in range(4):
                    kt = 4 * g4 + u
                    for c in range(NCH):
                        self.mm(ps[:, u * 128:(u + 1) * 128], self.hT[:, c, kt * 128:(kt + 1) * 128], wv[:, c, :],
                                c == 0, c == NCH - 1, [bwv, self.b_hT[kt]], [bps])
                ps4 = ps[:, :].rearrange("p (t h d) -> p t h d", t=4, h=2)
                self.cp(DVE, vaug[:, 4 * g4:4 * g4 + 4, 0, 0:64], ps4[:, :, 0, :], [bps], [b_v[g4]])
                self.cp(DVE, vaug[:, 4 * g4:4 * g4 + 4, 1, 64:128], ps4[:, :, 1, :], [bps], [b_v[g4]])
            tiles = [(hh, jq, i) for hh in range(2) for jq in range(4) for i in range(4 * jq + 4)]

            def emitS(tile, idx):
                hh, jq, i = tile
                r_ = i - 4 * jq
                qlo = max(512 * jq, 128 * i)
                w = 512 * (jq + 1) - qlo
                S, bS = pS[idx % 4], b_pS[idx % 4]
                pt, bpt = pT[idx % 4], b_pT[idx % 4]
                self.mm(S[:, 0:w], kaug[hh][0:70, i * 128:(i + 1) * 128], qaug[hh][0:70, qlo:qlo + w],
                        True, r_ < 0, [b_k[hh], b_q[hh]], [bS])
                if r_ >= 0:
                    self.mm(S[:, 0:128], self.ident[:], self.cmask[:], False, True, [self.b_const], [bS])
                self.act(pt[:, 0:w], S[:, 0:w], AF.Exp, [bS], [bpt])

            def emitPV(tile, idx, bi):
                hh, jq, i = tile
                qlo = max(512 * jq, 128 * i)
                w = 512 * (jq + 1) - qlo
                qoff = qlo - 512 * jq
                pt, bpt = pT[idx % 4], b_pT[idx % 4]
                O, bO = pO[bi % 2], b_pO[bi % 2]
                last = (i == 4 * jq + 3)
                self.mm(O[:, qoff:qoff + w], vaug[:, i, hh, :], pt[:, 0:w], i == 0, last, [bpt, b_v[i // 4]], [bO])
                if last:
                    pr = slice(0, 64) if hh == 0 else slice(64, 128)
                    sr = slice(64, 128) if hh == 0 else slice(0, 64)
                    ri, bri = rinv[bi % 2], b_rinv[bi % 2]
                    tm, btm = tmpn[bi % 2], b_tmpn[bi % 2]
                    self.recip(ri[pr, :], O[sr, :], [bO], [bri])
                    self.tt(DVE, tm[pr, :], O[pr, :], ri[pr, :], ALU.mult, [bO, bri], [btm])
                    qs = slice(jq * 512, (jq + 1) * 512)
                    self.stt(self.ybr[pr, j, qs], tm[pr, :], 0.5, sg[pr, qs], ALU.mult, ALU.mult, [btm, b_sg[jq]], [self.b_ybr[j]])

            bis = []
            for (hh, jq, i) in tiles:
                bis.append(blk + hh * 4 + jq)
            LAG = 2
            for n_, tile in enumerate(tiles):
                emitS(tile, sidx + n_)
                if n_ >= LAG:
                    emitPV(tiles[n_ - LAG], sidx + n_ - LAG, bis[n_ - LAG])
            for n_ in range(len(tiles) - LAG, len(tiles)):
                emitPV(tiles[n_], sidx + n_, bis[n_])
            sidx += len(tiles)
            blk += 8

    def ssd(self, s, l):
        P = self.P
        scr = self.scr_d
        HB = 16
        dt_all = self.carve(0, 256, F32)
        dA_all = self.carve(512, 256, F32)
        bc3 = self.carve(1024, 48, F32)
        cwT = self.carve(1120, 80, F32).rearrange("p (c j) -> p c j", c=16)
        snw = self.carve(1280, 8, F32)
        Lm = self.carve(1296, 128, F32)
        U = self.carve(1552, 128, F32)
        onesf = self.carve(1808, 128, F32)
        mask01 = self.carve(2064, 128)
        E = self.carve(2192, 48, F32)
        b_dt, b_bc3, b_cw, b_snw, b_cm, b_E = Buf("dt_all"), Buf("bc3"), Buf("cwT"), Buf("snw"), Buf("ssdconst"), Buf("E")
        self.memset(POOL, onesf, 1.0, [b_cm])
        P.op(POOL, lambda e: e.affine_select(out=Lm, in_=self.onesb[:], pattern=[[1, 128]], compare_op=ALU.is_ge, fill=0.0,
                                              base=0, channel_multiplier=-1), [self.b_const], [b_cm])
        P.op(POOL, lambda e: e.affine_select(out=mask01, in_=self.onesb[:], pattern=[[1, 128]], compare_op=ALU.is_ge, fill=0.0,
                                              base=0, channel_multiplier=-1), [self.b_const], [b_cm])
        P.op(POOL, lambda e: e.affine_select(out=U, in_=self.onesb[:], pattern=[[-1, 128]], compare_op=ALU.is_gt, fill=0.0,
                                              base=0, channel_multiplier=1), [self.b_const], [b_cm])
        P.dma(SP, out=bc3[:, 0:16], in_=self.dt_bias_d[l:l + 1].partition_broadcast(128), writes=[b_bc3])
        P.dma(SP, out=bc3[:, 16:32], in_=self.a_log_d[l:l + 1].partition_broadcast(128), writes=[b_bc3])
        P.dma(SP, out=bc3[:, 32:48], in_=self.d_skip_d[l:l + 1].partition_broadcast(128), writes=[b_bc3])
        self.act(bc3[:, 16:32], bc3[:, 16:32], AF.Exp, [b_bc3], [b_bc3])
        self.ts(DVE, bc3[:, 16:32], bc3[:, 16:32], -1.0, None, ALU.mult, None, [b_bc3], [b_bc3])
        for j in range(4):
            P.dma(SP, out=cwT[:, :, j:j + 1], in_=self.conv_w_d[l, j:j + 1, :].rearrange("o (c p) -> p c o", p=128), writes=[b_cw],
                  allow_slow_non_contiguous=True)
        P.dma(SP, out=cwT[:, :, 4:5], in_=self.conv_b_d[l:l + 1, :].rearrange("o (c p) -> p c o", p=128), writes=[b_cw],
              allow_slow_non_contiguous=True)
        P.dma(SP, out=snw, in_=self.ssm_norm_w_d[l:l + 1, :].rearrange("o (c p) -> p (o c)", p=128), writes=[b_snw],
              allow_slow_non_contiguous=True)
        wdt, bwdt = self.wload(self.w_in_cols(l, C_SDT, 16), 16)
        for g4 in range(4):
            ps, bps = self.pj()
            for u_ in range(4):
                kt = 4 * g4 + u_
                for c in range(NCH):
                    self.mm(ps[:, u_ * 16:(u_ + 1) * 16], self.hT[:, c, kt * 128:(kt + 1) * 128], wdt[:, c, 0:16],
                            c == 0, c == NCH - 1, [bwdt, self.b_hT[kt]], [bps])
            self.tt(DVE, dt_all[:, g4 * 64:(g4 + 1) * 64].rearrange("p (t h) -> p t h", t=4),
                    ps[:, 0:64].rearrange("p (t h) -> p t h", t=4), bc3[:, 0:16].unsqueeze(1).to_broadcast([128, 4, 16]), ALU.add,
                    [bps, b_bc3], [b_dt])
        self.act(dt_all, dt_all, AF.Exp, [b_dt], [b_dt])
        self.act(dt_all, dt_all, AF.Ln, [b_dt], [b_dt], bias=1.0)
        self.tt(DVE, dA_all.rearrange("p (t h) -> p t h", t=NT), dt_all.rearrange("p (t h) -> p t h", t=NT),
                bc3[:, 16:32].unsqueeze(1).to_broadcast([128, NT, 16]), ALU.mult, [b_dt, b_bc3], [b_dt])
        if getattr(self, 'ssd_level', 9) < 1:
            return
        cins = [self.carve(2304, 2080, F32), self.carve(6464, 2080, F32)]
        cout = [self.carve(10624, L), self.carve(12672, L)]
        pa = [self.carve(14720 + 1024 * i_, 512, F32) for i_ in range(4)]
        b_cin, b_cout = [Buf("cin0"), Buf("cin1")], [Buf("cout0"), Buf("cout1")]
        b_pa = [Buf("pa%d" % i_) for i_ in range(4)]
        b_scr = [Buf("scr%d" % i) for i in range(24)]
        PAD = 4
        for i_ in range(2):
            self.memset(DVE, cins[i_][:, 0:PAD], 0.0, [b_cin[i_]])
        kk = [0]
        wl = {}

        def emit_proj(cc, tb):
            if tb == 0:
                wl[cc] = self.wload(self.w_in_cols(l, C_SX + 128 * cc, 128), 128)
            w_, bw_ = wl[cc]
            cin, b_u = cins[cc % 2], b_cin[cc % 2]
            ps, bps = self.pj()
            self.proj_fm(w_, bw_, 128, tb, ps, bps)
            self.act(cin[:, PAD + tb * 512:PAD + (tb + 1) * 512], ps[:, :], AF.Copy, [bps], [b_u])

        def emit_conv(cc, tb):
            k = kk[0]
            kk[0] += 1
            cin, b_u = cins[cc % 2], b_cin[cc % 2]
            co, bco = cout[cc % 2], b_cout[cc % 2]
            a_, ba_ = self.pb[2 + k % 4], self.b_pb[2 + k % 4]
            p_, bp_ = pa[k % 4], b_pa[k % 4]
            o0 = tb * 512 + PAD - 3
            self.act(a_[:, :], cin[:, o0 + 3:o0 + 3 + 512], AF.Identity, [b_u, b_cw], [ba_], scale=cwT[:, cc, 3:4], bias=cwT[:, cc, 4:5])
            self.ts(POOL, p_, cin[:, o0 + 1:o0 + 1 + 512], cwT[:, cc, 1:2], 0.0, ALU.mult, ALU.add, [b_u, b_cw], [bp_])
            self.stt(a_[:, :], cin[:, o0 + 2:o0 + 2 + 512], cwT[:, cc, 2:3], a_[:, :], ALU.mult, ALU.add, [b_u, b_cw, ba_], [ba_])
            self.stt(a_[:, :], cin[:, o0:o0 + 512], cwT[:, cc, 0:1], a_[:, :], ALU.mult, ALU.add, [b_u, b_cw, ba_], [ba_])
            self.tt(DVE, a_[:, :], a_[:, :], p_, ALU.add, [ba_, bp_], [ba_])
            pend.append((cc, tb, a_, ba_))

        def emit_back():
            cc, tb, a_, ba_ = pend.pop(0)
            co, bco = cout[cc % 2], b_cout[cc % 2]
            self.act(co[:, tb * 512:(tb + 1) * 512], a_[:, :], AF.Silu, [ba_], [bco])
            if tb == 3:
                P.dma(SP, out=scr[cc * 128:(cc + 1) * 128, :], in_=co, reads=[bco], writes=[b_scr[cc]])

        pend = []
        for tb in range(4):
            emit_proj(0, tb)
        for cc in range(16):
            for tb in range(4):
                if cc + 1 < 16:
                    emit_proj(cc + 1, tb)
                emit_conv(cc, tb)
                if len(pend) > 2:
                    emit_back()
        while pend:
            emit_back()
        for c in range(NCH):
            w_, bw_ = self.wload(self.w_in_cols(l, C_SZ + 128 * c, 128), 128)
            co, bco = cout[c % 2], b_cout[c % 2]
            for tb in range(4):
                ps, bps = self.pj()
                self.proj_fm(w_, bw_, 128, tb, ps, bps)
                self.act(co[:, tb * 512:(tb + 1) * 512], ps[:, :], AF.Silu, [bps], [bco])
            P.dma(SP, out=scr[2048 + c * 128:2048 + (c + 1) * 128, :], in_=co, reads=[bco], writes=[b_scr[16 + c]])
        if getattr(self, 'ssd_level', 9) < 2:
            return
        P.barrier()
        xsT = self.carve(2304, 1024).rearrange("p (c t) -> p c t", c=8)
        BT = self.carve(3328, 512).rearrange("p (c t) -> p c t", c=4)
        CTs = [self.carve(3840 + 512 * i_, 512).rearrange("p (c t) -> p c t", c=4) for i_ in range(2)]
        zsT = self.carve(4864, 1024).rearrange("p (c t) -> p c t", c=8)
        rhsseg = self.carve(5888, 1024, F32).rearrange("p (h l) -> p h l", h=8)
        decs = [self.carve(7936 + 2048 * i_, 2048).rearrange("p (h l) -> p h l", h=16) for i_ in range(2)]
        CBm = self.carve(12032, 512).rearrange("p (g l) -> p g l", g=4)
        xdts = [self.carve(12544 + 1024 * i_, 1024).rearrange("p (h d) -> p h d", h=16) for i_ in range(2)]
        xdd = self.carve(14592, 1024).rearrange("p (h d) -> p h d", h=16)
        ytoks = [self.carve(15616 + 1024 * i_, 1024) for i_ in range(2)]
        y1 = self.carve(17664, 1024, F32)
        Btoks = [self.carve(19712 + 512 * i_, 512).rearrange("p (g n) -> p g n", g=4) for i_ in range(2)]
        state = self.carve(20736, 1024, F32)
        state_bf = self.carve(22784, 1024)
        sq = self.carve(23808, 256).rearrange("p (c t) -> p c t", c=2)
        rt = self.carve(24064, 128, F32)
        Es = [self.carve(24320 + 96 * i_, 48, F32) for i_ in range(2)]
        b_xs, b_B, b_z = Buf("xsT"), Buf("BT"), Buf("zsT")
        b_Cs = [Buf("CT0"), Buf("CT1")]
        b_rs, b_CBm, b_xdd, b_y1 = [Buf(n) for n in "rhsseg CBm xdd y1".split()]
        b_decs, b_xdts, b_ytoks, b_Btoks, b_Es = [[Buf("%s%d" % (n, i_)) for i_ in range(2)] for n in "dec xdt ytok Btok E".split()]
        b_state, b_sbf, b_sq, b_rt = [Buf(n) for n in "state state_bf sq rt".split()]
        pb, bpb = self.pb, self.b_pb

        def load_front_tiles(kt):
            tsl = slice(kt * 128, (kt + 1) * 128)
            P.dma(SP, out=xsT, in_=scr[0:1024, tsl].rearrange("(c p) t -> p c t", p=128), reads=b_scr[0:8], writes=[b_xs])
            P.dma(SP, out=BT, in_=scr[1024:1536, tsl].rearrange("(c p) t -> p c t", p=128), reads=b_scr[8:12], writes=[b_B])
            P.dma(SP, out=CTs[kt % 2], in_=scr[1536:2048, tsl].rearrange("(c p) t -> p c t", p=128), reads=b_scr[12:16],
                  writes=[b_Cs[kt % 2]])

        def front(kt):
            E, b_E = Es[kt % 2], b_Es[kt % 2]
            dec, b_dec = decs[kt % 2], b_decs[kt % 2]
            xdt, b_xdt = xdts[kt % 2], b_xdts[kt % 2]
            ytok, b_ytok = ytoks[kt % 2], b_ytoks[kt % 2]
            Btok, b_Btok = Btoks[kt % 2], b_Btoks[kt % 2]
            CT, b_C = CTs[kt % 2], b_Cs[kt % 2]
            dAk = dA_all[:, kt * 16:(kt + 1) * 16]
            self.mm(pb[0][:, 0:16], Lm, dAk, True, True, [b_cm, b_dt], [bpb[0]])
            self.mm(pb[0][:, 16:32], U, dAk, True, True, [b_cm, b_dt], [bpb[0]])
            self.mm(pb[0][:, 32:48], onesf, dAk, True, True, [b_cm, b_dt], [bpb[0]])
            self.act(E, pb[0][:, 0:48], AF.Exp, [bpb[0]], [b_E])
            yield
            for half in range(2):
                self.tt(POOL, rhsseg, Lm.unsqueeze(1).to_broadcast([128, 8, 128]),
                        dAk[:, 8 * half:8 * half + 8].unsqueeze(2).to_broadcast([128, 8, 128]), ALU.mult, [b_cm, b_dt], [b_rs])
                yield
                for q in range(2):
                    self.mm(pb[2 + q][:, :], U, rhsseg[:, 4 * q:4 * q + 4, :].rearrange("p h l -> p (h l)"), True, True,
                            [b_cm, b_rs], [bpb[2 + q]])
                    self.act(dec[:, 8 * half + 4 * q:8 * half + 4 * q + 4, :].rearrange("p h l -> p (h l)"), pb[2 + q][:, :],
                             AF.Exp, [bpb[2 + q]], [b_dec])
                yield
            for g in range(4):
                self.mm(pb[0][:, g * 128:(g + 1) * 128], BT[:, g, :], CT[:, g, :], True, True, [b_B, b_C], [bpb[0]])
            self.tt(DVE, CBm, pb[0][:, :].rearrange("p (g l) -> p g l", g=4), mask01.unsqueeze(1).to_broadcast([128, 4, 128]),
                    ALU.mult, [bpb[0], b_cm], [b_CBm])
            yield
            d4 = dec.rearrange("p (g j) l -> p g j l", g=4)
            self.tt(POOL, d4, d4, CBm.unsqueeze(2).to_broadcast([128, 4, 4, 128]), ALU.mult, [b_dec, b_CBm], [b_dec])
            yield
            pxb = pb[1][:].bitcast(BF16)
            for c in range(NCH):
                self.tr(pxb[:, c * 128:(c + 1) * 128], xsT[:, c, :], [b_xs], [bpb[1]])
            px3 = pxb.rearrange("p (h d) -> p h d", h=16)
            self.tt(DVE, xdt, px3, dt_all[:, kt * 16:(kt + 1) * 16].unsqueeze(2).to_broadcast([128, 16, 64]), ALU.mult,
                    [bpb[1], b_dt], [b_xdt])
            self.tt(DVE, ytok.rearrange("p (h d) -> p h d", h=16), px3, bc3[:, 32:48].unsqueeze(2).to_broadcast([128, 16, 64]),
                    ALU.mult, [bpb[1], b_bc3], [b_ytok])
            yield
            if kt < NT - 1:
                pbb = pb[0][:].bitcast(BF16)
                for g in range(4):
                    self.tr(pbb[:, g * 128:(g + 1) * 128], BT[:, g, :], [b_B], [bpb[0]])
                self.act(Btok.rearrange("p g n -> p (g n)"), pbb[:, 0:512], AF.Copy, [bpb[0]], [b_Btok])
            yield

        def back(kt):
            tsl = slice(kt * 128, (kt + 1) * 128)
            E, b_E = Es[kt % 2], b_Es[kt % 2]
            dec, b_dec = decs[kt % 2], b_decs[kt % 2]
            xdt, b_xdt = xdts[kt % 2], b_xdts[kt % 2]
            ytok, b_ytok = ytoks[kt % 2], b_ytoks[kt % 2]
            Btok, b_Btok = Btoks[kt % 2], b_Btoks[kt % 2]
            CT, b_C = CTs[kt % 2], b_Cs[kt % 2]
            P.dma(SP, out=zsT, in_=scr[2048:3072, tsl].rearrange("(c p) t -> p c t", p=128), reads=b_scr[16:24], writes=[b_z])
            if kt < NT - 1:
                self.tt(POOL, xdd, xdt, E[:, 16:32].unsqueeze(2).to_broadcast([128, 16, 64]), ALU.mult, [b_xdt, b_E], [b_xdd])
            for h in range(HB):
                self.mm(pb[4 + h // 8][:, (h % 8) * 64:(h % 8 + 1) * 64], dec[:, h, :], xdt[:, h, :], True, True,
                        [b_dec, b_xdt], [bpb[4 + h // 8]])
            yield
            if kt > 0:
                for g in range(4):
                    self.mm(pb[6 + g // 2][:, (g % 2) * 256:(g % 2 + 1) * 256], CT[:, g, :], state_bf[:, g * 256:(g + 1) * 256],
                            True, True, [b_C, b_sbf], [bpb[6 + g // 2]])
            yield
            for hb in range(2):
                hs = slice(hb * 512, (hb + 1) * 512)
                if kt > 0:
                    self.tt(DVE, y1[:, hs].rearrange("p (h d) -> p h d", h=8), pb[6 + hb][:, :].rearrange("p (h d) -> p h d", h=8),
                            E[:, 8 * hb:8 * hb + 8].unsqueeze(2).to_broadcast([128, 8, 64]), ALU.mult, [bpb[6 + hb], b_E], [b_y1])
                    self.tt(DVE, y1[:, hs], y1[:, hs], pb[4 + hb][:, :], ALU.add, [b_y1, bpb[4 + hb]], [b_y1])
                    self.tt(DVE, ytok[:, hs], y1[:, hs], ytok[:, hs], ALU.add, [b_y1, b_ytok], [b_ytok])
                else:
                    self.tt(DVE, ytok[:, hs], pb[4 + hb][:, :], ytok[:, hs], ALU.add, [bpb[4 + hb], b_ytok], [b_ytok])
                yield
            if kt < NT - 1:
                for g in range(4):
                    self.mm(pb[6 + g // 2][:, (g % 2) * 256:(g % 2 + 1) * 256], Btok[:, g, :],
                            xdd[:, 4 * g:4 * g + 4, :].rearrange("p h d -> p (h d)"), True, True, [b_Btok, b_xdd], [bpb[6 + g // 2]])
                yield
                for hb in range(2):
                    hs = slice(hb * 512, (hb + 1) * 512)
                    if kt > 0:
                        s3 = state[:, hs].rearrange("p (h d) -> p h d", h=8)
                        self.tt(POOL, s3, s3, E[:, 32 + 8 * hb:32 + 8 * hb + 8].unsqueeze(2).to_broadcast([128, 8, 64]), ALU.mult,
                                [b_state, b_E], [b_state])
                        self.tt(DVE, state[:, hs], state[:, hs], pb[6 + hb][:, :], ALU.add, [b_state, bpb[6 + hb]], [b_state])
                    else:
                        self.cp(DVE, state[:, hs], pb[6 + hb][:, :], [bpb[6 + hb]], [b_state])
                    self.cp(POOL, state_bf[:, hs], state[:, hs], [b_state], [b_sbf])
                    yield
            pyb = pb[4][:].bitcast(BF16)
            for c in range(NCH):
                self.tr(pyb[:, c * 128:(c + 1) * 128], ytok[:, c * 128:(c + 1) * 128], [b_ytok], [bpb[4]])
            self.tt(DVE, zsT, pyb.rearrange("p (c t) -> p c t", c=8), zsT, ALU.mult, [bpb[4], b_z], [b_z])
            yield
            for g in range(4):
                self.act(sq.rearrange("p c t -> p (c t)"), zsT[:, 2 * g:2 * g + 2, :].rearrange("p c t -> p (c t)"), AF.Square,
                         [b_z], [b_sq])
                self.mm(pb[5][:, 0:128], self.onesb[:], sq[:, 0, :], True, False, [self.b_const, b_sq], [bpb[5]])
                self.mm(pb[5][:, 0:128], self.onesb[:], sq[:, 1, :], False, True, [self.b_const, b_sq], [bpb[5]])
                self.act(rt, pb[5][:, 0:128], AF.Sqrt, [bpb[5]], [b_rt], scale=1.0 / 256, bias=1e-6)
                self.recip(rt, rt, [b_rt], [b_rt])
                for c2 in range(2):
                    c = 2 * g + c2
                    self.stt(self.ybr[:, c, tsl], zsT[:, c, :], snw[:, c:c + 1], rt, ALU.mult, ALU.mult, [b_z, b_snw, b_rt],
                             [self.b_ybr[c]])
                yield

        def interleave(*gens):
            gens = [g for g in gens if g is not None]
            while gens:
                for g in list(gens):
                    try:
                        next(g)
                    except StopIteration:
                        gens.remove(g)

        load_front_tiles(0)
        interleave(front(0))
        for kt in range(NT):
            nxt = None
            if kt + 1 < NT:
                load_front_tiles(kt + 1)
                nxt = front(kt + 1)
            interleave(nxt, back(kt))

    def diff(self, s, l):
        P = self.P
        lam_init = 0.8 - 0.6 * math.exp(-0.3 * l)
        cosE = self.carve(0, L)
        sinE = self.carve(2048, L)
        qT = self.carve(4096, L)
        kT = self.carve(6144, L)
        vaug = self.carve(8192, 2064).rearrange("p (t d) -> p t d", t=NT)
        sg = self.carve(10272, L)
        o = self.carve(12320, L, F32).rearrange("p (t d) -> p t d", t=NT)
        on = self.carve(16416, L).rearrange("p (t d) -> p t d", t=NT)
        pT = [self.carve(18464 + 512 * i, 512) for i in range(4)]
        xq = [self.carve(20512, 512), self.carve(21024, 512)]
        tf = [self.carve(21536, 512, F32), self.carve(22560, 512, F32)]
        pm = self.carve(23584, 128)
        b_tab = Buf("ropetab")
        b_q, b_k = Buf("dq"), Buf("dk")
        b_v = [Buf("dv%d" % i) for i in range(4)]
        b_sg = [Buf("dsg%d" % i) for i in range(4)]
        b_o = [Buf("o%d" % i) for i in range(NT)]
        b_on = [Buf("on%d" % i) for i in range(NT)]
        b_pT = [Buf("dpT%d" % i) for i in range(4)]
        b_xq = [Buf("xq0"), Buf("xq1")]
        b_tf = [Buf("tf0"), Buf("tf1")]
        b_pm = Buf("pm")
        b_rc = [Buf("drc%d" % i) for i in range(4)]
        b_lam = Buf("lam")
        b_ss2 = Buf("ss2")
        sm = self.small
        lamt = sm[:, 96:104]
        slw = sm[:, 104:106]
        ss2 = sm[:, 112:128]
        rt2 = sm[:, 128:144]
        rstd2 = sm[:, 144:160]
        P.dma(POOL, out=cosE, in_=self.rope_d[0], reads=[self.b_rope], writes=[b_tab])
        P.dma(POOL, out=sinE, in_=self.rope_d[1], reads=[self.b_rope], writes=[b_tab])
        self.memset(POOL, pm, 0.0, [b_pm])
        for a in range(2):
            P.op(POOL, lambda e, a=a: e.affine_select(out=pm[:, 64 * a:64 * a + 8], in_=self.onesb[:, 0:8], pattern=[[-1, 8]],
                                                      compare_op=ALU.is_equal, fill=0.0, base=-64 * a - 8, channel_multiplier=1),
                 [self.b_const], [b_pm])
            P.op(POOL, lambda e, a=a: e.affine_select(out=pm[:, 64 * a + 8:64 * a + 16], in_=self.onesb[:, 0:8], pattern=[[-1, 8]],
                                                      compare_op=ALU.is_equal, fill=0.0, base=-64 * a, channel_multiplier=1),
                 [self.b_const], [b_pm])
        dl = tf[0][:, 0:256]
        P.dma(SP, out=dl, in_=self.diff_lambda_d[l:l + 1].rearrange("o a d -> o (a d)").partition_broadcast(128), writes=[b_tf[0]])
        junk = tf[1][:, 0:64]
        self.stt(junk, dl[:, 0:64], 1.0, dl[:, 64:128], ALU.mult, ALU.mult, [b_tf[0]], [b_tf[1], b_lam], accum_out=lamt[:, 0:1])
        self.stt(junk, dl[:, 128:192], 1.0, dl[:, 192:256], ALU.mult, ALU.mult, [b_tf[0]], [b_tf[1], b_lam], accum_out=lamt[:, 1:2])
        self.act(lamt[:, 2:4], lamt[:, 0:2], AF.Exp, [b_lam], [b_lam])
        self.tt(DVE, lamt[:, 4:5], lamt[:, 2:3], lamt[:, 3:4], ALU.subtract, [b_lam], [b_lam])
        self.ts(DVE, lamt[:, 5:6], lamt[:, 4:5], -1.0, -lam_init, ALU.mult, ALU.add, [b_lam], [b_lam])
        neglam = lamt[:, 5:6]
        P.dma(SP, out=slw[:, 0:1], in_=self.subln_w_d[l:l + 1].rearrange("o p -> p o"), writes=[b_lam], allow_slow_non_contiguous=True)
        self.ts(DVE, slw[:, 0:1], slw[:, 0:1], (1.0 - lam_init) * 0.5, None, ALU.mult, None, [b_lam], [b_lam])
        for g4 in range(4):
            self.memset(DVE, vaug[:, 4 * g4:4 * g4 + 4, 128:129], 1.0, [b_v[g4]])
        pS = [self.pb[2], self.pb[3]]
        b_pS = [self.b_pb[2], self.b_pb[3]]
        pO = self.pb[4:8]
        b_pO = self.b_pb[4:8]
        sidx = 0
        k2 = 0
        for h in range(8):
            wq, bwq = self.wload(self.w_in_cols(l, C_DQ + 128 * h, 128), 128)
            wk, bwk = self.wload(self.w_in_cols(l, C_DK + 128 * h, 128), 128)
            wv, bwv = self.wload(self.w_in_cols(l, C_DV + 128 * h, 128), 128)
            wg, bwg = self.wload(self.w_in_cols(l, C_DG + 128 * h, 128), 128)
            for tb in range(4):
                sl = slice(tb * 512, (tb + 1) * 512)
                units = []
                for ui, (w_, bw_, dst, bdst, scale) in enumerate(((wq, bwq, qT, b_q, 0.125), (wk, bwk, kT, b_k, 1.0))):
                    ps, bps = self.pj()
                    self.proj_fm(w_, bw_, 128, tb, ps, bps)
                    x_, bx_ = xq[ui], b_xq[ui]
                    self.ts(DVE, x_, ps[:, :], scale, None, ALU.mult, None, [bps], [bx_])
                    units.append((x_, bx_, tf[ui], b_tf[ui], dst, bdst))
                for (x_, bx_, t1, bt1, dst, bdst) in units:
                    S, bS = pS[sidx % 2], b_pS[sidx % 2]
                    sidx += 1
                    self.mm(S[:, :], pm, x_, True, True, [b_pm, bx_], [bS])
                    self.tt(DVE, t1, S[:, :], sinE[:, sl], ALU.mult, [bS, b_tab], [bt1])
                    self.tt(DVE, x_, x_, cosE[:, sl], ALU.mult, [bx_, b_tab], [bx_])
                    self.tt(DVE, dst[:, sl], t1, x_, ALU.add, [bt1, bx_], [bdst])
                ps, bps = self.pj()
                self.proj_fm(wg, bwg, 128, tb, ps, bps)
                t1, bt1 = tf[k2 % 2], b_tf[k2 % 2]
                k2 += 1
                self.act(t1, ps[:, :], AF.Tanh, [bps], [bt1], scale=0.5)
                self.stt(sg[:, sl], t1, 1.0, ps[:, :], ALU.add, ALU.mult, [bt1, bps], [b_sg[tb]])
            for g4 in range(4):
                ps, bps = self.pj()
                for u in range(4):
                    kt = 4 * g4 + u
                    for c in range(NCH):
                        self.mm(ps[:, u * 128:(u + 1) * 128], self.hT[:, c, kt * 128:(kt + 1) * 128], wv[:, c, :],
                                c == 0, c == NCH - 1, [bwv, self.b_hT[kt]], [bps])
                self.cp(DVE, vaug[:, 4 * g4:4 * g4 + 4, 0:128], ps[:, :].rearrange("p (t d) -> p t d", t=4), [bps], [b_v[g4]])
            tiles = [(comp, jq, i) for comp in range(2) for jq in range(4) for i in range(4 * jq + 4)]

            def emitS(tile, idx):
                comp, jq, i = tile
                pr = slice(64 * comp, 64 * comp + 64)
                r_ = i - 4 * jq
                qlo = max(512 * jq, 128 * i)
                w = 512 * (jq + 1) - qlo
                S, bS = pS[idx % 2], b_pS[idx % 2]
                pt, bpt = pT[idx % 4], b_pT[idx % 4]
                self.mm(S[:, 0:w], kT[pr, i * 128:(i + 1) * 128], qT[pr, qlo:qlo + w], True, r_ < 0, [b_k, b_q], [bS])
                if r_ >= 0:
                    self.mm(S[:, 0:128], self.ident[:], self.bmask[:], False, True, [self.b_const], [bS])
                self.act(pt[:, 0:w], S[:, 0:w], AF.Exp, [bS], [bpt])

            def emitPV(tile, idx):
                comp, jq, i = tile
                qlo = max(512 * jq, 128 * i)
                w = 512 * (jq + 1) - qlo
                pt, bpt = pT[idx % 4], b_pT[idx % 4]
                for cc in range(w // 128):
                    qc = qlo // 128 + cc
                    O, bO = pO[qc % 4], b_pO[qc % 4]
                    self.mm(O[:, 0:129], pt[:, cc * 128:(cc + 1) * 128], vaug[:, i, :], i == 0, i == qc,
                            [bpt, b_v[i // 4]], [bO])
                    if i == qc:
                        rcq = self.rc[:, qc:qc + 1]
                        self.recip(rcq, O[:, 128:129], [bO], [b_rc[qc % 4]])
                        if comp == 0:
                            self.ts(DVE, o[:, qc, :], O[:, 0:128], rcq, None, ALU.mult, None, [bO, b_rc[qc % 4]], [b_o[qc]])
                        else:
                            t1, bt1 = tf[qc % 2], b_tf[qc % 2]
                            self.ts(DVE, t1[:, 0:128], O[:, 0:128], rcq, neglam, ALU.mult, ALU.mult,
                                    [bO, b_rc[qc % 4], b_lam], [bt1])
                            self.tt(DVE, o[:, qc, :], t1[:, 0:128], o[:, qc, :], ALU.add, [bt1, b_o[qc]], [b_o[qc]])
                            self.stt(t1[:, 128:256], o[:, qc, :], 1.0, o[:, qc, :], ALU.mult, ALU.mult,
                                     [b_o[qc]], [bt1, b_ss2], accum_out=ss2[:, qc:qc + 1])

            for n_, tile in enumerate(tiles):
                emitS(tile, sidx + n_)
                if n_ > 0:
                    emitPV(tiles[n_ - 1], sidx + n_ - 1)
            emitPV(tiles[-1], sidx + len(tiles) - 1)
            sidx += len(tiles)
            self.act(rt2, ss2, AF.Sqrt, [b_ss2], [b_ss2], scale=1.0 / 128, bias=1e-6)
            self.recip(rstd2, rt2, [b_ss2], [b_ss2])
            for qc in range(NT):
                self.ts(POOL if qc % 2 else DVE, on[:, qc, :], o[:, qc, :], rstd2[:, qc:qc + 1], None, ALU.mult, None,
                        [b_o[qc], b_ss2], [b_on[qc]])
            for g4 in range(4):
                ps, bps = self.pj()
                psb = ps[:].bitcast(BF16)
                for u in range(4):
                    qc = 4 * g4 + u
                    self.tr(psb[:, u * 128:(u + 1) * 128], on[:, qc, :], [b_on[qc]], [bps])
                self.stt(self.ybr[:, h, g4 * 512:(g4 + 1) * 512], psb[:, 0:512], slw[:, 0:1], sg[:, g4 * 512:(g4 + 1) * 512],
                         ALU.mult, ALU.mult, [bps, b_sg[g4], b_lam], [self.b_ybr[h]])


_NC_CACHE = {}


def kernel(x, norm_w, w_in, b_forget, conv_w, conv_b, dt_bias, a_log, d_skip, ssm_norm_w, diff_lambda, subln_w,
           w_branch, w_out, final_norm_w):
    ncores = 8
    nseq = x.shape[0] // ncores
    if 'nc' not in _NC_CACHE:
        _NC_CACHE['nc'] = Gen(nseq, 2, "abc", final=True).build()
    nc = _NC_CACHE['nc']
    f = lambda a: np.ascontiguousarray(np.asarray(a, dtype=np.float32))
    shared = dict(norm_w=f(norm_w), w_in=f(w_in), b_forget=f(b_forget), conv_w=f(conv_w), conv_b=f(conv_b),
                  dt_bias=f(dt_bias), a_log=f(a_log), d_skip=f(d_skip), ssm_norm_w=f(ssm_norm_w),
                  diff_lambda=f(diff_lambda), subln_w=f(subln_w), w_branch=f(w_branch), w_out=f(w_out),
                  final_norm_w=f(final_norm_w).reshape(1, -1))
    xs = f(x)
    in_maps = [dict(shared, x=np.ascontiguousarray(xs[i * nseq:(i + 1) * nseq])) for i in range(ncores)]
    res = run_bass_kernel_spmd(nc, in_maps, core_ids=list(range(ncores)))
    return np.concatenate([np.asarray(r["out"], dtype=np.float32) for r in res.results], axis=0)
```

```python
import numpy as np
import concourse.bass as bass
import concourse.mybir as mybir
from concourse.bass_utils import run_bass_kernel_spmd

F32 = mybir.dt.float32
BF16 = mybir.dt.bfloat16
AF = mybir.ActivationFunctionType
ALU = mybir.AluOpType
AX = mybir.AxisListType

PE, DVE, ACT, POOL, SP = 'tensor', 'vector', 'scalar', 'gpsimd', 'sync'
ENGS = [PE, DVE, ACT, POOL, SP]


class Buf:
    __slots__ = ('name', 'w', 'r', 'rd', 'dsem')

    def __init__(self, name):
        self.name = name
        self.w = None
        self.r = {}
        self.rd = []
        self.dsem = None


class Op:
    __slots__ = ('eng', 'fn', 'deps', 'dma_sem', 'tok', 'waited')

    def __init__(self, eng, fn):
        self.eng = eng
        self.fn = fn
        self.deps = []
        self.dma_sem = None
        self.tok = None
        self.waited = False


class DmaSem:
    def __init__(self, sem):
        self.sem = sem
        self.count = 0
        self.last = None
        self.q = None


class Prog:
    def __init__(self, nc, stack):
        self.nc = nc
        self.stack = stack
        self.ops = {e: [] for e in ENGS}
        self.esem = {}
        self.nops = 0
        self.dsems = []
        self.nsem = 0
        self.dpool = {}
        self.drr = {}

    def enter(self, cm):
        return self.stack.enter_context(cm)

    def sbuf(self, name, shape, dtype):
        return self.enter(self.nc.sbuf_tensor(name, list(shape), dtype))

    def psum(self, name, shape, dtype):
        return self.enter(self.nc.psum_tensor(name, list(shape), dtype))

    def dram(self, name, shape, dtype, kind):
        return self.nc.dram_tensor(name, list(shape), dtype, kind=kind).ap()

    def dsem(self, name=None):
        self.nsem += 1
        d = DmaSem(self.enter(self.nc.semaphore(name or ('ds%d' % self.nsem))))
        self.dsems.append(d)
        return d

    def op(self, eng, fn, reads=(), writes=(), dsem=None):
        o = Op(eng, fn)
        deps = []
        for b in reads:
            if b.w is not None:
                deps.append(b.w)
        for b in writes:
            if b.w is not None:
                deps.append(b.w)
            deps.extend(b.r.values())
            deps.extend(b.rd)
        seen = set()
        for d in deps:
            if id(d) in seen or d is o:
                continue
            seen.add(id(d))
            if d.dma_sem is None and d.eng == eng and eng == PE:
                continue
            o.deps.append(d)
        if dsem is not None:
            if dsem.last is not None and all(d is not dsem.last for d in o.deps):
                o.deps.append(dsem.last)
            o.dma_sem = dsem
            dsem.count += 16
            dsem.last = o
            o.tok = (dsem.sem, dsem.count)
        for b in reads:
            if dsem is not None:
                b.rd.append(o)
            else:
                b.r[eng] = o
        for b in writes:
            b.w = o
            b.r = {}
            b.rd = []
        self.ops[eng].append(o)
        self.nops += 1
        return o

    def dma(self, eng, out, in_, reads=(), writes=(), **kw):
        b = writes[0]
        if b.dsem is None or b.dsem.q != eng:
            pool = self.dpool.setdefault(eng, [])
            if len(pool) < 24:
                d = self.dsem()
                d.q = eng
                pool.append(d)
            else:
                d = pool[self.drr.get(eng, 0) % 24]
                self.drr[eng] = self.drr.get(eng, 0) + 1
            b.dsem = d
        return self.op(eng, lambda e: e.dma_start(out=out, in_=in_, **kw), reads=reads, writes=writes, dsem=b.dsem)

    def barrier(self):
        lasts = []
        for e in ENGS:
            if self.ops[e]:
                lasts.append(self.ops[e][-1])
        for d in self.dsems:
            if d.last is not None:
                lasts.append(d.last)
        new = []
        for e in ENGS:
            o = Op(e, lambda eng: eng.nop())
            for d in lasts:
                if d.dma_sem is None and d.eng == e:
                    continue
                o.deps.append(d)
            new.append(o)
        for o in new:
            self.ops[o.eng].append(o)

    def finish(self):
        o = Op(SP, lambda eng: eng.nop())
        for d in self.dsems:
            if d.last is not None:
                o.deps.append(d.last)
        self.ops[SP].append(o)
        self.emit()

    def emit(self):
        nc = self.nc
        for e in ENGS:
            self.esem[e] = self.enter(nc.semaphore('sem_' + e))
        for e in ENGS:
            for o in self.ops[e]:
                for d in o.deps:
                    d.waited = True
        self.nmile = {}
        for e in ENGS:
            c = 0
            for o in self.ops[e]:
                if o.dma_sem is None and o.waited:
                    c += 1
                    o.tok = (self.esem[e], c)
            self.nmile[e] = c
        prog = self

        def body(e):
            def run(eng):
                have = {}
                for o in prog.ops[e]:
                    need = {}
                    for d in o.deps:
                        s, v = d.tok
                        k = id(s)
                        if have.get(k, 0) >= v:
                            continue
                        if k not in need or need[k][1] < v:
                            need[k] = (s, v)
                    for k, (s, v) in need.items():
                        eng.wait_ge(s, v)
                        have[k] = v
                    ins = o.fn(eng)
                    if o.dma_sem is not None:
                        ins.then_inc(o.tok[0], 16)
                    elif o.waited:
                        ins.then_inc(o.tok[0], 1)
            return run

        with nc.Block() as block:
            block.tensor(body(PE))
            block.vector(body(DVE))
            block.scalar(body(ACT))
            block.gpsimd(body(POOL))
            block.sync(body(SP))


import math
import numpy as np
from contextlib import ExitStack

L = 2048
D = 1024
NT = 16
NCH = 8
N_IN = 14368
C_FQ, C_FK, C_FV, C_FF, C_FG = 0, 1024, 2048, 3072, 3088
C_SZ, C_SX, C_SB, C_SC, C_SDT = 4112, 5136, 6160, 6672, 7184
C_DQ, C_DK, C_DV, C_DG, C_MG = 7200, 8224, 9248, 10272, 11296
NEG = -30000.0
ARENA = 24576
NSLOT = 6


class Gen:
    def __init__(self, nseq, nlayers, branches, final=True, dbg=None):
        self.nseq, self.nlayers, self.branches, self.final, self.dbg = nseq, nlayers, branches, final, dbg

    def mm(self, out, lhsT, rhs, start, stop, reads, writes):
        return self.P.op(PE, lambda e: e.matmul(out, lhsT=lhsT, rhs=rhs, start=start, stop=stop), reads, writes)

    def tr(self, out, in_, reads, writes):
        ident = self.ident
        return self.P.op(PE, lambda e: e.transpose(out=out, in_=in_, identity=ident[:]), list(reads) + [self.b_const], writes)

    def act(self, out, in_, func, reads, writes, **kw):
        return self.P.op(ACT, lambda e: e.activation(out=out, in_=in_, func=func, **kw), reads, writes)

    def tt(self, eng, out, in0, in1, op, reads, writes):
        return self.P.op(eng, lambda e: e.tensor_tensor(out=out, in0=in0, in1=in1, op=op), reads, writes)

    def ts(self, eng, out, in0, s1, s2, op0, op1, reads, writes, **kw):
        if op1 is None:
            return self.P.op(eng, lambda e: e.tensor_scalar(out=out, in0=in0, scalar1=s1, scalar2=None, op0=op0, **kw), reads, writes)
        return self.P.op(eng, lambda e: e.tensor_scalar(out=out, in0=in0, scalar1=s1, scalar2=s2, op0=op0, op1=op1, **kw), reads, writes)

    def stt(self, out, in0, scalar, in1, op0, op1, reads, writes, **kw):
        return self.P.op(DVE, lambda e: e.scalar_tensor_tensor(out=out, in0=in0, scalar=scalar, in1=in1, op0=op0, op1=op1, **kw), reads, writes)

    def cp(self, eng, out, in_, reads, writes):
        return self.P.op(eng, lambda e: e.tensor_copy(out=out, in_=in_), reads, writes)

    def memset(self, eng, ap, val, writes):
        return self.P.op(eng, lambda e: e.memset(ap, val), (), writes)

    def recip(self, out, in_, reads, writes):
        return self.P.op(DVE, lambda e: e.reciprocal(out=out, in_=in_), reads, writes)

    def carve(self, off, n, dtype=BF16):
        if dtype == F32:
            return self.arena[:, off:off + 2 * n].bitcast(F32)
        return self.arena[:, off:off + n]

    def wload(self, src, n):
        i = self.wrr % NSLOT
        self.wrr += 1
        ap = self.wslot[i][:, :, 0:n]
        buf = self.b_wslot[i]
        self.P.dma(POOL, out=ap, in_=src.rearrange("(c p) n -> p c n", p=128), writes=[buf])
        return ap, buf

    def w_in_cols(self, l, c0, n):
        return self.w_in_d[l, :, c0:c0 + n]

    def pj(self):
        i = self.pjrr % 2
        self.pjrr += 1
        return self.pb[i], self.b_pb[i]

    def hbufs(self, tb):
        return self.b_hT[4 * tb:4 * tb + 4]

    def proj_fm(self, w, bw, M, tb, ps, bps, rhs_src=None, rhs_bufs=None):
        src = self.hT if rhs_src is None else rhs_src
        rb = self.hbufs(tb) if rhs_bufs is None else rhs_bufs
        for c in range(NCH):
            self.mm(ps[0:M, :], w[:, c, 0:M], src[:, c, tb * 512:(tb + 1) * 512], c == 0, c == NCH - 1,
                    [bw] + list(rb), [bps])

    def build(self):
        nc = bass.Bass("TRN2", target_bir_lowering=False)
        self.nc = nc
        nseq = self.nseq
        di = lambda name, shape: nc.dram_tensor(name, list(shape), F32, kind="ExternalInput").ap()
        self.x_d = di("x", [nseq, L, D])
        self.norm_w_d = di("norm_w", [2, D])
        self.w_in_d = di("w_in", [2, D, N_IN])
        self.b_forget_d = di("b_forget", [2, 16])
        self.conv_w_d = di("conv_w", [2, 4, 2048])
        self.conv_b_d = di("conv_b", [2, 2048])
        self.dt_bias_d = di("dt_bias", [2, 16])
        self.a_log_d = di("a_log", [2, 16])
        self.d_skip_d = di("d_skip", [2, 16])
        self.ssm_norm_w_d = di("ssm_norm_w", [2, D])
        self.diff_lambda_d = di("diff_lambda", [2, 4, 64])
        self.subln_w_d = di("subln_w", [2, 128])
        self.w_branch_d = di("w_branch", [2, 3, D, D])
        self.w_out_d = di("w_out", [2, D, D])
        self.final_norm_w_d = di("final_norm_w", [1, D])
        self.rope_d = nc.dram_tensor("rope_scr", [2, 128, L], F32, kind="Internal").ap()
        self.out_d = nc.dram_tensor("out", [nseq, L, D], F32, kind="ExternalOutput").ap()
        self.gk_d = nc.dram_tensor("gk_scr", [16, 6, L], BF16, kind="Internal").ap()
        self.scr_d = nc.dram_tensor("ssd_scr", [3072, L], BF16, kind="ExternalOutput").ap()
        if self.dbg:
            self.dbg_d = nc.dram_tensor("dbg", [128, 8, L], BF16, kind="ExternalOutput").ap()
        with ExitStack() as st:
            P = Prog(nc, st)
            self.P = P
            self.xres = P.sbuf("xres", [128, NT, D], F32)
            self.hT = P.sbuf("hT", [128, NCH, L], BF16)
            self.ybr = P.sbuf("ybr", [128, NCH, L], BF16)
            self.wout = P.sbuf("wout", [128, NCH, D], BF16)
            self.arena = P.sbuf("arena", [128, ARENA], BF16)
            self.wslot = [P.sbuf("wslot%d" % i, [128, NCH, 128], BF16) for i in range(NSLOT)]
            self.ident = P.sbuf("ident", [128, 128], BF16)
            self.cmask = P.sbuf("cmask", [128, 128], BF16)
            self.bmask = P.sbuf("bmask", [128, 128], BF16)
            self.onesb = P.sbuf("onesb", [128, 128], BF16)
            self.zerob = P.sbuf("zerob", [128, 128], BF16)
            self.small = P.sbuf("small", [128, 256], F32)
            self.pb = [P.psum("pb%d" % i, [128, 512], F32) for i in range(8)]
            self.b_pb = [Buf("pb%d" % i) for i in range(8)]
            self.b_x = [Buf("x%d" % t) for t in range(NT)]
            self.b_hT = [Buf("hT%d" % t) for t in range(NT)]
            self.b_ybr = [Buf("ybr%d" % c) for c in range(NCH)]
            self.b_wout = Buf("wout")
            self.b_wslot = [Buf("ws%d" % i) for i in range(NSLOT)]
            self.b_const = Buf("const")
            self.b_small = Buf("small")
            self.b_gk = Buf("gk")
            self.b_out = [Buf("out%d" % t) for t in range(NT)]
            self.wrr = 0
            self.pjrr = 0
            sm = self.small
            self.nwT = sm[:, 0:16]
            self.negbf = sm[0:16, 16:18]
            self.ss = sm[:, 32:48]
            self.rt = sm[:, 48:64]
            self.rstd = sm[:, 64:80]
            self.rc = sm[:, 80:96]
            self.consts()
            self.rope_tables()
            for s in range(nseq):
                for t in range(NT):
                    P.dma(SP, out=self.xres[:, t, :], in_=self.x_d[s, t * 128:(t + 1) * 128, :], writes=[self.b_x[t]])
                for l in range(self.nlayers):
                    self.layer(s, l)
                self.final_out(s)
            P.finish()
        return nc

    def consts(self):
        P = self.P
        bc = self.b_const
        self.memset(POOL, self.onesb[:], 1.0, [bc])
        self.memset(POOL, self.zerob[:], 0.0, [bc])
        P.op(POOL, lambda e: e.affine_select(out=self.ident[:], in_=self.onesb[:], pattern=[[-1, 128]], compare_op=ALU.is_equal,
                                              fill=0.0, base=0, channel_multiplier=1), [bc], [bc])
        P.op(POOL, lambda e: e.affine_select(out=self.cmask[:], in_=self.zerob[:], pattern=[[1, 128]], compare_op=ALU.is_ge,
                                              fill=NEG, base=0, channel_multiplier=-1), [bc], [bc])
        self.memset(POOL, self.bmask[:], 0.0, [bc])
        self.memset(POOL, self.bmask[64:128, 0:64], NEG, [bc])
        P.dma(SP, out=self.nwT.rearrange("p (l c) -> p l c", l=2), in_=self.norm_w_d.rearrange("l (c p) -> p l c", p=128),
              writes=[self.b_small], allow_slow_non_contiguous=True)
        P.dma(SP, out=self.negbf, in_=self.b_forget_d.rearrange("l h -> h l"), writes=[self.b_small], allow_slow_non_contiguous=True)
        self.ts(DVE, self.negbf, self.negbf, -1.0, None, ALU.mult, None, [self.b_small], [self.b_small])


    def rope_tables(self):
        P = self.P
        I32 = mybir.dt.int32
        TWO_PI = 2.0 * math.pi
        C1 = 6.28125
        C2 = TWO_PI - C1
        MAGIC = 12582912.0
        wrow = self.carve(0, 128, F32)[0:1, :]
        one2 = self.carve(256, 2, F32)[0:1, :]
        wcol = self.small[:, 160:161]
        pos_i = self.arena[:, 4096:8192].bitcast(I32)
        pos = self.carve(8192, L, F32)
        ang = self.carve(12288, L, F32)
        kr = self.carve(16384, L, F32)
        r = self.carve(20480, L, F32)
        b_w, b_pi, b_pos, b_ang, b_kr, b_r = [Buf(n) for n in "wrow posi pos ang kr r".split()]
        self.memset(DVE, wrow, 0.0, [b_w])
        self.memset(DVE, one2, 1.0, [b_w])
        for m in range(128):
            i = m % 64
            if i < 16:
                f = i % 8
                wf = float(np.float32(500000.0) ** np.float32(-(2.0 * f) / 16.0))
                self.memset(DVE, wrow[:, m:m + 1], -wf if i < 8 else wf, [b_w])
        ps, bps = self.pj()
        self.mm(ps[:, 0:2], wrow, one2, True, True, [b_w], [bps])
        self.cp(DVE, wcol, ps[:, 0:1], [bps], [self.b_small])
        P.op(POOL, lambda e: e.iota(pos_i, pattern=[[1, L]], base=0, channel_multiplier=0), (), [b_pi])
        self.cp(DVE, pos, pos_i, [b_pi], [b_pos])
        self.ts(DVE, ang, pos, wcol, None, ALU.mult, None, [b_pos, self.b_small], [b_ang])
        self.ts(DVE, kr, ang, 1.0 / TWO_PI, MAGIC, ALU.mult, ALU.add, [b_ang], [b_kr])
        self.ts(DVE, kr, kr, -MAGIC, None, ALU.add, None, [b_kr], [b_kr])
        self.stt(r, kr, -C1, ang, ALU.mult, ALU.add, [b_kr, b_ang], [b_r])
        self.stt(r, kr, -C2, r, ALU.mult, ALU.add, [b_kr, b_r], [b_r])
        self.ts(DVE, r, r, 3.1415925, -3.1415925, ALU.min, ALU.max, [b_r], [b_r])
        b_rope = Buf("rope_scr")
        self.b_rope = b_rope
        self.act(ang, r, AF.Sin, [b_r], [b_ang])
        self.act(kr, r, AF.Abs, [b_r], [b_kr])
        self.act(pos, kr, AF.Sin, [b_kr], [b_pos], scale=-1.0, bias=math.pi / 2.0)
        P.dma(SP, out=self.rope_d[0], in_=pos, reads=[b_pos], writes=[b_rope])
        P.dma(SP, out=self.rope_d[1], in_=ang, reads=[b_ang], writes=[b_rope])
        P.barrier()

    def layer(self, s, l):
        P = self.P
        for eb in range(2):
            P.dma(POOL, out=self.wout[:, :, eb * 512:(eb + 1) * 512],
                  in_=self.w_out_d[l, :, eb * 512:(eb + 1) * 512].rearrange("(c p) n -> p c n", p=128), writes=[self.b_wout])
        P.barrier()
        self.rmsnorm_T(l)
        for n, br in enumerate("abc"):
            if br not in self.branches:
                continue
            P.barrier()
            if br == 'a':
                self.fox(s, l)
            elif br == 'b':
                self.ssd(s, l)
            else:
                self.diff(s, l)
            if self.dbg == br and l == self.nlayers - 1:
                for c in range(NCH):
                    P.dma(SP, out=self.dbg_d[:, c, :], in_=self.ybr[:, c, :], reads=[self.b_ybr[c]], writes=[Buf("dbg%d" % c)])
            P.barrier()
            self.merge(l, n)

    def rmsnorm_T(self, l):
        junk = self.carve(0, 1024)
        hn = [self.carve(1024, 1024), self.carve(2048, 1024)]
        b_junk = Buf("junk")
        b_hn = [Buf("hn0"), Buf("hn1")]
        b_ss = Buf("ss")
        for t in range(NT):
            self.act(junk, self.xres[:, t, :], AF.Square, [self.b_x[t]], [b_junk, b_ss], accum_out=self.ss[:, t:t + 1])
        self.act(self.rt, self.ss, AF.Sqrt, [b_ss], [b_ss], scale=1.0 / D, bias=1e-6)
        self.recip(self.rstd, self.rt, [b_ss], [b_ss])
        for t in range(NT):
            k = t % 2
            self.act(hn[k], self.xres[:, t, :], AF.Copy, [self.b_x[t], b_ss], [b_hn[k]], scale=self.rstd[:, t:t + 1])
            ps, bps = self.pj()
            psb = ps[:].bitcast(BF16)
            for c in range(NCH):
                self.tr(psb[:, c * 128:(c + 1) * 128], hn[k][:, c * 128:(c + 1) * 128], [b_hn[k]], [bps])
            self.tt(DVE, self.hT[:, :, t * 128:(t + 1) * 128], psb.rearrange("p (c n) -> p c n", c=NCH),
                    self.nwT[:, l * 8:(l + 1) * 8].unsqueeze(2).to_broadcast([128, NCH, 128]), ALU.mult,
                    [bps, self.b_small], [self.b_hT[t]])

    def merge(self, l, n):
        gp = self.carve(0, NCH * L).rearrange("p (c t) -> p c t", c=NCH)
        th = [self.carve(NCH * L, 512, F32), self.carve(NCH * L + 1024, 512, F32)]
        b_gp = [Buf("gp%d" % t) for t in range(4)]
        b_th = [Buf("th0"), Buf("th1")]
        k = 0
        for co in range(NCH):
            wb, bwb = self.wload(self.w_branch_d[l, n, :, co * 128:(co + 1) * 128], 128)
            wg, bwg = self.wload(self.w_in_cols(l, C_MG + n * 1024 + co * 128, 128), 128)
            for tb in range(4):
                ps1, bps1 = self.pj()
                self.proj_fm(wb, bwb, 128, tb, ps1, bps1, rhs_src=self.ybr, rhs_bufs=self.b_ybr)
                ps2, bps2 = self.pb[2 + k % 2], self.b_pb[2 + k % 2]
                self.proj_fm(wg, bwg, 128, tb, ps2, bps2)
                t_ = th[k % 2]
                self.act(t_, ps2[:, :], AF.Tanh, [bps2], [b_th[k % 2]], scale=0.5)
                self.stt(gp[:, co, tb * 512:(tb + 1) * 512], t_, 1.0, ps1[:, :], ALU.add, ALU.mult,
                         [b_th[k % 2], bps1], [b_gp[tb]])
                k += 1
        for t in range(NT):
            for eb in range(2):
                ps, bps = self.pb[4 + (2 * t + eb) % 4], self.b_pb[4 + (2 * t + eb) % 4]
                for co in range(NCH):
                    self.mm(ps[:, :], gp[:, co, t * 128:(t + 1) * 128], self.wout[:, co, eb * 512:(eb + 1) * 512],
                            co == 0, co == NCH - 1, [b_gp[t // 4], self.b_wout], [bps])
                xs = self.xres[:, t, eb * 512:(eb + 1) * 512]
                self.stt(xs, ps[:, :], 0.5, xs, ALU.mult, ALU.add, [bps, self.b_x[t]], [self.b_x[t]])

    def final_out(self, s):
        P = self.P
        P.barrier()
        b_ss = Buf("fss")
        if self.final:
            junk = self.carve(0, 1024)
            fnw = self.carve(1024, 1024, F32)
            b_junk, b_fnw = Buf("fjunk"), Buf("fnw")
            P.dma(SP, out=fnw, in_=self.final_norm_w_d.partition_broadcast(128), writes=[b_fnw])
            for t in range(NT):
                self.act(junk, self.xres[:, t, :], AF.Square, [self.b_x[t]], [b_junk, b_ss], accum_out=self.ss[:, t:t + 1])
            self.act(self.rt, self.ss, AF.Sqrt, [b_ss], [b_ss], scale=1.0 / D, bias=1e-6)
            self.recip(self.rstd, self.rt, [b_ss], [b_ss])
            for t in range(NT):
                xs = self.xres[:, t, :]
                self.stt(xs, xs, self.rstd[:, t:t + 1], fnw, ALU.mult, ALU.mult, [self.b_x[t], b_ss, b_fnw], [self.b_x[t]])
        for t in range(NT):
            P.dma(SP, out=self.out_d[s, t * 128:(t + 1) * 128, :], in_=self.xres[:, t, :], reads=[self.b_x[t]], writes=[self.b_out[t]])

    def fox(self, s, l):
        P = self.P
        lf = self.carve(0, L, F32)[0:16, :]
        G = self.carve(4096, L, F32)[0:16, :]
        r = self.carve(8192, L, F32)[0:16, :]
        parts = [self.carve(12288 + 2048 * j, L)[0:16, :] for j in range(6)]
        b_lf, b_G, b_r = Buf("lf"), Buf("G"), Buf("r")
        b_parts = [Buf("part%d" % j) for j in range(6)]
        wff, bwff = self.wload(self.w_in_cols(l, C_FF, 16), 16)
        for tb in range(4):
            ps, bps = self.pj()
            self.proj_fm(wff, bwff, 16, tb, ps, bps)
            self.act(lf[:, tb * 512:(tb + 1) * 512], ps[0:16, :], AF.Exp, [bps, self.b_small], [b_lf],
                     scale=-1.0, bias=self.negbf[:, l:l + 1])
        self.act(lf, lf, AF.Ln, [b_lf], [b_lf], bias=1.0)
        P.op(DVE, lambda e: e.tensor_tensor_scan(out=G, data0=lf, data1=lf, initial=0.0, op0=ALU.add, op1=ALU.max), [b_lf], [b_G])
        self.cp(DVE, parts[0], G, [b_G], [b_parts[0]])
        self.tt(DVE, r, G, parts[0], ALU.subtract, [b_G, b_parts[0]], [b_r])
        self.cp(DVE, parts[1], r, [b_r], [b_parts[1]])
        self.tt(DVE, r, r, parts[1], ALU.subtract, [b_r, b_parts[1]], [b_r])
        self.cp(DVE, parts[2], r, [b_r], [b_parts[2]])
        for j in range(3):
            self.ts(DVE, parts[3 + j], parts[j], -1.0, None, ALU.mult, None, [b_parts[j]], [b_parts[3 + j]])
        for j in range(6):
            P.dma(SP, out=self.gk_d[:, j, :], in_=parts[j], reads=[b_parts[j]], writes=[self.b_gk])
        P.barrier()
        qaug = [self.carve(0, L), self.carve(2048, L)]
        kaug = [self.carve(4096, L), self.carve(6144, L)]
        vaug = self.carve(8192, 4096).rearrange("p (t h d) -> p t h d", t=NT, h=2)
        sg = self.carve(12288, L)
        pT = [self.carve(14336 + 512 * i, 512) for i in range(4)]
        th = [self.carve(16384, 512, F32), self.carve(17408, 512, F32)]
        rinv = [self.carve(18432, 512, F32), self.carve(19456, 512, F32)]
        tmpn = [self.carve(20480, 512, F32), self.carve(21504, 512, F32)]
        b_q = [Buf("qaug0"), Buf("qaug1")]
        b_k = [Buf("kaug0"), Buf("kaug1")]
        b_v = [Buf("vaug%d" % i) for i in range(4)]
        b_sg = [Buf("sg%d" % i) for i in range(4)]
        b_pT = [Buf("pT%d" % i) for i in range(4)]
        b_th = [Buf("fth0"), Buf("fth1")]
        b_rinv = [Buf("rinv0"), Buf("rinv1")]
        b_tmpn = [Buf("tmpn0"), Buf("tmpn1")]
        for hh in range(2):
            self.memset(DVE, kaug[hh][64:67, :], 1.0, [b_k[hh]])
            self.memset(DVE, qaug[hh][64:67, :], 1.0, [b_q[hh]])
            P.dma(SP, out=qaug[hh][67:70, :], in_=qaug[hh][64:67, :], reads=[b_q[hh]], writes=[b_q[hh]])
        for g4 in range(4):
            self.memset(DVE, vaug[:, 4 * g4:4 * g4 + 4, 0, 64:128], 1.0, [b_v[g4]])
            self.memset(DVE, vaug[:, 4 * g4:4 * g4 + 4, 1, 0:64], 1.0, [b_v[g4]])
        pS = self.pb[2:6]
        b_pS = self.b_pb[2:6]
        pO = self.pb[6:8]
        b_pO = self.b_pb[6:8]
        sidx = 0
        thk = 0
        blk = 0
        for j in range(8):
            wq, bwq = self.wload(self.w_in_cols(l, C_FQ + 128 * j, 128), 128)
            wk, bwk = self.wload(self.w_in_cols(l, C_FK + 128 * j, 128), 128)
            wv, bwv = self.wload(self.w_in_cols(l, C_FV + 128 * j, 128), 128)
            wg, bwg = self.wload(self.w_in_cols(l, C_FG + 128 * j, 128), 128)
            for hh in range(2):
                h = 2 * j + hh
                P.dma(SP, out=kaug[hh][67:70, :], in_=self.gk_d[h, 0:3, :], reads=[self.b_gk], writes=[b_k[hh]])
                P.dma(SP, out=qaug[hh][64:67, :], in_=self.gk_d[h, 3:6, :], reads=[self.b_gk], writes=[b_q[hh]])
            for tb in range(4):
                sl = slice(tb * 512, (tb + 1) * 512)
                ps, bps = self.pj()
                self.proj_fm(wq, bwq, 128, tb, ps, bps)
                for hh in range(2):
                    self.ts(DVE, qaug[hh][0:64, sl], ps[64 * hh:64 * hh + 64, :], 0.125, None, ALU.mult, None, [bps], [b_q[hh]])
                ps, bps = self.pj()
                self.proj_fm(wk, bwk, 128, tb, ps, bps)
                for hh in range(2):
                    self.cp(DVE, kaug[hh][0:64, sl], ps[64 * hh:64 * hh + 64, :], [bps], [b_k[hh]])
                ps, bps = self.pj()
                self.proj_fm(wg, bwg, 128, tb, ps, bps)
                t_ = th[thk % 2]
                self.act(t_, ps[:, :], AF.Tanh, [bps], [b_th[thk % 2]], scale=0.5)
                self.stt(sg[:, sl], t_, 1.0, ps[:, :], ALU.add, ALU.mult, [b_th[thk % 2], bps], [b_sg[tb]])
                thk += 1
            for g4 in range(4):
                ps, bps = self.pj()
                for u in range(4):
                    kt = 4 * g4 + u
                    for c in range(NCH):
                        self.mm(ps[:, u * 128:(u + 1) * 128], self.hT[:, c, kt * 128:(kt + 1) * 128], wv[:, c, :],
                                c == 0, c == NCH - 1, [bwv, self.b_hT[kt]], [bps])
                ps4 = ps[:, :].rearrange("p (t h d) -> p t h d", t=4, h=2)
                self.cp(DVE, vaug[:, 4 * g4:4 * g4 + 4, 0, 0:64], ps4[:, :, 0, :], [bps], [b_v[g4]])
                self.cp(DVE, vaug[:, 4 * g4:4 * g4 + 4, 1, 64:128], ps4[:, :, 1, :], [bps], [b_v[g4]])
            tiles = [(hh, jq, i) for hh in range(2) for jq in range(4) for i in range(4 * jq + 4)]

            def emitS(tile, idx):
                hh, jq, i = tile
                r_ = i - 4 * jq
                qlo = max(512 * jq, 128 * i)
                w = 512 * (jq + 1) - qlo
                S, bS = pS[idx % 4], b_pS[idx % 4]
                pt, bpt = pT[idx % 4], b_pT[idx % 4]
                self.mm(S[:, 0:w], kaug[hh][0:70, i * 128:(i + 1) * 128], qaug[hh][0:70, qlo:qlo + w],
                        True, r_ < 0, [b_k[hh], b_q[hh]], [bS])
                if r_ >= 0:
                    self.mm(S[:, 0:128], self.ident[:], self.cmask[:], False, True, [self.b_const], [bS])
                self.act(pt[:, 0:w], S[:, 0:w], AF.Exp, [bS], [bpt])

            def emitPV(tile, idx, bi):
                hh, jq, i = tile
                qlo = max(512 * jq, 128 * i)
                w = 512 * (jq + 1) - qlo
                qoff = qlo - 512 * jq
                pt, bpt = pT[idx % 4], b_pT[idx % 4]
                O, bO = pO[bi % 2], b_pO[bi % 2]
                last = (i == 4 * jq + 3)
                self.mm(O[:, qoff:qoff + w], vaug[:, i, hh, :], pt[:, 0:w], i == 0, last, [bpt, b_v[i // 4]], [bO])
                if last:
                    pr = slice(0, 64) if hh == 0 else slice(64, 128)
                    sr = slice(64, 128) if hh == 0 else slice(0, 64)
                    ri, bri = rinv[bi % 2], b_rinv[bi % 2]
                    tm, btm = tmpn[bi % 2], b_tmpn[bi % 2]
                    self.recip(ri[pr, :], O[sr, :], [bO], [bri])
                    self.tt(DVE, tm[pr, :], O[pr, :], ri[pr, :], ALU.mult, [bO, bri], [btm])
                    qs = slice(jq * 512, (jq + 1) * 512)
                    self.stt(self.ybr[pr, j, qs], tm[pr, :], 0.5, sg[pr, qs], ALU.mult, ALU.mult, [btm, b_sg[jq]], [self.b_ybr[j]])

            bis = []
            for (hh, jq, i) in tiles:
                bis.append(blk + hh * 4 + jq)
            LAG = 3
            for n_, tile in enumerate(tiles):
                emitS(tile, sidx + n_)
                if n_ >= LAG:
                    emitPV(tiles[n_ - LAG], sidx + n_ - LAG, bis[n_ - LAG])
            for n_ in range(len(tiles) - LAG, len(tiles)):
                emitPV(tiles[n_], sidx + n_, bis[n_])
            sidx += len(tiles)
            blk += 8

    def ssd(self, s, l):
        P = self.P
        scr = self.scr_d
        HB = 16
        dt_all = self.carve(0, 256, F32)
        dA_all = self.carve(512, 256, F32)
        bc3 = self.carve(1024, 48, F32)
        cwT = self.carve(1120, 80, F32).rearrange("p (c j) -> p c j", c=16)
        snw = self.carve(1280, 8, F32)
        Lm = self.carve(1296, 128, F32)
        U = self.carve(1552, 128, F32)
        onesf = self.carve(1808, 128, F32)
        mask01 = self.carve(2064, 128)
        E = self.carve(2192, 48, F32)
        b_dt, b_bc3, b_cw, b_snw, b_cm, b_E = Buf("dt_all"), Buf("bc3"), Buf("cwT"), Buf("snw"), Buf("ssdconst"), Buf("E")
        self.memset(POOL, onesf, 1.0, [b_cm])
        P.op(POOL, lambda e: e.affine_select(out=Lm, in_=self.onesb[:], pattern=[[1, 128]], compare_op=ALU.is_ge, fill=0.0,
                                              base=0, channel_multiplier=-1), [self.b_const], [b_cm])
        P.op(POOL, lambda e: e.affine_select(out=mask01, in_=self.onesb[:], pattern=[[1, 128]], compare_op=ALU.is_ge, fill=0.0,
                                              base=0, channel_multiplier=-1), [self.b_const], [b_cm])
        P.op(POOL, lambda e: e.affine_select(out=U, in_=self.onesb[:], pattern=[[-1, 128]], compare_op=ALU.is_gt, fill=0.0,
                                              base=0, channel_multiplier=1), [self.b_const], [b_cm])
        P.dma(SP, out=bc3[:, 0:16], in_=self.dt_bias_d[l:l + 1].partition_broadcast(128), writes=[b_bc3])
        P.dma(SP, out=bc3[:, 16:32], in_=self.a_log_d[l:l + 1].partition_broadcast(128), writes=[b_bc3])
        P.dma(SP, out=bc3[:, 32:48], in_=self.d_skip_d[l:l + 1].partition_broadcast(128), writes=[b_bc3])
        self.act(bc3[:, 16:32], bc3[:, 16:32], AF.Exp, [b_bc3], [b_bc3])
        self.ts(DVE, bc3[:, 16:32], bc3[:, 16:32], -1.0, None, ALU.mult, None, [b_bc3], [b_bc3])
        for j in range(4):
            P.dma(SP, out=cwT[:, :, j:j + 1], in_=self.conv_w_d[l, j:j + 1, :].rearrange("o (c p) -> p c o", p=128), writes=[b_cw],
                  allow_slow_non_contiguous=True)
        P.dma(SP, out=cwT[:, :, 4:5], in_=self.conv_b_d[l:l + 1, :].rearrange("o (c p) -> p c o", p=128), writes=[b_cw],
              allow_slow_non_contiguous=True)
        P.dma(SP, out=snw, in_=self.ssm_norm_w_d[l:l + 1, :].rearrange("o (c p) -> p (o c)", p=128), writes=[b_snw],
              allow_slow_non_contiguous=True)
        wdt, bwdt = self.wload(self.w_in_cols(l, C_SDT, 16), 16)
        for g4 in range(4):
            ps, bps = self.pj()
            for u_ in range(4):
                kt = 4 * g4 + u_
                for c in range(NCH):
                    self.mm(ps[:, u_ * 16:(u_ + 1) * 16], self.hT[:, c, kt * 128:(kt + 1) * 128], wdt[:, c, 0:16],
                            c == 0, c == NCH - 1, [bwdt, self.b_hT[kt]], [bps])
            self.tt(DVE, dt_all[:, g4 * 64:(g4 + 1) * 64].rearrange("p (t h) -> p t h", t=4),
                    ps[:, 0:64].rearrange("p (t h) -> p t h", t=4), bc3[:, 0:16].unsqueeze(1).to_broadcast([128, 4, 16]), ALU.add,
                    [bps, b_bc3], [b_dt])
        self.act(dt_all, dt_all, AF.Exp, [b_dt], [b_dt])
        self.act(dt_all, dt_all, AF.Ln, [b_dt], [b_dt], bias=1.0)
        self.tt(DVE, dA_all.rearrange("p (t h) -> p t h", t=NT), dt_all.rearrange("p (t h) -> p t h", t=NT),
                bc3[:, 16:32].unsqueeze(1).to_broadcast([128, NT, 16]), ALU.mult, [b_dt, b_bc3], [b_dt])
        if getattr(self, 'ssd_level', 9) < 1:
            return
        cins = [self.carve(2304, 2080, F32), self.carve(6464, 2080, F32)]
        cout = [self.carve(10624, L), self.carve(12672, L)]
        pa = [self.carve(14720 + 1024 * i_, 512, F32) for i_ in range(4)]
        b_cin, b_cout = [Buf("cin0"), Buf("cin1")], [Buf("cout0"), Buf("cout1")]
        b_pa = [Buf("pa%d" % i_) for i_ in range(4)]
        b_scr = [Buf("scr%d" % i) for i in range(24)]
        PAD = 4
        for i_ in range(2):
            self.memset(DVE, cins[i_][:, 0:PAD], 0.0, [b_cin[i_]])
        kk = [0]
        wl = {}

        def emit_proj(cc, tb):
            if tb == 0:
                wl[cc] = self.wload(self.w_in_cols(l, C_SX + 128 * cc, 128), 128)
            w_, bw_ = wl[cc]
            cin, b_u = cins[cc % 2], b_cin[cc % 2]
            ps, bps = self.pj()
            self.proj_fm(w_, bw_, 128, tb, ps, bps)
            self.act(cin[:, PAD + tb * 512:PAD + (tb + 1) * 512], ps[:, :], AF.Copy, [bps], [b_u])

        def emit_conv(cc, tb):
            k = kk[0]
            kk[0] += 1
            cin, b_u = cins[cc % 2], b_cin[cc % 2]
            co, bco = cout[cc % 2], b_cout[cc % 2]
            a_, ba_ = self.pb[2 + k % 4], self.b_pb[2 + k % 4]
            p_, bp_ = pa[k % 4], b_pa[k % 4]
            o0 = tb * 512 + PAD - 3
            self.act(a_[:, :], cin[:, o0 + 3:o0 + 3 + 512], AF.Identity, [b_u, b_cw], [ba_], scale=cwT[:, cc, 3:4], bias=cwT[:, cc, 4:5])
            self.ts(POOL, p_, cin[:, o0 + 1:o0 + 1 + 512], cwT[:, cc, 1:2], 0.0, ALU.mult, ALU.add, [b_u, b_cw], [bp_])
            self.stt(a_[:, :], cin[:, o0 + 2:o0 + 2 + 512], cwT[:, cc, 2:3], a_[:, :], ALU.mult, ALU.add, [b_u, b_cw, ba_], [ba_])
            self.stt(a_[:, :], cin[:, o0:o0 + 512], cwT[:, cc, 0:1], a_[:, :], ALU.mult, ALU.add, [b_u, b_cw, ba_], [ba_])
            self.tt(DVE, a_[:, :], a_[:, :], p_, ALU.add, [ba_, bp_], [ba_])
            pend.append((cc, tb, a_, ba_))

        def emit_back():
            cc, tb, a_, ba_ = pend.pop(0)
            co, bco = cout[cc % 2], b_cout[cc % 2]
            self.act(co[:, tb * 512:(tb + 1) * 512], a_[:, :], AF.Silu, [ba_], [bco])
            if tb == 3:
                P.dma(SP, out=scr[cc * 128:(cc + 1) * 128, :], in_=co, reads=[bco], writes=[b_scr[cc]])

        pend = []
        for tb in range(4):
            emit_proj(0, tb)
        for cc in range(16):
            for tb in range(4):
                if cc + 1 < 16:
                    emit_proj(cc + 1, tb)
                emit_conv(cc, tb)
                if len(pend) > 2:
                    emit_back()
        while pend:
            emit_back()
        for c in range(NCH):
            w_, bw_ = self.wload(self.w_in_cols(l, C_SZ + 128 * c, 128), 128)
            co, bco = cout[c % 2], b_cout[c % 2]
            for tb in range(4):
                ps, bps = self.pj()
                self.proj_fm(w_, bw_, 128, tb, ps, bps)
                self.act(co[:, tb * 512:(tb + 1) * 512], ps[:, :], AF.Silu, [bps], [bco])
            P.dma(SP, out=scr[2048 + c * 128:2048 + (c + 1) * 128, :], in_=co, reads=[bco], writes=[b_scr[16 + c]])
        if getattr(self, 'ssd_level', 9) < 2:
            return
        P.barrier()
        xsT = self.carve(2304, 1024).rearrange("p (c t) -> p c t", c=8)
        BT = self.carve(3328, 512).rearrange("p (c t) -> p c t", c=4)
        CTs = [self.carve(3840 + 512 * i_, 512).rearrange("p (c t) -> p c t", c=4) for i_ in range(2)]
        zsT = self.carve(4864, 1024).rearrange("p (c t) -> p c t", c=8)
        rhsseg = self.carve(5888, 1024, F32).rearrange("p (h l) -> p h l", h=8)
        decs = [self.carve(7936 + 2048 * i_, 2048).rearrange("p (h l) -> p h l", h=16) for i_ in range(2)]
        CBm = self.carve(12032, 512).rearrange("p (g l) -> p g l", g=4)
        xdts = [self.carve(12544 + 1024 * i_, 1024).rearrange("p (h d) -> p h d", h=16) for i_ in range(2)]
        xdd = self.carve(14592, 1024).rearrange("p (h d) -> p h d", h=16)
        ytoks = [self.carve(15616 + 1024 * i_, 1024) for i_ in range(2)]
        y1 = self.carve(17664, 1024, F32)
        Btoks = [self.carve(19712 + 512 * i_, 512).rearrange("p (g n) -> p g n", g=4) for i_ in range(2)]
        state = self.carve(20736, 1024, F32)
        state_bf = self.carve(22784, 1024)
        sq = self.carve(23808, 256).rearrange("p (c t) -> p c t", c=2)
        rt = self.carve(24064, 128, F32)
        Es = [self.carve(24320 + 96 * i_, 48, F32) for i_ in range(2)]
        b_xs, b_B, b_z = Buf("xsT"), Buf("BT"), Buf("zsT")
        b_Cs = [Buf("CT0"), Buf("CT1")]
        b_rs, b_CBm, b_xdd, b_y1 = [Buf(n) for n in "rhsseg CBm xdd y1".split()]
        b_decs, b_xdts, b_ytoks, b_Btoks, b_Es = [[Buf("%s%d" % (n, i_)) for i_ in range(2)] for n in "dec xdt ytok Btok E".split()]
        b_state, b_sbf, b_sq, b_rt = [Buf(n) for n in "state state_bf sq rt".split()]
        pb, bpb = self.pb, self.b_pb

        def load_front_tiles(kt):
            tsl = slice(kt * 128, (kt + 1) * 128)
            P.dma(SP, out=xsT, in_=scr[0:1024, tsl].rearrange("(c p) t -> p c t", p=128), reads=b_scr[0:8], writes=[b_xs])
            P.dma(SP, out=BT, in_=scr[1024:1536, tsl].rearrange("(c p) t -> p c t", p=128), reads=b_scr[8:12], writes=[b_B])
            P.dma(SP, out=CTs[kt % 2], in_=scr[1536:2048, tsl].rearrange("(c p) t -> p c t", p=128), reads=b_scr[12:16],
                  writes=[b_Cs[kt % 2]])

        def front(kt):
            E, b_E = Es[kt % 2], b_Es[kt % 2]
            dec, b_dec = decs[kt % 2], b_decs[kt % 2]
            xdt, b_xdt = xdts[kt % 2], b_xdts[kt % 2]
            ytok, b_ytok = ytoks[kt % 2], b_ytoks[kt % 2]
            Btok, b_Btok = Btoks[kt % 2], b_Btoks[kt % 2]
            CT, b_C = CTs[kt % 2], b_Cs[kt % 2]
            dAk = dA_all[:, kt * 16:(kt + 1) * 16]
            self.mm(pb[0][:, 0:16], Lm, dAk, True, True, [b_cm, b_dt], [bpb[0]])
            self.mm(pb[0][:, 16:32], U, dAk, True, True, [b_cm, b_dt], [bpb[0]])
            self.mm(pb[0][:, 32:48], onesf, dAk, True, True, [b_cm, b_dt], [bpb[0]])
            self.act(E, pb[0][:, 0:48], AF.Exp, [bpb[0]], [b_E])
            yield
            for half in range(2):
                self.tt(POOL, rhsseg, Lm.unsqueeze(1).to_broadcast([128, 8, 128]),
                        dAk[:, 8 * half:8 * half + 8].unsqueeze(2).to_broadcast([128, 8, 128]), ALU.mult, [b_cm, b_dt], [b_rs])
                yield
                for q in range(2):
                    self.mm(pb[2 + q][:, :], U, rhsseg[:, 4 * q:4 * q + 4, :].rearrange("p h l -> p (h l)"), True, True,
                            [b_cm, b_rs], [bpb[2 + q]])
                    self.act(dec[:, 8 * half + 4 * q:8 * half + 4 * q + 4, :].rearrange("p h l -> p (h l)"), pb[2 + q][:, :],
                             AF.Exp, [bpb[2 + q]], [b_dec])
                yield
            for g in range(4):
                self.mm(pb[0][:, g * 128:(g + 1) * 128], BT[:, g, :], CT[:, g, :], True, True, [b_B, b_C], [bpb[0]])
            self.tt(DVE, CBm, pb[0][:, :].rearrange("p (g l) -> p g l", g=4), mask01.unsqueeze(1).to_broadcast([128, 4, 128]),
                    ALU.mult, [bpb[0], b_cm], [b_CBm])
            yield
            d4 = dec.rearrange("p (g j) l -> p g j l", g=4)
            self.tt(POOL, d4, d4, CBm.unsqueeze(2).to_broadcast([128, 4, 4, 128]), ALU.mult, [b_dec, b_CBm], [b_dec])
            yield
            pxb = pb[1][:].bitcast(BF16)
            for c in range(NCH):
                self.tr(pxb[:, c * 128:(c + 1) * 128], xsT[:, c, :], [b_xs], [bpb[1]])
            px3 = pxb.rearrange("p (h d) -> p h d", h=16)
            self.tt(DVE, xdt, px3, dt_all[:, kt * 16:(kt + 1) * 16].unsqueeze(2).to_broadcast([128, 16, 64]), ALU.mult,
                    [bpb[1], b_dt], [b_xdt])
            self.tt(DVE, ytok.rearrange("p (h d) -> p h d", h=16), px3, bc3[:, 32:48].unsqueeze(2).to_broadcast([128, 16, 64]),
                    ALU.mult, [bpb[1], b_bc3], [b_ytok])
            yield
            if kt < NT - 1:
                pbb = pb[0][:].bitcast(BF16)
                for g in range(4):
                    self.tr(pbb[:, g * 128:(g + 1) * 128], BT[:, g, :], [b_B], [bpb[0]])
                self.act(Btok.rearrange("p g n -> p (g n)"), pbb[:, 0:512], AF.Copy, [bpb[0]], [b_Btok])
            yield

        def back(kt):
            tsl = slice(kt * 128, (kt + 1) * 128)
            E, b_E = Es[kt % 2], b_Es[kt % 2]
            dec, b_dec = decs[kt % 2], b_decs[kt % 2]
            xdt, b_xdt = xdts[kt % 2], b_xdts[kt % 2]
            ytok, b_ytok = ytoks[kt % 2], b_ytoks[kt % 2]
            Btok, b_Btok = Btoks[kt % 2], b_Btoks[kt % 2]
            CT, b_C = CTs[kt % 2], b_Cs[kt % 2]
            P.dma(SP, out=zsT, in_=scr[2048:3072, tsl].rearrange("(c p) t -> p c t", p=128), reads=b_scr[16:24], writes=[b_z])
            if kt < NT - 1:
                self.tt(POOL, xdd, xdt, E[:, 16:32].unsqueeze(2).to_broadcast([128, 16, 64]), ALU.mult, [b_xdt, b_E], [b_xdd])
            for h in range(HB):
                self.mm(pb[4 + h // 8][:, (h % 8) * 64:(h % 8 + 1) * 64], dec[:, h, :], xdt[:, h, :], True, True,
                        [b_dec, b_xdt], [bpb[4 + h // 8]])
            yield
            if kt > 0:
                for g in range(4):
                    self.mm(pb[6 + g // 2][:, (g % 2) * 256:(g % 2 + 1) * 256], CT[:, g, :], state_bf[:, g * 256:(g + 1) * 256],
                            True, True, [b_C, b_sbf], [bpb[6 + g // 2]])
            yield
            for hb in range(2):
                hs = slice(hb * 512, (hb + 1) * 512)
                if kt > 0:
                    self.tt(DVE, y1[:, hs].rearrange("p (h d) -> p h d", h=8), pb[6 + hb][:, :].rearrange("p (h d) -> p h d", h=8),
                            E[:, 8 * hb:8 * hb + 8].unsqueeze(2).to_broadcast([128, 8, 64]), ALU.mult, [bpb[6 + hb], b_E], [b_y1])
                    self.tt(DVE, y1[:, hs], y1[:, hs], pb[4 + hb][:, :], ALU.add, [b_y1, bpb[4 + hb]], [b_y1])
                    self.tt(DVE, ytok[:, hs], y1[:, hs], ytok[:, hs], ALU.add, [b_y1, b_ytok], [b_ytok])
                else:
                    self.tt(DVE, ytok[:, hs], pb[4 + hb][:, :], ytok[:, hs], ALU.add, [bpb[4 + hb], b_ytok], [b_ytok])
                yield
            if kt < NT - 1:
                for g in range(4):
                    self.mm(pb[6 + g // 2][:, (g % 2) * 256:(g % 2 + 1) * 256], Btok[:, g, :],
                            xdd[:, 4 * g:4 * g + 4, :].rearrange("p h d -> p (h d)"), True, True, [b_Btok, b_xdd], [bpb[6 + g // 2]])
                yield
                for hb in range(2):
                    hs = slice(hb * 512, (hb + 1) * 512)
                    if kt > 0:
                        s3 = state[:, hs].rearrange("p (h d) -> p h d", h=8)
                        self.tt(POOL, s3, s3, E[:, 32 + 8 * hb:32 + 8 * hb + 8].unsqueeze(2).to_broadcast([128, 8, 64]), ALU.mult,
                                [b_state, b_E], [b_state])
                        self.tt(DVE, state[:, hs], state[:, hs], pb[6 + hb][:, :], ALU.add, [b_state, bpb[6 + hb]], [b_state])
                    else:
                        self.cp(DVE, state[:, hs], pb[6 + hb][:, :], [bpb[6 + hb]], [b_state])
                    self.cp(POOL, state_bf[:, hs], state[:, hs], [b_state], [b_sbf])
                    yield
            pyb = pb[4][:].bitcast(BF16)
            for c in range(NCH):
                self.tr(pyb[:, c * 128:(c + 1) * 128], ytok[:, c * 128:(c + 1) * 128], [b_ytok], [bpb[4]])
            self.tt(DVE, zsT, pyb.rearrange("p (c t) -> p c t", c=8), zsT, ALU.mult, [bpb[4], b_z], [b_z])
            yield
            for g in range(4):
                self.act(sq.rearrange("p c t -> p (c t)"), zsT[:, 2 * g:2 * g + 2, :].rearrange("p c t -> p (c t)"), AF.Square,
                         [b_z], [b_sq])
                self.mm(pb[5][:, 0:128], self.onesb[:], sq[:, 0, :], True, False, [self.b_const, b_sq], [bpb[5]])
                self.mm(pb[5][:, 0:128], self.onesb[:], sq[:, 1, :], False, True, [self.b_const, b_sq], [bpb[5]])
                self.act(rt, pb[5][:, 0:128], AF.Sqrt, [bpb[5]], [b_rt], scale=1.0 / 256, bias=1e-6)
                self.recip(rt, rt, [b_rt], [b_rt])
                for c2 in range(2):
                    c = 2 * g + c2
                    self.stt(self.ybr[:, c, tsl], zsT[:, c, :], snw[:, c:c + 1], rt, ALU.mult, ALU.mult, [b_z, b_snw, b_rt],
                             [self.b_ybr[c]])
                yield

        def interleave(*gens):
            gens = [g for g in gens if g is not None]
            while gens:
                for g in list(gens):
                    try:
                        next(g)
                    except StopIteration:
                        gens.remove(g)

        load_front_tiles(0)
        interleave(front(0))
        for kt in range(NT):
            nxt = None
            if kt + 1 < NT:
                load_front_tiles(kt + 1)
                nxt = front(kt + 1)
            interleave(nxt, back(kt))

    def diff(self, s, l):
        P = self.P
        lam_init = 0.8 - 0.6 * math.exp(-0.3 * l)
        cosE = self.carve(0, L)
        sinE = self.carve(2048, L)
        qT = self.carve(4096, L)
        kT = self.carve(6144, L)
        vaug = self.carve(8192, 2064).rearrange("p (t d) -> p t d", t=NT)
        sg = self.carve(10272, L)
        o = self.carve(12320, L, F32).rearrange("p (t d) -> p t d", t=NT)
        on = self.carve(16416, L).rearrange("p (t d) -> p t d", t=NT)
        pT = [self.carve(18464 + 512 * i, 512) for i in range(4)]
        xq = [self.carve(20512, 512), self.carve(21024, 512)]
        tf = [self.carve(21536, 512, F32), self.carve(22560, 512, F32)]
        pm = self.carve(23584, 128)
        b_tab = Buf("ropetab")
        b_q, b_k = Buf("dq"), Buf("dk")
        b_v = [Buf("dv%d" % i) for i in range(4)]
        b_sg = [Buf("dsg%d" % i) for i in range(4)]
        b_o = [Buf("o%d" % i) for i in range(NT)]
        b_on = [Buf("on%d" % i) for i in range(NT)]
        b_pT = [Buf("dpT%d" % i) for i in range(4)]
        b_xq = [Buf("xq0"), Buf("xq1")]
        b_tf = [Buf("tf0"), Buf("tf1")]
        b_pm = Buf("pm")
        b_rc = [Buf("drc%d" % i) for i in range(4)]
        b_lam = Buf("lam")
        b_ss2 = Buf("ss2")
        sm = self.small
        lamt = sm[:, 96:104]
        slw = sm[:, 104:106]
        ss2 = sm[:, 112:128]
        rt2 = sm[:, 128:144]
        rstd2 = sm[:, 144:160]
        P.dma(POOL, out=cosE, in_=self.rope_d[0], reads=[self.b_rope], writes=[b_tab])
        P.dma(POOL, out=sinE, in_=self.rope_d[1], reads=[self.b_rope], writes=[b_tab])
        self.memset(POOL, pm, 0.0, [b_pm])
        for a in range(2):
            P.op(POOL, lambda e, a=a: e.affine_select(out=pm[:, 64 * a:64 * a + 8], in_=self.onesb[:, 0:8], pattern=[[-1, 8]],
                                                      compare_op=ALU.is_equal, fill=0.0, base=-64 * a - 8, channel_multiplier=1),
                 [self.b_const], [b_pm])
            P.op(POOL, lambda e, a=a: e.affine_select(out=pm[:, 64 * a + 8:64 * a + 16], in_=self.onesb[:, 0:8], pattern=[[-1, 8]],
                                                      compare_op=ALU.is_equal, fill=0.0, base=-64 * a, channel_multiplier=1),
                 [self.b_const], [b_pm])
        dl = tf[0][:, 0:256]
        P.dma(SP, out=dl, in_=self.diff_lambda_d[l:l + 1].rearrange("o a d -> o (a d)").partition_broadcast(128), writes=[b_tf[0]])
        junk = tf[1][:, 0:64]
        self.stt(junk, dl[:, 0:64], 1.0, dl[:, 64:128], ALU.mult, ALU.mult, [b_tf[0]], [b_tf[1], b_lam], accum_out=lamt[:, 0:1])
        self.stt(junk, dl[:, 128:192], 1.0, dl[:, 192:256], ALU.mult, ALU.mult, [b_tf[0]], [b_tf[1], b_lam], accum_out=lamt[:, 1:2])
        self.act(lamt[:, 2:4], lamt[:, 0:2], AF.Exp, [b_lam], [b_lam])
        self.tt(DVE, lamt[:, 4:5], lamt[:, 2:3], lamt[:, 3:4], ALU.subtract, [b_lam], [b_lam])
        self.ts(DVE, lamt[:, 5:6], lamt[:, 4:5], -1.0, -lam_init, ALU.mult, ALU.add, [b_lam], [b_lam])
        neglam = lamt[:, 5:6]
        P.dma(SP, out=slw[:, 0:1], in_=self.subln_w_d[l:l + 1].rearrange("o p -> p o"), writes=[b_lam], allow_slow_non_contiguous=True)
        self.ts(DVE, slw[:, 0:1], slw[:, 0:1], (1.0 - lam_init) * 0.5, None, ALU.mult, None, [b_lam], [b_lam])
        for g4 in range(4):
            self.memset(DVE, vaug[:, 4 * g4:4 * g4 + 4, 128:129], 1.0, [b_v[g4]])
        pS = [self.pb[2], self.pb[3]]
        b_pS = [self.b_pb[2], self.b_pb[3]]
        pS3 = [self.pb[2], self.pb[3], self.pb[0]]
        b_pS3 = [self.b_pb[2], self.b_pb[3], self.b_pb[0]]
        pO = self.pb[4:8]
        b_pO = self.b_pb[4:8]
        sidx = 0
        k2 = 0
        for h in range(8):
            wq, bwq = self.wload(self.w_in_cols(l, C_DQ + 128 * h, 128), 128)
            wk, bwk = self.wload(self.w_in_cols(l, C_DK + 128 * h, 128), 128)
            wv, bwv = self.wload(self.w_in_cols(l, C_DV + 128 * h, 128), 128)
            wg, bwg = self.wload(self.w_in_cols(l, C_DG + 128 * h, 128), 128)
            for tb in range(4):
                sl = slice(tb * 512, (tb + 1) * 512)
                units = []
                for ui, (w_, bw_, dst, bdst, scale) in enumerate(((wq, bwq, qT, b_q, 0.125), (wk, bwk, kT, b_k, 1.0))):
                    ps, bps = self.pj()
                    self.proj_fm(w_, bw_, 128, tb, ps, bps)
                    x_, bx_ = xq[ui], b_xq[ui]
                    self.ts(DVE, x_, ps[:, :], scale, None, ALU.mult, None, [bps], [bx_])
                    units.append((x_, bx_, tf[ui], b_tf[ui], dst, bdst))
                for (x_, bx_, t1, bt1, dst, bdst) in units:
                    S, bS = pS[sidx % 2], b_pS[sidx % 2]
                    sidx += 1
                    self.mm(S[:, :], pm, x_, True, True, [b_pm, bx_], [bS])
                    self.tt(DVE, t1, S[:, :], sinE[:, sl], ALU.mult, [bS, b_tab], [bt1])
                    self.tt(DVE, x_, x_, cosE[:, sl], ALU.mult, [bx_, b_tab], [bx_])
                    self.tt(DVE, dst[:, sl], t1, x_, ALU.add, [bt1, bx_], [bdst])
                ps, bps = self.pj()
                self.proj_fm(wg, bwg, 128, tb, ps, bps)
                t1, bt1 = tf[k2 % 2], b_tf[k2 % 2]
                k2 += 1
                self.act(t1, ps[:, :], AF.Tanh, [bps], [bt1], scale=0.5)
                self.stt(sg[:, sl], t1, 1.0, ps[:, :], ALU.add, ALU.mult, [bt1, bps], [b_sg[tb]])
            for g4 in range(4):
                ps, bps = self.pj()
                for u in range(4):
                    kt = 4 * g4 + u
                    for c in range(NCH):
                        self.mm(ps[:, u * 128:(u + 1) * 128], self.hT[:, c, kt * 128:(kt + 1) * 128], wv[:, c, :],
                                c == 0, c == NCH - 1, [bwv, self.b_hT[kt]], [bps])
                self.cp(DVE, vaug[:, 4 * g4:4 * g4 + 4, 0:128], ps[:, :].rearrange("p (t d) -> p t d", t=4), [bps], [b_v[g4]])
            tiles = [(comp, jq, i) for comp in range(2) for jq in range(4) for i in range(4 * jq + 4)]

            def emitS(tile, idx):
                comp, jq, i = tile
                pr = slice(64 * comp, 64 * comp + 64)
                r_ = i - 4 * jq
                qlo = max(512 * jq, 128 * i)
                w = 512 * (jq + 1) - qlo
                S, bS = pS3[idx % 3], b_pS3[idx % 3]
                pt, bpt = pT[idx % 4], b_pT[idx % 4]
                self.mm(S[:, 0:w], kT[pr, i * 128:(i + 1) * 128], qT[pr, qlo:qlo + w], True, r_ < 0, [b_k, b_q], [bS])
                if r_ >= 0:
                    self.mm(S[:, 0:128], self.ident[:], self.bmask[:], False, True, [self.b_const], [bS])
                self.act(pt[:, 0:w], S[:, 0:w], AF.Exp, [bS], [bpt])

            def emitPV(tile, idx):
                comp, jq, i = tile
                qlo = max(512 * jq, 128 * i)
                w = 512 * (jq + 1) - qlo
                pt, bpt = pT[idx % 4], b_pT[idx % 4]
                for cc in range(w // 128):
                    qc = qlo // 128 + cc
                    O, bO = pO[qc % 4], b_pO[qc % 4]
                    self.mm(O[:, 0:129], pt[:, cc * 128:(cc + 1) * 128], vaug[:, i, :], i == 0, i == qc,
                            [bpt, b_v[i // 4]], [bO])
                    if i == qc:
                        rcq = self.rc[:, qc:qc + 1]
                        self.recip(rcq, O[:, 128:129], [bO], [b_rc[qc % 4]])
                        if comp == 0:
                            self.ts(DVE, o[:, qc, :], O[:, 0:128], rcq, None, ALU.mult, None, [bO, b_rc[qc % 4]], [b_o[qc]])
                        else:
                            t1, bt1 = tf[qc % 2], b_tf[qc % 2]
                            self.ts(DVE, t1[:, 0:128], O[:, 0:128], rcq, neglam, ALU.mult, ALU.mult,
                                    [bO, b_rc[qc % 4], b_lam], [bt1])
                            self.tt(DVE, o[:, qc, :], t1[:, 0:128], o[:, qc, :], ALU.add, [bt1, b_o[qc]], [b_o[qc]])
                            self.stt(t1[:, 128:256], o[:, qc, :], 1.0, o[:, qc, :], ALU.mult, ALU.mult,
                                     [b_o[qc]], [bt1, b_ss2], accum_out=ss2[:, qc:qc + 1])

            LAG = 2
            for n_, tile in enumerate(tiles):
                emitS(tile, sidx + n_)
                if n_ >= LAG:
                    emitPV(tiles[n_ - LAG], sidx + n_ - LAG)
            for n_ in range(len(tiles) - LAG, len(tiles)):
                emitPV(tiles[n_], sidx + n_)
            sidx += len(tiles)
            self.act(rt2, ss2, AF.Sqrt, [b_ss2], [b_ss2], scale=1.0 / 128, bias=1e-6)
            self.recip(rstd2, rt2, [b_ss2], [b_ss2])
            for qc in range(NT):
                self.ts(POOL if qc % 2 else DVE, on[:, qc, :], o[:, qc, :], rstd2[:, qc:qc + 1], None, ALU.mult, None,
                        [b_o[qc], b_ss2], [b_on[qc]])
            for g4 in range(4):
                ps, bps = self.pj()
                psb = ps[:].bitcast(BF16)
                for u in range(4):
                    qc = 4 * g4 + u
                    self.tr(psb[:, u * 128:(u + 1) * 128], on[:, qc, :], [b_on[qc]], [bps])
                self.stt(self.ybr[:, h, g4 * 512:(g4 + 1) * 512], psb[:, 0:512], slw[:, 0:1], sg[:, g4 * 512:(g4 + 1) * 512],
                         ALU.mult, ALU.mult, [bps, b_sg[g4], b_lam], [self.b_ybr[h]])


_NC_CACHE = {}


def kernel(x, norm_w, w_in, b_forget, conv_w, conv_b, dt_bias, a_log, d_skip, ssm_norm_w, diff_lambda, subln_w,
           w_branch, w_out, final_norm_w):
    ncores = 8
    nseq = x.shape[0] // ncores
    if 'nc' not in _NC_CACHE:
        _NC_CACHE['nc'] = Gen(nseq, 2, "abc", final=True).build()
    nc = _NC_CACHE['nc']
    f = lambda a: np.ascontiguousarray(np.asarray(a, dtype=np.float32))
    shared = dict(norm_w=f(norm_w), w_in=f(w_in), b_forget=f(b_forget), conv_w=f(conv_w), conv_b=f(conv_b),
                  dt_bias=f(dt_bias), a_log=f(a_log), d_skip=f(d_skip), ssm_norm_w=f(ssm_norm_w),
                  diff_lambda=f(diff_lambda), subln_w=f(subln_w), w_branch=f(w_branch), w_out=f(w_out),
                  final_norm_w=f(final_norm_w).reshape(1, -1))
    xs = f(x)
    in_maps = [dict(shared, x=np.ascontiguousarray(xs[i * nseq:(i + 1) * nseq])) for i in range(ncores)]
    res = run_bass_kernel_spmd(nc, in_maps, core_ids=list(range(ncores)))
    return np.concatenate([np.asarray(r["out"], dtype=np.float32) for r in res.results], axis=0)
```
